# Optimizing a Trainium2 kernel written in Bass

```python
import jax, jax.numpy as jnp
from jax import lax
import numpy as np

D_MODEL = 1024
BATCH = 32
SEQ = 2048
DEPTH = 1

GRID_W = 64
N_HEADS = 16
HEAD_DIM = D_MODEL // N_HEADS
ATTN_WIDTH = N_HEADS * HEAD_DIM
WIN_ROWS = 8
WIN_COLS = 16
Q_COLS = 16
K_COLS = 32
N_CBLK = GRID_W // Q_COLS
LRU_WIDTH = D_MODEL
LRU_BLOCKS = 16
LRU_BLOCK = LRU_WIDTH // LRU_BLOCKS
LRU_C = 8.0
CONV_W = 4
CONV_LEFT = 2
D_FF = 2816
FFN_RES = 0.5
EPS = 1e-6
IN_WIDTH = 2 * LRU_WIDTH + 3 * ATTN_WIDTH + 2 * D_MODEL

kernel_name = 'hybrid_rglru_natten_macaron_block'


def _rmsnorm(x, g):
    xf = x.astype(jnp.float32)
    y = xf * lax.rsqrt(jnp.mean(xf * xf, axis=-1, keepdims=True) + EPS)
    return (y * g.astype(jnp.float32)).astype(x.dtype)


def _swiglu(h, w_gu, w_down):
    g, u = jnp.split(h @ w_gu, 2, axis=-1)
    return (jax.nn.silu(g) * u) @ w_down


def _centred_dwconv(x, w, b):
    S = x.shape[1]
    xp = jnp.pad(x, ((0, 0), (CONV_LEFT, CONV_W - 1 - CONV_LEFT), (0, 0)))
    y = b
    for k in range(CONV_W):
        y = y + xp[:, k:k + S] * w[k]
    return y


def _lin_combine(left, right):
    a1, b1 = left
    a2, b2 = right
    return a1 * a2, a2 * b1 + b2


def _rglru_scan(xc, w_gates, b_gates, lam):
    B_, S, W = xc.shape
    xb = xc.reshape(B_, S, LRU_BLOCKS, LRU_BLOCK)
    gates = jnp.einsum('bsni,gnio->gbsno', xb, w_gates.astype(jnp.float32)).reshape(2, B_, S, W)
    gates = gates + b_gates.astype(jnp.float32)[:, None, None, :]
    r = jax.nn.sigmoid(gates[0])
    i = jax.nn.sigmoid(gates[1])
    log_a = -LRU_C * r * jax.nn.softplus(-lam.astype(jnp.float32))
    a = jnp.exp(log_a)
    bx = jnp.sqrt(-jnp.expm1(2.0 * log_a)) * (i * xc)
    _, h = lax.associative_scan(_lin_combine, (a, bx), axis=1)
    return h


def _col_tables():
    qc = np.arange(GRID_W).reshape(N_CBLK, Q_COLS)
    kstart = np.clip(np.arange(N_CBLK) * Q_COLS - WIN_COLS // 2, 0, GRID_W - K_COLS)
    kc = kstart[:, None] + np.arange(K_COLS)[None, :]
    cs = np.clip(qc - WIN_COLS // 2, 0, GRID_W - WIN_COLS)
    valid = (kc[:, None, :] >= cs[:, :, None]) & (kc[:, None, :] < cs[:, :, None] + WIN_COLS)
    dc = np.clip(kc[:, None, :] - qc[:, :, None], -(WIN_COLS - 1), WIN_COLS - 1) + WIN_COLS - 1
    return kc, valid, dc


def _neighbourhood_attention(q, k, v, rpb):
    B_, S, H, Dh = q.shape
    rows = S // GRID_W
    kr = min(WIN_ROWS, rows)
    kc, valid, dc = _col_tables()
    kg = k.reshape(B_, rows, GRID_W, H, Dh)[:, :, kc]
    vg = v.reshape(B_, rows, GRID_W, H, Dh)[:, :, kc]
    qg = jnp.moveaxis(q.reshape(B_, rows, N_CBLK, Q_COLS, H, Dh), 1, 0)
    rpb32 = rpb.astype(jnp.float32)
    mask = valid[None, None, :, :, None, :]

    def row_block(args):
        r, q_r = args
        rs = jnp.clip(r - kr // 2, 0, rows - kr)
        k_r = lax.dynamic_slice_in_dim(kg, rs, kr, axis=1)
        v_r = lax.dynamic_slice_in_dim(vg, rs, kr, axis=1)
        s = jnp.einsum('bnqhd,brnkhd->bhnqrk', q_r, k_r, preferred_element_type=jnp.float32)
        dr = rs + jnp.arange(kr) - r + WIN_ROWS - 1
        bias = rpb32[:, dr][:, :, dc]
        s = s + jnp.transpose(bias, (0, 2, 3, 1, 4))[None]
        s = jnp.where(mask, s, -jnp.inf)
        shp = s.shape
        p = jax.nn.softmax(s.reshape(shp[:4] + (kr * K_COLS,)), axis=-1).reshape(shp)
        return jnp.einsum('bhnqrk,brnkhd->bnqhd', p.astype(v.dtype), v_r)

    o = lax.map(row_block, (jnp.arange(rows), qg))
    return jnp.moveaxis(o, 0, 1).reshape(B_, S, H * Dh)


def setup_inputs(seed: int = 0) -> dict:
    key = jax.random.key(seed)
    ks = jax.random.split(key, 20)
    f32 = jnp.float32
    nrm = lambda k, shape, s: jax.random.normal(k, shape, f32) * s
    u = jax.random.uniform(ks[10], (DEPTH, 2, LRU_WIDTH), f32, minval=0.9, maxval=0.999)
    sg = u ** (1.0 / LRU_C)
    lru_lambda = jnp.log(sg) - jnp.log1p(-sg)
    return {
        'x': jax.random.normal(ks[0], (BATCH, SEQ, D_MODEL), f32),
        'norm_ffn1': 1.0 + nrm(ks[1], (DEPTH, D_MODEL), 0.02),
        'w_ffn1_gu': nrm(ks[2], (DEPTH, D_MODEL, 2 * D_FF), D_MODEL ** -0.5),
        'w_ffn1_down': nrm(ks[3], (DEPTH, D_FF, D_MODEL), D_FF ** -0.5),
        'norm_mix': 1.0 + nrm(ks[4], (DEPTH, D_MODEL), 0.02),
        'w_in': nrm(ks[5], (DEPTH, D_MODEL, IN_WIDTH), D_MODEL ** -0.5),
        'conv_w': nrm(ks[6], (DEPTH, CONV_W, LRU_WIDTH), CONV_W ** -0.5),
        'conv_b': nrm(ks[7], (DEPTH, LRU_WIDTH), 0.01),
        'lru_w_gates': nrm(ks[8], (DEPTH, 2, 2, LRU_BLOCKS, LRU_BLOCK, LRU_BLOCK), LRU_BLOCK ** -0.5),
        'lru_b_gates': nrm(ks[9], (DEPTH, 2, 2, LRU_WIDTH), 0.1),
        'lru_lambda': lru_lambda,
        'q_norm': 1.0 + nrm(ks[11], (DEPTH, HEAD_DIM), 0.02),
        'k_norm': 1.0 + nrm(ks[12], (DEPTH, HEAD_DIM), 0.02),
        'rel_pos_bias': nrm(ks[13], (DEPTH, N_HEADS, 2 * WIN_ROWS - 1, 2 * WIN_COLS - 1), 0.1),
        'w_out': nrm(ks[14], (DEPTH, D_MODEL, D_MODEL), D_MODEL ** -0.5),
        'norm_ffn2': 1.0 + nrm(ks[15], (DEPTH, D_MODEL), 0.02),
        'w_ffn2_gu': nrm(ks[16], (DEPTH, D_MODEL, 2 * D_FF), D_MODEL ** -0.5),
        'w_ffn2_down': nrm(ks[17], (DEPTH, D_FF, D_MODEL), D_FF ** -0.5),
    }


def reference(x, norm_ffn1, w_ffn1_gu, w_ffn1_down, norm_mix, w_in, conv_w, conv_b,
              lru_w_gates, lru_b_gates, lru_lambda, q_norm, k_norm, rel_pos_bias, w_out,
              norm_ffn2, w_ffn2_gu, w_ffn2_down):
    B_, S, _ = x.shape
    splits = list(np.cumsum([LRU_WIDTH, LRU_WIDTH, ATTN_WIDTH, ATTN_WIDTH, ATTN_WIDTH, D_MODEL]))
    for l in range(DEPTH):
        x = x + FFN_RES * _swiglu(_rmsnorm(x, norm_ffn1[l]), w_ffn1_gu[l], w_ffn1_down[l])

        h = _rmsnorm(x, norm_mix[l])
        xr, gr, q, k, v, ga, gb = jnp.split(h @ w_in[l], splits, axis=-1)

        xc = _centred_dwconv(xr, conv_w[l], conv_b[l]).astype(jnp.float32)
        h_fwd = _rglru_scan(xc, lru_w_gates[l, 0], lru_b_gates[l, 0], lru_lambda[l, 0])
        h_bwd = jnp.flip(_rglru_scan(jnp.flip(xc, 1), lru_w_gates[l, 1], lru_b_gates[l, 1],
                                     lru_lambda[l, 1]), 1)
        y_lru = ((h_fwd + h_bwd) * jax.nn.gelu(gr.astype(jnp.float32))).astype(x.dtype)

        q = _rmsnorm(q.reshape(B_, S, N_HEADS, HEAD_DIM), q_norm[l]) * (HEAD_DIM ** -0.5)
        k = _rmsnorm(k.reshape(B_, S, N_HEADS, HEAD_DIM), k_norm[l])
        v = v.reshape(B_, S, N_HEADS, HEAD_DIM)
        y_att = _neighbourhood_attention(q, k, v, rel_pos_bias[l]).astype(x.dtype)

        y = jax.nn.sigmoid(ga) * y_lru + jax.nn.sigmoid(gb) * y_att
        x = x + y @ w_out[l]

        x = x + FFN_RES * _swiglu(_rmsnorm(x, norm_ffn2[l]), w_ffn2_gu[l], w_ffn2_down[l])
    return x
```

```python
import bisect
from contextlib import ExitStack

import numpy as np
import concourse.bass as bass
import concourse.mybir as mybir
from concourse.bass_utils import run_bass_kernel_spmd

F32 = mybir.dt.float32
BF16 = mybir.dt.bfloat16
AF = mybir.ActivationFunctionType
ALU = mybir.AluOpType

NCORES = 8
NSEQ = 4
SEQ = 2048
D = 1024
DFF = 2816
NFC = 22
EPS = 1e-6
NSLOT = 14
TILES_PER_SEQ = 196
FFN1_BASE, MIX_BASE, FFN2_BASE = 0, 66, 130
NP_COLS = 112


def _tix(c, k):
    if c == 0:
        return k
    base = 7 + (c - 1) * 8
    return base + k if k < 1 else base + 1 + k


def _wix(c):
    return 8 + 8 * c if c < 7 else 63
ETAB_COLS = 2688
NEG = -30000.0

GRID_W = 64; ROWS = 32; WIN_ROWS = 8; WIN_COLS = 16
SPECIAL = [0, 1, 14, 15]


def _rs_of(r):
    return int(np.clip(r - WIN_ROWS // 2, 0, ROWS - WIN_ROWS))


def _chunks_for_tile(m):
    lo = _rs_of(2 * m); hi = _rs_of(2 * m + 1) + WIN_ROWS - 1
    return list(range(lo // 2, hi // 2 + 1))


def _table_layout():
    lay = {}
    off = 640
    for m in range(16):
        js = _chunks_for_tile(m)
        if m in SPECIAL:
            lay[m] = (off, js); off += 512
        else:
            lay[m] = (0, js)
    return lay


def _build_index_tables():
    lay = _table_layout()
    dr = np.zeros((128, ETAB_COLS), np.int64); dc = np.zeros((128, ETAB_COLS), np.int64)
    valid = np.zeros((128, ETAB_COLS), bool)
    done_interior = False
    for m in range(16):
        off, js = lay[m]
        if m not in SPECIAL and done_interior:
            continue
        for jj, j in enumerate(js):
            for kl in range(2):
                for ql in range(2):
                    kr = 2 * j + kl; qr = 2 * m + ql
                    vr = _rs_of(qr) <= kr < _rs_of(qr) + WIN_ROWS
                    kc = np.arange(64)[:, None]; qc = np.arange(64)[None, :]
                    csq = np.clip(qc - WIN_COLS // 2, 0, GRID_W - WIN_COLS)
                    v = (kc >= csq) & (kc < csq + WIN_COLS) & vr
                    d_r = kr - qr + WIN_ROWS - 1
                    d_c = np.clip(kc - qc, -(WIN_COLS - 1), WIN_COLS - 1) + WIN_COLS - 1
                    sl = (slice(kl * 64, kl * 64 + 64),
                          slice(off + jj * 128 + ql * 64, off + jj * 128 + ql * 64 + 64))
                    valid[sl] = v
                    dr[sl] = np.where(v, d_r, 0)
                    dc[sl] = np.where(v, d_c, 0)
        if m not in SPECIAL:
            done_interior = True
    return dr, dc, valid


class Sched:
    def __init__(self, nc):
        self.nc = nc
        self.eng = {'pe': nc.tensor, 'act': nc.scalar, 'dve': nc.vector, 'pool': nc.gpsimd, 'sp': nc.sync}
        self.sem = {e: nc.alloc_semaphore('prog_' + e) for e in ['pe', 'act', 'dve', 'pool']}
        self.cnt = {e: 0 for e in self.sem}
        self.ins = {e: [] for e in self.sem}
        self.sig_idx = {e: [] for e in self.sem}
        self.sig_cnt = {e: [] for e in self.sem}
        self.obs = {e: {} for e in self.eng}
        self.lw = {}
        self.rd = {}
        self.dma_total = {}
        self.dma_sems = {}

    def cover(self, e, idx):
        pos = bisect.bisect_left(self.sig_idx[e], idx)
        if pos < len(self.sig_idx[e]):
            return self.sig_cnt[e][pos]
        last = len(self.ins[e]) - 1
        self.ins[e][last].then_inc(self.sem[e], 1)
        self.cnt[e] += 1
        self.sig_idx[e].append(last); self.sig_cnt[e].append(self.cnt[e])
        return self.cnt[e]

    def _wait(self, engine, evts):
        need = {}
        for ev in evts:
            if ev[0] == 'c':
                if ev[1] == engine and engine == 'pe':
                    continue
                key = ev[1]; sem = self.sem[key]; val = self.cover(key, ev[2])
            else:
                key = ev[1]; sem = self.dma_sems[key]; val = self.dma_total[key]
            if need.get(key, (None, 0))[1] < val:
                need[key] = (sem, val)
        for key, (sem, val) in need.items():
            if self.obs[engine].get(key, 0) < val:
                self.eng[engine].wait_ge(sem, val)
                self.obs[engine][key] = val

    def _deps(self, reads, writes):
        evts = set()
        for k in reads:
            if k in self.lw:
                evts.add(self.lw[k])
        for k in writes:
            if k in self.lw:
                evts.add(self.lw[k])
            for ev in self.rd.get(k, {}).values():
                evts.add(ev)
        return evts

    def emit(self, engine, fn, reads=(), writes=(), signal=True):
        self._wait(engine, self._deps(reads, writes))
        ins = fn()
        idx = len(self.ins[engine]); self.ins[engine].append(ins)
        if signal:
            ins.then_inc(self.sem[engine], 1)
            self.cnt[engine] += 1
            self.sig_idx[engine].append(idx); self.sig_cnt[engine].append(self.cnt[engine])
        ev = ('c', engine, idx)
        for k in writes:
            self.lw[k] = ev; self.rd[k] = {}
        for k in reads:
            if k not in writes:
                self.rd.setdefault(k, {})[engine] = ev
        return ins

    def dma(self, fn, semkey, reads=(), writes=(), queue='sp'):
        if semkey not in self.dma_sems:
            self.dma_sems[semkey] = self.nc.alloc_semaphore('dma_' + semkey)
            self.dma_total[semkey] = 0
        self._wait(queue, self._deps(reads, writes))
        ins = fn()
        ins.then_inc(self.dma_sems[semkey], 16)
        self.dma_total[semkey] += 16
        ev = ('d', semkey)
        for k in writes:
            self.lw[k] = ev; self.rd[k] = {}
        for k in reads:
            self.rd.setdefault(k, {})[ev] = ev
        return ins

    def fence(self):
        evs = [('c', e, len(self.ins[e]) - 1) for e in self.sem if self.ins[e]]
        for e in ['pe', 'act', 'dve', 'pool', 'sp']:
            self._wait(e, evs)

    def wait_dma_all(self, engine='sp'):
        self._wait(engine, [('d', k) for k in self.dma_sems])


def build_nc(nseq=NSEQ, stop_after=None):
    nc = bass.Bass("TRN2", target_bir_lowering=False)
    S = Sched(nc)
    ntok = nseq * SEQ
    dram = lambda n, s, dt=F32, kind="ExternalInput": nc.dram_tensor(n, s, dt, kind=kind).ap()
    x = dram("x", [ntok, D])
    w_gu = [dram("w_ffn1_gu", [D, 2 * DFF]), dram("w_ffn2_gu", [D, 2 * DFF])]
    w_dn = [dram("w_ffn1_down", [DFF, D]), dram("w_ffn2_down", [DFF, D])]
    w_in = dram("w_in", [D, 7 * D])
    w_out = dram("w_out", [D, D])
    pvec_d = dram("pvec", [128, NP_COLS])
    gw_d = dram("gw", [128, 32 * 128])
    qk_d = dram("qk", [128, 2])
    ident_d = dram("ident", [128, 128])
    ebias_d = dram("ebias", [16, 128, ETAB_COLS])
    out = dram("out", [ntok, D], kind="ExternalOutput")
    wscr = dram("wscr", [TILES_PER_SEQ, 128, 1024], BF16, kind="Internal")
    escr = dram("escr", [16, 128, ETAB_COLS], BF16, kind="Internal")

    lay = _table_layout()
    top = ExitStack()
    uid = [0]

    def sb(n, s, dt, st=top):
        uid[0] += 1
        return st.enter_context(nc.sbuf_tensor(f"s{uid[0]}_{n}", s, dt))

    ps = lambda n, s, dt: top.enter_context(nc.psum_tensor(n, s, dt))

    ident = sb("ident", [128, 128], F32)
    identb = sb("identb", [128, 128], BF16)
    onesb = sb("onesb", [128, 128], BF16)
    bones = sb("bones", [128, 128], BF16)
    pvec = sb("pvec", [128, NP_COLS], F32)
    gwb = sb("gwb", [128, 32, 128], BF16)
    qk = sb("qksc", [128, 2], F32)
    cst = sb("cst", [128, 4], F32)
    cA = sb("cA", [128, 32], F32)
    PSA = ps("psA", [128, 1024], F32); PSB = ps("psB", [128, 1024], F32); PSC = ps("psC", [128, 1024], F32)
    PSD = ps("psD", [128, 512], F32); PST = ps("psT", [128, 1024], BF16)
    banks = [(PSA, 0, 'psA0'), (PSA, 512, 'psA1'), (PSB, 0, 'psB0'), (PSB, 512, 'psB1'),
             (PSC, 0, 'psC0'), (PSC, 512, 'psC1')]
    PSD_KEYS = ['psD']

    def bank_ap(i, n=512):
        t, o, k = banks[i]
        return t[:, o:o + n], k

    G1, GM, G2, CW, CB, BG, LAM = 0, 8, 16, 24, 56, 64, 96

    S.dma(lambda: nc.sync.dma_start(out=ident[:], in_=ident_d[:, :]), 'c0', writes=['ident'])
    S.dma(lambda: nc.sync.dma_start(out=pvec[:], in_=pvec_d[:, :]), 'c0', writes=['pvec'])
    S.dma(lambda: nc.sync.dma_start(out=qk[:], in_=qk_d[:, :]), 'c0', writes=['qk'])
    S.emit('dve', lambda: nc.vector.tensor_copy(out=identb[:], in_=ident[:]), ['ident'], ['identb'])
    S.emit('dve', lambda: nc.vector.memset(onesb[:], 1.0), [], ['onesb'])
    S.emit('dve', lambda: nc.vector.memset(bones[:], 0.0), [], ['bones'])
    S.emit('dve', lambda: nc.vector.memset(bones[0:64, 0:64], 1.0), [], ['bones'])
    S.emit('dve', lambda: nc.vector.memset(bones[64:128, 64:128], 1.0), [], ['bones'])
    S.emit('dve', lambda: nc.vector.memset(cst[:, 0:1], EPS), [], ['cst'])
    S.emit('dve', lambda: nc.vector.memset(cst[:, 1:2], 1.0), [], ['cst'])
    S.emit('dve', lambda: nc.vector.tensor_scalar(out=qk[:, 0:1], in0=qk[:, 0:1], scalar1=0.125, scalar2=None,
                                                  op0=ALU.mult), ['qk'], ['qk'])

    with ExitStack() as pst:
        psb = lambda n, s, dt: sb(n, s, dt, pst)
        lx = psb("lx", [128, 16], F32); lp = psb("lp", [128, 16], F32); lb = psb("lb", [128, 16], F32)
        lm = psb("lm", [128, 16], F32)
        lam = pvec[:, LAM:LAM + 16]
        S.emit('act', lambda: nc.scalar.activation(out=lx[:], in_=lam, func=AF.Exp, scale=-1.0), ['pvec'], ['lx'])
        S.emit('act', lambda: nc.scalar.activation(out=lb[:], in_=lx[:], func=AF.Ln, bias=cst[:, 1:2], scale=1.0),
               ['lx', 'cst'], ['lb'])
        S.emit('dve', lambda: nc.vector.tensor_scalar(out=lp[:], in0=lx[:], scalar1=-0.2, scalar2=0.25,
                                                      op0=ALU.mult, op1=ALU.add), ['lx'], ['lp'])
        for coef in (1.0 / 3.0, 0.5, 1.0):
            S.emit('dve', lambda: nc.vector.tensor_tensor(out=lp[:], in0=lp[:], in1=lx[:], op=ALU.mult), ['lp', 'lx'], ['lp'])
            S.emit('dve', lambda: nc.vector.tensor_scalar(out=lp[:], in0=lp[:], scalar1=-1.0, scalar2=coef,
                                                          op0=ALU.mult, op1=ALU.add), ['lp'], ['lp'])
        S.emit('dve', lambda: nc.vector.tensor_tensor(out=lp[:], in0=lp[:], in1=lx[:], op=ALU.mult), ['lp', 'lx'], ['lp'])
        S.emit('dve', lambda: nc.vector.tensor_single_scalar(out=lm[:], in_=lx[:], scalar=0.05, op=ALU.is_lt), ['lx'], ['lm'])
        S.emit('dve', lambda: nc.vector.tensor_tensor(out=lp[:], in0=lp[:], in1=lb[:], op=ALU.subtract), ['lp', 'lb'], ['lp'])
        S.emit('dve', lambda: nc.vector.tensor_tensor(out=lp[:], in0=lp[:], in1=lm[:], op=ALU.mult), ['lp', 'lm'], ['lp'])
        S.emit('dve', lambda: nc.vector.tensor_tensor(out=lp[:], in0=lp[:], in1=lb[:], op=ALU.add), ['lp', 'lb'], ['lp'])
        S.emit('dve', lambda: nc.vector.tensor_scalar(out=cA[:, 0:16], in0=lp[:], scalar1=-8.0, scalar2=None, op0=ALU.mult),
               ['lp'], ['cA'])
        S.emit('dve', lambda: nc.vector.tensor_scalar(out=cA[:, 16:32], in0=lp[:], scalar1=-16.0, scalar2=None, op0=ALU.mult),
               ['lp'], ['cA'])

        stf = psb("stf", [128, 2, 7168], F32)
        stb = psb("stb", [128, 56, 8, 128], BF16)
        stbf = stb[:].rearrange("p n c j -> p (n c j)")
        cast_engs = ['act', 'dve']
        cnt = [0]

        def cast(out_ap, in_ap, rd, wr):
            e = cast_engs[cnt[0] % 2]; cnt[0] += 1
            if e == 'act':
                S.emit('act', lambda: nc.scalar.copy(out=out_ap, in_=in_ap), rd, wr)
            elif e == 'dve':
                S.emit('dve', lambda: nc.vector.tensor_copy(out=out_ap, in_=in_ap), rd, wr)
            else:
                S.emit('pool', lambda: nc.gpsimd.tensor_copy(out=out_ap, in_=in_ap), rd, wr)

        ld = [0]

        def stage_load(src_ap, ncols, inner=None):
            sl = ld[0] % 2; ld[0] += 1
            dst = stf[:, sl, 0:ncols]
            if inner is not None:
                dst = dst.rearrange("p (f n) -> p f n", n=inner)
            S.dma(lambda: nc.sync.dma_start(out=dst, in_=src_ap), f'stf{sl}', writes=[f'stf{sl}'])
            return sl

        def colchunk_weight(W, ncols, store_fn):
            nt = ncols // 128
            for cc in range(8):
                sl = stage_load(W[cc * 128:(cc + 1) * 128, :], ncols)
                cast(stb[:, 0:nt, cc, :], stf[:, sl, 0:ncols].rearrange("p (n j) -> p n j", j=128),
                     [f'stf{sl}'], ['stb'])
            store_fn()

        def store_tiles(n0, cnt_, i0, step):
            src = stb[:, n0:n0 + cnt_, :, :].rearrange("p n c j -> p n (c j)")
            dst = wscr[i0:i0 + step * (cnt_ - 1) + 1:step].rearrange("n p x -> p n x")
            S.dma(lambda: nc.sync.dma_start(out=dst, in_=src), 'stst', reads=['stb'])

        for fi, base in ((0, FFN1_BASE), (1, FFN2_BASE)):
            def st_gu(base=base):
                for G in range(2):
                    for kind in range(2):
                        store_tiles(kind * NFC + G * 11, 11, base + G * 33 + kind, 2)
            colchunk_weight(w_gu[fi], 2 * DFF, st_gu)
            for G in range(2):
                for half, (f0, nf) in enumerate(((0, 6), (6, 5))):
                    fa = G * 11 + f0
                    sl = stage_load(w_dn[fi][fa * 128:(fa + nf) * 128, :].rearrange("(f p) n -> p f n", p=128), nf * 1024, 1024)
                    cast(stbf[:, 0:nf * 1024], stf[:, sl, 0:nf * 1024], [f'stf{sl}'], ['stb'])
                    i0 = base + G * 33 + 22 + f0
                    S.dma(lambda: nc.sync.dma_start(out=wscr[i0:i0 + nf].rearrange("n p x -> p n x"),
                                                    in_=stbf[:, 0:nf * 1024].rearrange("p (n x) -> p n x", x=1024)),
                          'stst', reads=['stb'])
        KORD = [0, 1, 4, 5, 6, 2, 3]

        def st_in():
            for fs in range(7):
                store_tiles(fs * 8, 1, MIX_BASE + _tix(0, KORD[fs]), 1)
                store_tiles(fs * 8 + 1, 7, MIX_BASE + _tix(1, KORD[fs]), 8)
        colchunk_weight(w_in, 7 * D, st_in)
        for half in range(2):
            sl = stage_load(w_out[half * 512:(half + 1) * 512, :].rearrange("(c p) n -> p c n", p=128), 4096, 1024)
            cast(stbf[:, 0:4096], stf[:, sl, 0:4096], [f'stf{sl}'], ['stb'])
            for ci in range(4):
                i0 = MIX_BASE + _wix(half * 4 + ci)
                S.dma(lambda: nc.sync.dma_start(out=wscr[i0], in_=stbf[:, ci * 1024:(ci + 1) * 1024]),
                      'stst', reads=['stb'])
        sl = stage_load(gw_d[:, :], 4096)
        S.emit('dve', lambda: nc.vector.tensor_copy(out=gwb[:].rearrange("p a b -> p (a b)"), in_=stf[:, sl, 0:4096]),
               [f'stf{sl}'], ['gwb'])
        for h in range(16):
            sl = stage_load(ebias_d[h], ETAB_COLS)
            S.emit('act', lambda: nc.scalar.activation(out=stbf[:, 0:ETAB_COLS], in_=stf[:, sl, 0:ETAB_COLS], func=AF.Exp),
                   [f'stf{sl}'], ['stb'])
            S.dma(lambda: nc.sync.dma_start(out=escr[h], in_=stbf[:, 0:ETAB_COLS]), 'stst', reads=['stb'])
        S.wait_dma_all('sp')
        S.fence()

    xT = sb("xT", [128, 8, SEQ], F32)
    hT = sb("hT", [128, 8, SEQ], BF16)
    ring = sb("ring", [128, NSLOT, 1024], BF16)
    total_tiles = TILES_PER_SEQ * nseq
    wst = {'next': 0}

    def load_upto(n):
        while wst['next'] < min(n, total_tiles):
            i = wst['next']; s = i % NSLOT
            S.dma(lambda: nc.sync.dma_start(out=ring[:, s, :], in_=wscr[i % TILES_PER_SEQ]), f'ring{s}',
                  writes=[f'ring{s}'])
            wst['next'] += 1

    def wget(i):
        load_upto(i + 1)
        return i % NSLOT

    def wdone(i):
        load_upto(i + 1 + NSLOT)

    load_upto(NSLOT)

    def tl(t):
        return slice(t * 512, (t + 1) * 512)

    def mm(out_ap, lhsT, rhs, start, stop, rd, wr):
        S.emit('pe', lambda: nc.tensor.matmul(out_ap, lhsT, rhs, start=start, stop=stop), rd, wr, signal=bool(stop))

    def rmsnorm(gcol, st):
        sq = sb("nsq", [128, 2, 512], BF16, st)
        sd = sb("nsd", [128, 2, 512], F32, st)
        for t in range(4):
            for c in range(8):
                b = c % 2
                S.emit('act', lambda: nc.scalar.activation(out=sq[:, b, :], in_=xT[:, c, tl(t)], func=AF.Square),
                       [f'xT{c}.{t}'], [f'nsq{b}'])
                mm(PSD[:, :], onesb[:], sq[:, b, :], c == 0, c == 7, ['onesb', f'nsq{b}'], PSD_KEYS)
            b = t % 2
            S.emit('act', lambda: nc.scalar.activation(out=sd[:, b, :], in_=PSD[:, :], func=AF.Ln,
                                                       bias=cst[:, 0:1], scale=1.0 / D), PSD_KEYS + ['cst'], [f'nsd{b}'])
            S.emit('act', lambda: nc.scalar.activation(out=sd[:, b, :], in_=sd[:, b, :], func=AF.Exp, scale=-0.5),
                   [f'nsd{b}'], [f'nsd{b}'])
            for c in range(8):
                S.emit('dve', lambda: nc.vector.scalar_tensor_tensor(out=hT[:, c, tl(t)], in0=xT[:, c, tl(t)],
                                                                     scalar=pvec[:, gcol + c:gcol + c + 1],
                                                                     in1=sd[:, b, :], op0=ALU.mult, op1=ALU.mult),
                       [f'xT{c}.{t}', 'pvec', f'nsd{b}'], [f'hT{c}.{t}'])

    def ffn(tbase, gcol):
        with ExitStack() as st:
            rmsnorm(gcol, st)
            act = sb("actT", [128, 11, SEQ], BF16, st)
            sg = sb("sg", [128, 2, 512], F32, st)
            k = 0
            for G in range(2):
                gb_ = tbase + G * 33
                for fl in range(11):
                    sg_ = wget(gb_ + 2 * fl); su_ = wget(gb_ + 2 * fl + 1)
                    for t in range(4):
                        pg, kg = bank_ap(k % 2); pu, ku = bank_ap(2 + k % 2); b = k % 2; k += 1
                        for cc in range(8):
                            mm(pg, ring[:, sg_, cc * 128:(cc + 1) * 128], hT[:, cc, tl(t)], cc == 0, cc == 7,
                               [f'ring{sg_}', f'hT{cc}.{t}'], [kg])
                        for cc in range(8):
                            mm(pu, ring[:, su_, cc * 128:(cc + 1) * 128], hT[:, cc, tl(t)], cc == 0, cc == 7,
                               [f'ring{su_}', f'hT{cc}.{t}'], [ku])
                        S.emit('act', lambda: nc.scalar.activation(out=sg[:, b, :], in_=pg, func=AF.Silu), [kg], [f'sg{b}'])
                        S.emit('dve', lambda: nc.vector.tensor_tensor(out=act[:, fl, tl(t)], in0=sg[:, b, :], in1=pu, op=ALU.mult),
                               [f'sg{b}', ku], [f'act{fl}.{t}'])
                    wdone(gb_ + 2 * fl); wdone(gb_ + 2 * fl + 1)
                dslots = [wget(gb_ + 22 + fl) for fl in range(11)]
                kd = 0
                for o in range(8):
                    for t in range(4):
                        pd, kdk = bank_ap(4 + kd % 2); kd += 1
                        for fl in range(11):
                            mm(pd, ring[:, dslots[fl], o * 128:(o + 1) * 128], act[:, fl, tl(t)], fl == 0, fl == 10,
                               [f'ring{dslots[fl]}', f'act{fl}.{t}'], [kdk])
                        S.emit('dve', lambda: nc.vector.scalar_tensor_tensor(out=xT[:, o, tl(t)], in0=pd, scalar=0.5,
                                                                             in1=xT[:, o, tl(t)], op0=ALU.mult, op1=ALU.add),
                               [kdk, f'xT{o}.{t}'], [f'xT{o}.{t}'])
                for fl in range(11):
                    wdone(gb_ + 22 + fl)
        S.fence()

    pk = [0]

    def proj(slot, t, bank_ids=(4, 5)):
        pb, kb = bank_ap(bank_ids[pk[0] % len(bank_ids)]); pk[0] += 1
        for cc in range(8):
            mm(pb, ring[:, slot, cc * 128:(cc + 1) * 128], hT[:, cc, tl(t)], cc == 0, cc == 7,
               [f'ring{slot}', f'hT{cc}.{t}'], [kb])
        return pb, kb

    def mixer(tbase, mst):
        with ExitStack() as st:
            rmsnorm(GM, st)
        S.fence()
        ytcs = [sb("ytcA", [128, SEQ], BF16, mst), sb("ytcB", [128, SEQ], BF16, mst)]

        def wout_accum(cprev):
            ytp = ytcs[cprev % 2]
            s_wo = wget(tbase + _wix(cprev))
            ko = 0
            for o in range(8):
                for t in range(4):
                    pb, kb = bank_ap(4 + ko % 2); ko += 1
                    mm(pb, ring[:, s_wo, o * 128:(o + 1) * 128], ytp[:, tl(t)], True, True,
                       [f'ring{s_wo}', f'ytc{cprev % 2}.{t}'], [kb])
                    S.emit('dve', lambda: nc.vector.tensor_tensor(out=xT[:, o, tl(t)], in0=pb, in1=xT[:, o, tl(t)], op=ALU.add),
                           [kb, f'xT{o}.{t}'], [f'xT{o}.{t}'])
            wdone(tbase + _wix(cprev))

        for c in range(8):
            tbk = lambda k_: tbase + _tix(c, k_)
            ytc = ytcs[c % 2]
            with ExitStack() as cst_:
                yA = sb("yA", [128, SEQ], BF16, cst_)
                with ExitStack() as st:
                    xr = sb("xr", [128, SEQ + 4], F32, st)
                    xcb = sb("xcb", [128, SEQ], BF16, st)
                    Rb_ = [sb("Rf", [128, SEQ], F32, st), sb("Rb", [128, SEQ], F32, st)]
                    Ib_ = [sb("If", [128, SEQ], F32, st), sb("Ib", [128, SEQ], F32, st)]
                    M0 = sb("M0", [128, SEQ], F32, st)
                    g2 = sb("g2", [128, SEQ], BF16, st)
                    tmp = sb("tmpA", [128, 2, 512], F32, st)
                    Mb_ = [M0[:, :], xr[:, 2:2 + SEQ]]
                    Mk = ['M0', 'xr']
                    s_xr = wget(tbk(0))
                    S.emit('pool', lambda: nc.gpsimd.memset(xr[:, 0:2], 0.0), [], ['xr'])
                    S.emit('pool', lambda: nc.gpsimd.memset(xr[:, 2 + SEQ:4 + SEQ], 0.0), [], ['xr'])
                    for t in range(4):
                        pb, kb = proj(s_xr, t, (0, 1, 2, 3))
                        S.emit('act', lambda: nc.scalar.copy(out=xr[:, 2 + t * 512:2 + (t + 1) * 512], in_=pb), [kb], ['xr'])
                    wdone(tbk(0))
                    cw = lambda k_: pvec[:, CW + k_ * 8 + c:CW + k_ * 8 + c + 1]
                    S.emit('pool', lambda: nc.gpsimd.tensor_scalar(out=M0[:, :], in0=xr[:, 0:SEQ], scalar1=cw(0),
                                                                   scalar2=pvec[:, CB + c:CB + c + 1], op0=ALU.mult, op1=ALU.add),
                           ['xr', 'pvec'], ['M0'])
                    for k_ in range(1, 4):
                        S.emit('dve', lambda: nc.vector.scalar_tensor_tensor(out=M0[:, :], in0=xr[:, k_:k_ + SEQ], scalar=cw(k_),
                                                                              in1=M0[:, :], op0=ALU.mult, op1=ALU.add),
                               ['xr', 'pvec', 'M0'], ['M0'])
                    S.emit('act', lambda: nc.scalar.copy(out=xcb[:, :], in_=M0[:, :]), ['M0'], ['xcb'])
                    for d in range(2):
                        for g_, (buf, bk) in enumerate(((Rb_[d], f'R{d}'), (Ib_[d], f'I{d}'))):
                            gi = (d * 2 + g_) * 8 + c
                            for t in range(4):
                                pb, kb = bank_ap((d * 8 + g_ * 4 + t) % 4)
                                mm(pb, gwb[:, gi, :], xcb[:, tl(t)], True, True, ['gwb', 'xcb'], [kb])
                                S.emit('act', lambda: nc.scalar.activation(out=buf[:, tl(t)], in_=pb, func=AF.Sigmoid,
                                                                           bias=pvec[:, BG + gi:BG + gi + 1], scale=1.0),
                                       [kb, 'pvec'], [bk])
                    if c > 0:
                        wout_accum(c - 1)
                    for d in range(2):
                        S.emit('act', lambda: nc.scalar.activation(out=Mb_[d], in_=Rb_[d][:, :], func=AF.Exp,
                                                                   scale=cA[:, 16 + d * 8 + c:16 + d * 8 + c + 1]),
                               [f'R{d}', 'cA'], [Mk[d]])
                        S.emit('act', lambda: nc.scalar.activation(out=Rb_[d][:, :], in_=Rb_[d][:, :], func=AF.Exp,
                                                                   scale=cA[:, d * 8 + c:d * 8 + c + 1]),
                               [f'R{d}', 'cA'], [f'R{d}'])
                    for d in range(2):
                        S.emit('act', lambda: nc.scalar.activation(out=Mb_[d], in_=Mb_[d], func=AF.Sqrt,
                                                                   bias=cst[:, 1:2], scale=-1.0), [Mk[d], 'cst'], [Mk[d]])
                    for d, e in ((0, 'dve'), (1, 'dve')):
                        eo = nc.vector if e == 'dve' else nc.gpsimd
                        S.emit(e, lambda: eo.tensor_tensor(out=Ib_[d][:, :], in0=Ib_[d][:, :], in1=xcb[:, :], op=ALU.mult),
                               [f'I{d}', 'xcb'], [f'I{d}'])
                        S.emit(e, lambda: eo.tensor_tensor(out=Ib_[d][:, :], in0=Ib_[d][:, :], in1=Mb_[d], op=ALU.mult),
                               [f'I{d}', Mk[d]], [f'I{d}'])
                    S.emit('dve', lambda: nc.vector.tensor_tensor_scan(out=Mb_[0], data0=Rb_[0][:, :], data1=Ib_[0][:, :],
                                                                       initial=0.0, op0=ALU.mult, op1=ALU.add),
                           ['R0', 'I0'], ['M0'])
                    S.emit('dve', lambda: nc.vector.tensor_tensor_scan(out=Mb_[1][:, ::-1], data0=Rb_[1][:, ::-1],
                                                                       data1=Ib_[1][:, ::-1], initial=0.0,
                                                                       op0=ALU.mult, op1=ALU.add),
                           ['R1', 'I1'], ['xr'])
                    S.emit('dve', lambda: nc.vector.tensor_tensor(out=Mb_[0], in0=Mb_[0], in1=Mb_[1], op=ALU.add),
                           ['M0', 'xr'], ['M0'])
                    s_gr = wget(tbk(1))
                    for t in range(4):
                        pb, kb = proj(s_gr, t, (0, 1, 2, 3)); b = t % 2
                        S.emit('act', lambda: nc.scalar.activation(out=tmp[:, b, :], in_=pb, func=AF.Square, scale=0.21145921592969426),
                               [kb], [f'tmpA{b}'])
                        S.emit('dve', lambda: nc.vector.scalar_tensor_tensor(out=tmp[:, b, :], in0=tmp[:, b, :], scalar=1.0, in1=pb,
                                                                             op0=ALU.add, op1=ALU.mult),
                               [f'tmpA{b}', kb], [f'tmpA{b}'])
                        S.emit('act', lambda: nc.scalar.activation(out=tmp[:, b, :], in_=tmp[:, b, :], func=AF.Sigmoid,
                                                                   scale=1.5957691216057308), [f'tmpA{b}'], [f'tmpA{b}'])
                        S.emit('dve', lambda: nc.vector.tensor_tensor(out=g2[:, tl(t)], in0=tmp[:, b, :], in1=pb, op=ALU.mult),
                               [f'tmpA{b}', kb], [f'g2.{t}'])
                    wdone(tbk(1))
                    s_ga = wget(tbk(2))
                    for t in range(4):
                        pb, kb = proj(s_ga, t, (0, 1, 2, 3)); b = t % 2
                        S.emit('act', lambda: nc.scalar.activation(out=tmp[:, b, :], in_=pb, func=AF.Sigmoid), [kb], [f'tmpA{b}'])
                        S.emit('dve', lambda: nc.vector.tensor_tensor(out=g2[:, tl(t)], in0=g2[:, tl(t)], in1=tmp[:, b, :], op=ALU.mult),
                               [f'tmpA{b}', f'g2.{t}'], [f'g2.{t}'])
                    wdone(tbk(2))
                    S.emit('dve', lambda: nc.vector.tensor_tensor(out=yA[:, :], in0=Mb_[0], in1=g2[:, :], op=ALU.mult),
                           ['M0'] + [f'g2.{t}' for t in range(4)], ['yA'])
                S.fence()
                with ExitStack() as st:
                    qn = sb("qn", [128, SEQ], BF16, st); kn = sb("kn", [128, SEQ], BF16, st)
                    Vc = sb("Vc", [128, 16, 2, 65], BF16, st)
                    sgb = sb("sgb", [128, SEQ], BF16, st)
                    Et = sb("Et", [128, 2, ETAB_COLS], BF16, st)
                    pT = sb("pT", [128, 4, 640], BF16, st)
                    sq = sb("bsq", [128, 2, 512], BF16, st)
                    sd = sb("bsd", [128, 2, 512], F32, st)
                    ybt = sb("ybt", [128, 2, 128], BF16, st)
                    rec = sb("rec", [128, 4], F32, st)
                    ytmp = sb("ytmp", [128, 2, 512], BF16, st)
                    for hh in range(2):
                        S.dma(lambda: nc.sync.dma_start(out=Et[:, hh, :], in_=escr[2 * c + hh]), f'et{hh}', writes=[f'Et{hh}'])
                    s_gb = wget(tbk(3))
                    for t in range(4):
                        pb, kb = proj(s_gb, t, (0, 1, 2, 3, 4, 5))
                        S.emit('act', lambda: nc.scalar.activation(out=sgb[:, tl(t)], in_=pb, func=AF.Sigmoid), [kb], [f'sgb{t}'])
                    wdone(tbk(3))
                    for wi, (dst, dk, col) in enumerate(((qn, 'qn', 0), (kn, 'kn', 1))):
                        s_ = wget(tbk(4 + wi))
                        for t in range(4):
                            pb, kb = proj(s_, t, (0, 1, 2, 3, 4, 5)); b = t % 2
                            S.emit('act', lambda: nc.scalar.activation(out=sq[:, b, :], in_=pb, func=AF.Square), [kb], [f'bsq{b}'])
                            mm(PSD[:, :], bones[:], sq[:, b, :], True, True, ['bones', f'bsq{b}'], PSD_KEYS)
                            S.emit('act', lambda: nc.scalar.activation(out=sd[:, b, :], in_=PSD[:, :], func=AF.Ln,
                                                                       bias=cst[:, 0:1], scale=1.0 / 64), PSD_KEYS + ['cst'], [f'bsd{b}'])
                            S.emit('act', lambda: nc.scalar.activation(out=sd[:, b, :], in_=sd[:, b, :], func=AF.Exp, scale=-0.5),
                                   [f'bsd{b}'], [f'bsd{b}'])
                            S.emit('dve', lambda: nc.vector.scalar_tensor_tensor(out=dst[:, tl(t)], in0=pb, scalar=qk[:, col:col + 1],
                                                                                 in1=sd[:, b, :], op0=ALU.mult, op1=ALU.mult),
                                   [kb, 'qk', f'bsd{b}'], [f'{dk}{t}'])
                        wdone(tbk(4 + wi))
                    s_v = wget(tbk(6))
                    S.emit('pool', lambda: nc.gpsimd.memset(Vc[:, :, :, 64:65], 1.0), [], ['Vones'])
                    for tq in range(4):
                        pb, kb = bank_ap(4 + tq % 2)
                        for tt in range(4):
                            tok = slice((tq * 4 + tt) * 128, (tq * 4 + tt + 1) * 128)
                            for cc in range(8):
                                mm(pb[:, tt * 128:(tt + 1) * 128], hT[:, cc, tok], ring[:, s_v, cc * 128:(cc + 1) * 128],
                                   cc == 0, cc == 7, [f'ring{s_v}', f'hT{cc}.{tq}'], [kb])
                        S.emit('act', lambda: nc.scalar.copy(out=Vc[:, tq * 4:tq * 4 + 4, :, 0:64],
                                                             in_=pb.rearrange("p (a h d) -> p a h d", a=4, h=2)),
                               [kb], [f'Vc{tq}'])
                    wdone(tbk(6))
                    iters = [(m, hh) for m in range(16) for hh in range(2)]

                    def sbuf_of(it):
                        return [(PSA, ['psA0', 'psA1']), (PSB, ['psB0', 'psB1'])][it % 2]

                    def scores(it):
                        m, hh = iters[it]
                        off, js = lay[m]; hb = hh * 64
                        SP, spk = sbuf_of(it)
                        for jj, j in enumerate(js):
                            mm(SP[:, jj * 128:(jj + 1) * 128], kn[hb:hb + 64, j * 128:(j + 1) * 128],
                               qn[hb:hb + 64, m * 128:(m + 1) * 128], True, True,
                               [f'kn{j // 4}', f'qn{m // 4}'], [spk[jj // 4]])

                    def stage_a(it):
                        m, hh = iters[it]
                        off, js = lay[m]; n = len(js)
                        SP, spk = sbuf_of(it)
                        pslot = it % 4
                        S.emit('act', lambda: nc.scalar.activation(out=pT[:, pslot, 0:n * 128], in_=SP[:, 0:n * 128], func=AF.Exp),
                               spk, [f'pT{pslot}'])
                        if it % 2 == 0:
                            S.emit('pool', lambda: nc.gpsimd.tensor_tensor(out=pT[:, pslot, 0:n * 128], in0=pT[:, pslot, 0:n * 128],
                                                                           in1=Et[:, hh, off:off + n * 128], op=ALU.mult),
                                   [f'pT{pslot}', f'Et{hh}'], [f'pT{pslot}'])
                        else:
                            S.emit('dve', lambda: nc.vector.tensor_tensor(out=pT[:, pslot, 0:n * 128], in0=pT[:, pslot, 0:n * 128],
                                                                          in1=Et[:, hh, off:off + n * 128], op=ALU.mult),
                                   [f'pT{pslot}', f'Et{hh}'], [f'pT{pslot}'])

                    def stage_b(it):
                        m, hh = iters[it]
                        off, js = lay[m]; n = len(js); t_q = m // 4; ys = m % 2; hb = hh * 64
                        pslot = it % 4
                        ops, okey = [(PSC[:, 0:65], 'psC0'), (PSC[:, 512:577], 'psC1'), (PSD[:, 0:65], 'psD')][it % 3]
                        for jj, j in enumerate(js):
                            mm(ops, pT[:, pslot, jj * 128:(jj + 1) * 128], Vc[:, j, hh, :], jj == 0, jj == n - 1,
                               [f'pT{pslot}', f'Vc{j // 4}', 'Vones'], [okey])
                        rcol = it % 4
                        S.emit('dve', lambda: nc.vector.reciprocal(out=rec[:, rcol:rcol + 1], in_=ops[:, 64:65]),
                               [okey], [f'rec{rcol}'])
                        S.emit('act', lambda: nc.scalar.mul(out=ybt[:, ys, hb:hb + 64], in_=ops[:, 0:64], mul=rec[:, rcol:rcol + 1]),
                               [okey, f'rec{rcol}'], [f'ybt{ys}h{hh}'])
                        if hh == 1:
                            pend.append(lambda m=m, t_q=t_q, ys=ys: tail(m, t_q, ys))

                    def tail(m, t_q, ys):
                        if True:
                            half = t_q % 2
                            S.emit('pe', lambda: nc.tensor.transpose(PST[:, half * 512 + (m % 4) * 128:half * 512 + (m % 4 + 1) * 128],
                                                                     ybt[:, ys, :], identb[:]),
                                   [f'ybt{ys}h0', f'ybt{ys}h1', 'identb'], ['psT'])
                            if m % 4 == 3:
                                S.emit('dve', lambda: nc.vector.tensor_tensor(out=ytmp[:, half, :], in0=PST[:, half * 512:(half + 1) * 512],
                                                                              in1=sgb[:, tl(t_q)], op=ALU.mult),
                                       ['psT', f'sgb{t_q}'], [f'ytmp{half}'])
                                S.emit('pool', lambda: nc.gpsimd.tensor_tensor(out=ytc[:, tl(t_q)], in0=ytmp[:, half, :],
                                                                               in1=yA[:, tl(t_q)], op=ALU.add),
                                       [f'ytmp{half}', 'yA'], [f'ytc{c % 2}.{t_q}'])

                    pend = []
                    scores(0); scores(1); stage_a(0)
                    for it in range(len(iters)):
                        if it + 2 < len(iters):
                            scores(it + 2)
                        if it + 1 < len(iters):
                            stage_a(it + 1)
                        todo = pend[:]; del pend[:]
                        stage_b(it)
                        for f_ in todo:
                            f_()
                    for f_ in pend:
                        f_()
                S.fence()
        wout_accum(7)
        S.fence()

    for s in range(nseq):
        row0 = s * SEQ
        wb = s * TILES_PER_SEQ
        with ExitStack() as st:
            xin = sb("xin", [128, 2, D], F32, st)
            for tt in range(16):
                sl = tt % 2
                S.dma(lambda: nc.sync.dma_start(out=xin[:, sl, :], in_=x[row0 + tt * 128:row0 + (tt + 1) * 128, :]),
                      f'xin{sl}', writes=[f'xin{sl}'])
                P_, pkeys = (PSA, ['psA0', 'psA1']) if tt % 2 == 0 else (PSB, ['psB0', 'psB1'])
                for c in range(8):
                    S.emit('pe', lambda: nc.tensor.transpose(P_[:, c * 128:(c + 1) * 128], xin[:, sl, c * 128:(c + 1) * 128], ident[:]),
                           [f'xin{sl}', 'ident'], [pkeys[c // 4]])
                t = tt // 4
                for hf in range(2):
                    dst = xT[:, hf * 4:hf * 4 + 4, tt * 128:(tt + 1) * 128]
                    src = P_[:, hf * 512:(hf + 1) * 512].rearrange("p (c j) -> p c j", j=128)
                    wk = [f'xT{c}.{t}' for c in range(hf * 4, hf * 4 + 4)]
                    if t % 2 == 0:
                        S.emit('act', lambda: nc.scalar.copy(out=dst, in_=src), [pkeys[hf]], wk)
                    else:
                        S.emit('dve', lambda: nc.vector.tensor_copy(out=dst, in_=src), [pkeys[hf]], wk)
        S.fence()
        if stop_after != 'load':
            ffn(wb + FFN1_BASE, G1)
            if stop_after != 'ffn1':
                with ExitStack() as mst:
                    mixer(wb + MIX_BASE, mst)
                if stop_after != 'mixer':
                    ffn(wb + FFN2_BASE, G2)
        with ExitStack() as st:
            xo = sb("xo", [128, 2, D], F32, st)
            for tt in range(16):
                sl = tt % 2; t = tt // 4
                P_, pkeys = (PSA, ['psA0', 'psA1']) if tt % 2 == 0 else (PSB, ['psB0', 'psB1'])
                for c in range(8):
                    S.emit('pe', lambda: nc.tensor.transpose(P_[:, c * 128:(c + 1) * 128], xT[:, c, tt * 128:(tt + 1) * 128], ident[:]),
                           [f'xT{c}.{t}', 'ident'], [pkeys[c // 4]])
                for hf in range(2):
                    if hf == 0:
                        S.emit('act', lambda: nc.scalar.copy(out=xo[:, sl, 0:512], in_=P_[:, 0:512]), [pkeys[0]], [f'xo{sl}'])
                    else:
                        S.emit('dve', lambda: nc.vector.tensor_copy(out=xo[:, sl, 512:1024], in_=P_[:, 512:1024]), [pkeys[1]], [f'xo{sl}'])
                S.dma(lambda: nc.sync.dma_start(out=out[row0 + tt * 128:row0 + (tt + 1) * 128, :], in_=xo[:, sl, :]),
                      f'xo{sl}', reads=[f'xo{sl}'])
            S.wait_dma_all('sp')
        S.fence()
    S.wait_dma_all('sp')
    top.close()
    return nc


def _host_layout(inputs):
    f = lambda k: np.ascontiguousarray(np.asarray(inputs[k], dtype=np.float32))
    pc = lambda v: np.ascontiguousarray(v.reshape(8, 128).T)
    cols = [pc(f('norm_ffn1')[0]), pc(f('norm_mix')[0]), pc(f('norm_ffn2')[0])]
    cw = f('conv_w')[0]
    cols += [pc(cw[k]) for k in range(4)]
    cols += [pc(f('conv_b')[0])]
    bgt = f('lru_b_gates')[0]
    cols += [pc(bgt[d, g]) for d in range(2) for g in range(2)]
    lam = f('lru_lambda')[0]
    cols += [pc(lam[d]) for d in range(2)]
    pvec = np.ascontiguousarray(np.concatenate(cols, axis=1))
    assert pvec.shape == (128, NP_COLS)
    wg = f('lru_w_gates')[0]
    gw = np.zeros((128, 32, 128), np.float32)
    for d in range(2):
        for g in range(2):
            for c in range(8):
                for bl in range(2):
                    gw[bl * 64:(bl + 1) * 64, (d * 2 + g) * 8 + c, bl * 64:(bl + 1) * 64] = wg[d, g, 2 * c + bl]
    qk = np.stack([np.tile(f('q_norm')[0], 2), np.tile(f('k_norm')[0], 2)], axis=1)
    dr, dc, valid = _build_index_tables()
    rpb = f('rel_pos_bias')[0]
    eb = rpb[:, dr, dc]
    eb[:, ~valid] = np.float32(NEG)
    return {
        "w_ffn1_gu": f('w_ffn1_gu')[0], "w_ffn2_gu": f('w_ffn2_gu')[0],
        "w_ffn1_down": f('w_ffn1_down')[0], "w_ffn2_down": f('w_ffn2_down')[0],
        "w_in": f('w_in')[0], "w_out": f('w_out')[0],
        "pvec": pvec, "gw": np.ascontiguousarray(gw.reshape(128, 32 * 128)),
        "qk": np.ascontiguousarray(qk.astype(np.float32)),
        "ident": np.eye(128, dtype=np.float32),
        "ebias": np.ascontiguousarray(eb.astype(np.float32)),
    }


_NC_CACHE = {}


def kernel(**inputs):
    shared = _host_layout(inputs)
    x = np.asarray(inputs['x'], dtype=np.float32)
    B = x.shape[0]
    per = B // NCORES
    if 'nc' not in _NC_CACHE:
        _NC_CACHE['nc'] = build_nc(per)
    nc = _NC_CACHE['nc']
    in_maps = []
    for i in range(NCORES):
        m = dict(shared)
        m["x"] = np.ascontiguousarray(x[i * per:(i + 1) * per].reshape(per * SEQ, D))
        in_maps.append(m)
    res = run_bass_kernel_spmd(nc, in_maps, core_ids=list(range(NCORES)))
    outs = [np.asarray(r["out"]).reshape(per, SEQ, D) for r in res.results]
    return np.concatenate(outs, axis=0).astype(np.float32)
```

```python
import bisect
from contextlib import ExitStack

import numpy as np
import concourse.bass as bass
import concourse.mybir as mybir
from concourse.bass_utils import run_bass_kernel_spmd

F32 = mybir.dt.float32
BF16 = mybir.dt.bfloat16
AF = mybir.ActivationFunctionType
ALU = mybir.AluOpType

NCORES = 8
NSEQ = 4
SEQ = 2048
D = 1024
DFF = 2816
NFC = 22
EPS = 1e-6
NSLOT = 14
TILES_PER_SEQ = 196
FFN1_BASE, MIX_BASE, FFN2_BASE = 0, 66, 130
NP_COLS = 112


def _tix(c, k):
    if c == 0:
        return k
    base = 7 + (c - 1) * 8
    return base + k if k < 1 else base + 1 + k


def _wix(c):
    return 8 + 8 * c if c < 7 else 63
ETAB_COLS = 2688
NEG = -30000.0

GRID_W = 64; ROWS = 32; WIN_ROWS = 8; WIN_COLS = 16
SPECIAL = [0, 1, 14, 15]


def _rs_of(r):
    return int(np.clip(r - WIN_ROWS // 2, 0, ROWS - WIN_ROWS))


def _chunks_for_tile(m):
    lo = _rs_of(2 * m); hi = _rs_of(2 * m + 1) + WIN_ROWS - 1
    return list(range(lo // 2, hi // 2 + 1))


def _table_layout():
    lay = {}
    off = 640
    for m in range(16):
        js = _chunks_for_tile(m)
        if m in SPECIAL:
            lay[m] = (off, js); off += 512
        else:
            lay[m] = (0, js)
    return lay


def _build_index_tables():
    lay = _table_layout()
    dr = np.zeros((128, ETAB_COLS), np.int64); dc = np.zeros((128, ETAB_COLS), np.int64)
    valid = np.zeros((128, ETAB_COLS), bool)
    done_interior = False
    for m in range(16):
        off, js = lay[m]
        if m not in SPECIAL and done_interior:
            continue
        for jj, j in enumerate(js):
            for kl in range(2):
                for ql in range(2):
                    kr = 2 * j + kl; qr = 2 * m + ql
                    vr = _rs_of(qr) <= kr < _rs_of(qr) + WIN_ROWS
                    kc = np.arange(64)[:, None]; qc = np.arange(64)[None, :]
                    csq = np.clip(qc - WIN_COLS // 2, 0, GRID_W - WIN_COLS)
                    v = (kc >= csq) & (kc < csq + WIN_COLS) & vr
                    d_r = kr - qr + WIN_ROWS - 1
                    d_c = np.clip(kc - qc, -(WIN_COLS - 1), WIN_COLS - 1) + WIN_COLS - 1
                    sl = (slice(kl * 64, kl * 64 + 64),
                          slice(off + jj * 128 + ql * 64, off + jj * 128 + ql * 64 + 64))
                    valid[sl] = v
                    dr[sl] = np.where(v, d_r, 0)
                    dc[sl] = np.where(v, d_c, 0)
        if m not in SPECIAL:
            done_interior = True
    return dr, dc, valid


class Sched:
    def __init__(self, nc):
        self.nc = nc
        self.eng = {'pe': nc.tensor, 'act': nc.scalar, 'dve': nc.vector, 'pool': nc.gpsimd, 'sp': nc.sync}
        self.sem = {e: nc.alloc_semaphore('prog_' + e) for e in ['pe', 'act', 'dve', 'pool']}
        self.cnt = {e: 0 for e in self.sem}
        self.ins = {e: [] for e in self.sem}
        self.sig_idx = {e: [] for e in self.sem}
        self.sig_cnt = {e: [] for e in self.sem}
        self.obs = {e: {} for e in self.eng}
        self.lw = {}
        self.rd = {}
        self.dma_total = {}
        self.dma_sems = {}

    def cover(self, e, idx):
        pos = bisect.bisect_left(self.sig_idx[e], idx)
        if pos < len(self.sig_idx[e]):
            return self.sig_cnt[e][pos]
        last = len(self.ins[e]) - 1
        self.ins[e][last].then_inc(self.sem[e], 1)
        self.cnt[e] += 1
        self.sig_idx[e].append(last); self.sig_cnt[e].append(self.cnt[e])
        return self.cnt[e]

    def _wait(self, engine, evts):
        need = {}
        for ev in evts:
            if ev[0] == 'c':
                if ev[1] == engine and engine == 'pe':
                    continue
                key = ev[1]; sem = self.sem[key]; val = self.cover(key, ev[2])
            else:
                key = ev[1]; sem = self.dma_sems[key]; val = self.dma_total[key]
            if need.get(key, (None, 0))[1] < val:
                need[key] = (sem, val)
        for key, (sem, val) in need.items():
            if self.obs[engine].get(key, 0) < val:
                self.eng[engine].wait_ge(sem, val)
                self.obs[engine][key] = val

    def _deps(self, reads, writes):
        evts = set()
        for k in reads:
            if k in self.lw:
                evts.add(self.lw[k])
        for k in writes:
            if k in self.lw:
                evts.add(self.lw[k])
            for ev in self.rd.get(k, {}).values():
                evts.add(ev)
        return evts

    def emit(self, engine, fn, reads=(), writes=(), signal=True):
        self._wait(engine, self._deps(reads, writes))
        ins = fn()
        idx = len(self.ins[engine]); self.ins[engine].append(ins)
        if signal:
            ins.then_inc(self.sem[engine], 1)
            self.cnt[engine] += 1
            self.sig_idx[engine].append(idx); self.sig_cnt[engine].append(self.cnt[engine])
        ev = ('c', engine, idx)
        for k in writes:
            self.lw[k] = ev; self.rd[k] = {}
        for k in reads:
            if k not in writes:
                self.rd.setdefault(k, {})[engine] = ev
        return ins

    def dma(self, fn, semkey, reads=(), writes=(), queue='sp'):
        if semkey not in self.dma_sems:
            self.dma_sems[semkey] = self.nc.alloc_semaphore('dma_' + semkey)
            self.dma_total[semkey] = 0
        self._wait(queue, self._deps(reads, writes))
        ins = fn()
        ins.then_inc(self.dma_sems[semkey], 16)
        self.dma_total[semkey] += 16
        ev = ('d', semkey)
        for k in writes:
            self.lw[k] = ev; self.rd[k] = {}
        for k in reads:
            self.rd.setdefault(k, {})[ev] = ev
        return ins

    def fence(self):
        evs = [('c', e, len(self.ins[e]) - 1) for e in self.sem if self.ins[e]]
        for e in ['pe', 'act', 'dve', 'pool', 'sp']:
            self._wait(e, evs)

    def wait_dma_all(self, engine='sp'):
        self._wait(engine, [('d', k) for k in self.dma_sems])


def build_nc(nseq=NSEQ, stop_after=None):
    nc = bass.Bass("TRN2", target_bir_lowering=False)
    S = Sched(nc)
    ntok = nseq * SEQ
    dram = lambda n, s, dt=F32, kind="ExternalInput": nc.dram_tensor(n, s, dt, kind=kind).ap()
    x = dram("x", [ntok, D])
    w_gu = [dram("w_ffn1_gu", [D, 2 * DFF]), dram("w_ffn2_gu", [D, 2 * DFF])]
    w_dn = [dram("w_ffn1_down", [DFF, D]), dram("w_ffn2_down", [DFF, D])]
    w_in = dram("w_in", [D, 7 * D])
    w_out = dram("w_out", [D, D])
    pvec_d = dram("pvec", [128, NP_COLS])
    gw_d = dram("gw", [128, 32 * 128])
    qk_d = dram("qk", [128, 2])
    ident_d = dram("ident", [128, 128])
    ebias_d = dram("ebias", [16, 128, ETAB_COLS])
    out = dram("out", [ntok, D], kind="ExternalOutput")
    wscr = dram("wscr", [TILES_PER_SEQ, 128, 1024], BF16, kind="Internal")
    escr = dram("escr", [16, 128, ETAB_COLS], BF16, kind="Internal")

    lay = _table_layout()
    top = ExitStack()
    uid = [0]

    def sb(n, s, dt, st=top):
        uid[0] += 1
        return st.enter_context(nc.sbuf_tensor(f"s{uid[0]}_{n}", s, dt))

    ps = lambda n, s, dt: top.enter_context(nc.psum_tensor(n, s, dt))

    ident = sb("ident", [128, 128], F32)
    identb = sb("identb", [128, 128], BF16)
    onesb = sb("onesb", [128, 128], BF16)
    bones = sb("bones", [128, 128], BF16)
    pvec = sb("pvec", [128, NP_COLS], F32)
    gwb = sb("gwb", [128, 32, 128], BF16)
    qk = sb("qksc", [128, 2], F32)
    cst = sb("cst", [128, 4], F32)
    cA = sb("cA", [128, 32], F32)
    PSA = ps("psA", [128, 1024], F32); PSB = ps("psB", [128, 1024], F32); PSC = ps("psC", [128, 1024], F32)
    PSD = ps("psD", [128, 512], F32); PST = ps("psT", [128, 1024], BF16)
    banks = [(PSA, 0, 'psA0'), (PSA, 512, 'psA1'), (PSB, 0, 'psB0'), (PSB, 512, 'psB1'),
             (PSC, 0, 'psC0'), (PSC, 512, 'psC1')]
    PSD_KEYS = ['psD']

    def bank_ap(i, n=512):
        t, o, k = banks[i]
        return t[:, o:o + n], k

    G1, GM, G2, CW, CB, BG, LAM = 0, 8, 16, 24, 56, 64, 96

    S.dma(lambda: nc.sync.dma_start(out=ident[:], in_=ident_d[:, :]), 'c0', writes=['ident'])
    S.dma(lambda: nc.sync.dma_start(out=pvec[:], in_=pvec_d[:, :]), 'c0', writes=['pvec'])
    S.dma(lambda: nc.sync.dma_start(out=qk[:], in_=qk_d[:, :]), 'c0', writes=['qk'])
    S.emit('dve', lambda: nc.vector.tensor_copy(out=identb[:], in_=ident[:]), ['ident'], ['identb'])
    S.emit('dve', lambda: nc.vector.memset(onesb[:], 1.0), [], ['onesb'])
    S.emit('dve', lambda: nc.vector.memset(bones[:], 0.0), [], ['bones'])
    S.emit('dve', lambda: nc.vector.memset(bones[0:64, 0:64], 1.0), [], ['bones'])
    S.emit('dve', lambda: nc.vector.memset(bones[64:128, 64:128], 1.0), [], ['bones'])
    S.emit('dve', lambda: nc.vector.memset(cst[:, 0:1], EPS), [], ['cst'])
    S.emit('dve', lambda: nc.vector.memset(cst[:, 1:2], 1.0), [], ['cst'])
    S.emit('dve', lambda: nc.vector.tensor_scalar(out=qk[:, 0:1], in0=qk[:, 0:1], scalar1=0.125, scalar2=None,
                                                  op0=ALU.mult), ['qk'], ['qk'])

    with ExitStack() as pst:
        psb = lambda n, s, dt: sb(n, s, dt, pst)
        lx = psb("lx", [128, 16], F32); lp = psb("lp", [128, 16], F32); lb = psb("lb", [128, 16], F32)
        lm = psb("lm", [128, 16], F32)
        lam = pvec[:, LAM:LAM + 16]
        S.emit('act', lambda: nc.scalar.activation(out=lx[:], in_=lam, func=AF.Exp, scale=-1.0), ['pvec'], ['lx'])
        S.emit('act', lambda: nc.scalar.activation(out=lb[:], in_=lx[:], func=AF.Ln, bias=cst[:, 1:2], scale=1.0),
               ['lx', 'cst'], ['lb'])
        S.emit('dve', lambda: nc.vector.tensor_scalar(out=lp[:], in0=lx[:], scalar1=-0.2, scalar2=0.25,
                                                      op0=ALU.mult, op1=ALU.add), ['lx'], ['lp'])
        for coef in (1.0 / 3.0, 0.5, 1.0):
            S.emit('dve', lambda: nc.vector.tensor_tensor(out=lp[:], in0=lp[:], in1=lx[:], op=ALU.mult), ['lp', 'lx'], ['lp'])
            S.emit('dve', lambda: nc.vector.tensor_scalar(out=lp[:], in0=lp[:], scalar1=-1.0, scalar2=coef,
                                                          op0=ALU.mult, op1=ALU.add), ['lp'], ['lp'])
        S.emit('dve', lambda: nc.vector.tensor_tensor(out=lp[:], in0=lp[:], in1=lx[:], op=ALU.mult), ['lp', 'lx'], ['lp'])
        S.emit('dve', lambda: nc.vector.tensor_single_scalar(out=lm[:], in_=lx[:], scalar=0.05, op=ALU.is_lt), ['lx'], ['lm'])
        S.emit('dve', lambda: nc.vector.tensor_tensor(out=lp[:], in0=lp[:], in1=lb[:], op=ALU.subtract), ['lp', 'lb'], ['lp'])
        S.emit('dve', lambda: nc.vector.tensor_tensor(out=lp[:], in0=lp[:], in1=lm[:], op=ALU.mult), ['lp', 'lm'], ['lp'])
        S.emit('dve', lambda: nc.vector.tensor_tensor(out=lp[:], in0=lp[:], in1=lb[:], op=ALU.add), ['lp', 'lb'], ['lp'])
        S.emit('dve', lambda: nc.vector.tensor_scalar(out=cA[:, 0:16], in0=lp[:], scalar1=-8.0, scalar2=None, op0=ALU.mult),
               ['lp'], ['cA'])
        S.emit('dve', lambda: nc.vector.tensor_scalar(out=cA[:, 16:32], in0=lp[:], scalar1=-16.0, scalar2=None, op0=ALU.mult),
               ['lp'], ['cA'])

        stf = psb("stf", [128, 2, 7168], F32)
        stb = psb("stb", [128, 56, 8, 128], BF16)
        stbf = stb[:].rearrange("p n c j -> p (n c j)")
        cast_engs = ['act', 'dve']
        cnt = [0]

        def cast(out_ap, in_ap, rd, wr):
            e = cast_engs[cnt[0] % 2]; cnt[0] += 1
            if e == 'act':
                S.emit('act', lambda: nc.scalar.copy(out=out_ap, in_=in_ap), rd, wr)
            elif e == 'dve':
                S.emit('dve', lambda: nc.vector.tensor_copy(out=out_ap, in_=in_ap), rd, wr)
            else:
                S.emit('pool', lambda: nc.gpsimd.tensor_copy(out=out_ap, in_=in_ap), rd, wr)

        ld = [0]

        def stage_load(src_ap, ncols, inner=None):
            sl = ld[0] % 2; ld[0] += 1
            dst = stf[:, sl, 0:ncols]
            if inner is not None:
                dst = dst.rearrange("p (f n) -> p f n", n=inner)
            S.dma(lambda: nc.sync.dma_start(out=dst, in_=src_ap), f'stf{sl}', writes=[f'stf{sl}'])
            return sl

        def colchunk_weight(W, ncols, store_fn):
            nt = ncols // 128
            for cc in range(8):
                sl = stage_load(W[cc * 128:(cc + 1) * 128, :], ncols)
                cast(stb[:, 0:nt, cc, :], stf[:, sl, 0:ncols].rearrange("p (n j) -> p n j", j=128),
                     [f'stf{sl}'], ['stb'])
            store_fn()

        def store_tiles(n0, cnt_, i0, step):
            src = stb[:, n0:n0 + cnt_, :, :].rearrange("p n c j -> p n (c j)")
            dst = wscr[i0:i0 + step * (cnt_ - 1) + 1:step].rearrange("n p x -> p n x")
            S.dma(lambda: nc.sync.dma_start(out=dst, in_=src), 'stst', reads=['stb'])

        for fi, base in ((0, FFN1_BASE), (1, FFN2_BASE)):
            def st_gu(base=base):
                for G in range(2):
                    for kind in range(2):
                        store_tiles(kind * NFC + G * 11, 11, base + G * 33 + kind, 2)
            colchunk_weight(w_gu[fi], 2 * DFF, st_gu)
            for G in range(2):
                for half, (f0, nf) in enumerate(((0, 6), (6, 5))):
                    fa = G * 11 + f0
                    sl = stage_load(w_dn[fi][fa * 128:(fa + nf) * 128, :].rearrange("(f p) n -> p f n", p=128), nf * 1024, 1024)
                    cast(stbf[:, 0:nf * 1024], stf[:, sl, 0:nf * 1024], [f'stf{sl}'], ['stb'])
                    i0 = base + G * 33 + 22 + f0
                    S.dma(lambda: nc.sync.dma_start(out=wscr[i0:i0 + nf].rearrange("n p x -> p n x"),
                                                    in_=stbf[:, 0:nf * 1024].rearrange("p (n x) -> p n x", x=1024)),
                          'stst', reads=['stb'])
        KORD = [0, 1, 4, 5, 6, 2, 3]

        def st_in():
            for fs in range(7):
                store_tiles(fs * 8, 1, MIX_BASE + _tix(0, KORD[fs]), 1)
                store_tiles(fs * 8 + 1, 7, MIX_BASE + _tix(1, KORD[fs]), 8)
        colchunk_weight(w_in, 7 * D, st_in)
        for half in range(2):
            sl = stage_load(w_out[half * 512:(half + 1) * 512, :].rearrange("(c p) n -> p c n", p=128), 4096, 1024)
            cast(stbf[:, 0:4096], stf[:, sl, 0:4096], [f'stf{sl}'], ['stb'])
            for ci in range(4):
                i0 = MIX_BASE + _wix(half * 4 + ci)
                S.dma(lambda: nc.sync.dma_start(out=wscr[i0], in_=stbf[:, ci * 1024:(ci + 1) * 1024]),
                      'stst', reads=['stb'])
        sl = stage_load(gw_d[:, :], 4096)
        S.emit('dve', lambda: nc.vector.tensor_copy(out=gwb[:].rearrange("p a b -> p (a b)"), in_=stf[:, sl, 0:4096]),
               [f'stf{sl}'], ['gwb'])
        for h in range(16):
            sl = stage_load(ebias_d[h], ETAB_COLS)
            S.emit('act', lambda: nc.scalar.activation(out=stbf[:, 0:ETAB_COLS], in_=stf[:, sl, 0:ETAB_COLS], func=AF.Exp),
                   [f'stf{sl}'], ['stb'])
            S.dma(lambda: nc.sync.dma_start(out=escr[h], in_=stbf[:, 0:ETAB_COLS]), 'stst', reads=['stb'])
        S.wait_dma_all('sp')
        S.fence()

    xT = sb("xT", [128, 8, SEQ], F32)
    hT = sb("hT", [128, 8, SEQ], BF16)
    ring = sb("ring", [128, NSLOT, 1024], BF16)
    total_tiles = TILES_PER_SEQ * nseq
    wst = {'next': 0}

    def load_upto(n):
        while wst['next'] < min(n, total_tiles):
            i = wst['next']; s = i % NSLOT
            S.dma(lambda: nc.sync.dma_start(out=ring[:, s, :], in_=wscr[i % TILES_PER_SEQ]), f'ring{s}',
                  writes=[f'ring{s}'])
            wst['next'] += 1

    def wget(i):
        load_upto(i + 1)
        return i % NSLOT

    def wdone(i):
        load_upto(i + 1 + NSLOT)

    load_upto(NSLOT)

    def tl(t):
        return slice(t * 512, (t + 1) * 512)

    def mm(out_ap, lhsT, rhs, start, stop, rd, wr):
        S.emit('pe', lambda: nc.tensor.matmul(out_ap, lhsT, rhs, start=start, stop=stop), rd, wr, signal=bool(stop))

    def rmsnorm(gcol, st):
        sq = sb("nsq", [128, 2, 512], BF16, st)
        sd = sb("nsd", [128, 2, 512], F32, st)
        for t in range(4):
            for c in range(8):
                b = c % 2
                S.emit('act', lambda: nc.scalar.activation(out=sq[:, b, :], in_=xT[:, c, tl(t)], func=AF.Square),
                       [f'xT{c}.{t}'], [f'nsq{b}'])
                mm(PSD[:, :], onesb[:], sq[:, b, :], c == 0, c == 7, ['onesb', f'nsq{b}'], PSD_KEYS)
            b = t % 2
            S.emit('act', lambda: nc.scalar.activation(out=sd[:, b, :], in_=PSD[:, :], func=AF.Ln,
                                                       bias=cst[:, 0:1], scale=1.0 / D), PSD_KEYS + ['cst'], [f'nsd{b}'])
            S.emit('act', lambda: nc.scalar.activation(out=sd[:, b, :], in_=sd[:, b, :], func=AF.Exp, scale=-0.5),
                   [f'nsd{b}'], [f'nsd{b}'])
            for c in range(8):
                S.emit('dve', lambda: nc.vector.scalar_tensor_tensor(out=hT[:, c, tl(t)], in0=xT[:, c, tl(t)],
                                                                     scalar=pvec[:, gcol + c:gcol + c + 1],
                                                                     in1=sd[:, b, :], op0=ALU.mult, op1=ALU.mult),
                       [f'xT{c}.{t}', 'pvec', f'nsd{b}'], [f'hT{c}.{t}'])

    def ffn(tbase, gcol):
        with ExitStack() as st:
            rmsnorm(gcol, st)
            act = sb("actT", [128, 11, SEQ], BF16, st)
            sg = sb("sg", [128, 2, 512], F32, st)
            k = 0
            for G in range(2):
                gb_ = tbase + G * 33
                for fl in range(11):
                    sg_ = wget(gb_ + 2 * fl); su_ = wget(gb_ + 2 * fl + 1)
                    for t in range(4):
                        pg, kg = bank_ap(k % 2); pu, ku = bank_ap(2 + k % 2); b = k % 2; k += 1
                        for cc in range(8):
                            mm(pg, ring[:, sg_, cc * 128:(cc + 1) * 128], hT[:, cc, tl(t)], cc == 0, cc == 7,
                               [f'ring{sg_}', f'hT{cc}.{t}'], [kg])
                        for cc in range(8):
                            mm(pu, ring[:, su_, cc * 128:(cc + 1) * 128], hT[:, cc, tl(t)], cc == 0, cc == 7,
                               [f'ring{su_}', f'hT{cc}.{t}'], [ku])
                        S.emit('act', lambda: nc.scalar.activation(out=sg[:, b, :], in_=pg, func=AF.Silu), [kg], [f'sg{b}'])
                        S.emit('dve', lambda: nc.vector.tensor_tensor(out=act[:, fl, tl(t)], in0=sg[:, b, :], in1=pu, op=ALU.mult),
                               [f'sg{b}', ku], [f'act{fl}.{t}'])
                    wdone(gb_ + 2 * fl); wdone(gb_ + 2 * fl + 1)
                dslots = [wget(gb_ + 22 + fl) for fl in range(11)]
                kd = 0
                for o in range(8):
                    for t in range(4):
                        pd, kdk = bank_ap(4 + kd % 2); kd += 1
                        for fl in range(11):
                            mm(pd, ring[:, dslots[fl], o * 128:(o + 1) * 128], act[:, fl, tl(t)], fl == 0, fl == 10,
                               [f'ring{dslots[fl]}', f'act{fl}.{t}'], [kdk])
                        S.emit('dve', lambda: nc.vector.scalar_tensor_tensor(out=xT[:, o, tl(t)], in0=pd, scalar=0.5,
                                                                             in1=xT[:, o, tl(t)], op0=ALU.mult, op1=ALU.add),
                               [kdk, f'xT{o}.{t}'], [f'xT{o}.{t}'])
                for fl in range(11):
                    wdone(gb_ + 22 + fl)
        S.fence()

    pk = [0]

    def proj(slot, t, bank_ids=(4, 5)):
        pb, kb = bank_ap(bank_ids[pk[0] % len(bank_ids)]); pk[0] += 1
        for cc in range(8):
            mm(pb, ring[:, slot, cc * 128:(cc + 1) * 128], hT[:, cc, tl(t)], cc == 0, cc == 7,
               [f'ring{slot}', f'hT{cc}.{t}'], [kb])
        return pb, kb

    def mixer(tbase, mst):
        with ExitStack() as st:
            rmsnorm(GM, st)
        S.fence()
        ytcs = [sb("ytcA", [128, SEQ], BF16, mst), sb("ytcB", [128, SEQ], BF16, mst)]

        def wout_accum(cprev):
            ytp = ytcs[cprev % 2]
            s_wo = wget(tbase + _wix(cprev))
            ko = 0
            for o in range(8):
                for t in range(4):
                    pb, kb = bank_ap(4 + ko % 2); ko += 1
                    mm(pb, ring[:, s_wo, o * 128:(o + 1) * 128], ytp[:, tl(t)], True, True,
                       [f'ring{s_wo}', f'ytc{cprev % 2}.{t}'], [kb])
                    S.emit('dve', lambda: nc.vector.tensor_tensor(out=xT[:, o, tl(t)], in0=pb, in1=xT[:, o, tl(t)], op=ALU.add),
                           [kb, f'xT{o}.{t}'], [f'xT{o}.{t}'])
            wdone(tbase + _wix(cprev))

        for c in range(8):
            tbk = lambda k_: tbase + _tix(c, k_)
            ytc = ytcs[c % 2]
            with ExitStack() as cst_:
                yA = sb("yA", [128, SEQ], BF16, cst_)
                with ExitStack() as st:
                    xr = sb("xr", [128, SEQ + 4], F32, st)
                    xcb = sb("xcb", [128, SEQ], BF16, st)
                    Rb_ = [sb("Rf", [128, SEQ], F32, st), sb("Rb", [128, SEQ], F32, st)]
                    Ib_ = [sb("If", [128, SEQ], F32, st), sb("Ib", [128, SEQ], F32, st)]
                    M0 = sb("M0", [128, SEQ], F32, st)
                    g2 = sb("g2", [128, SEQ], BF16, st)
                    tmp = sb("tmpA", [128, 2, 512], F32, st)
                    Mb_ = [M0[:, :], xr[:, 2:2 + SEQ]]
                    Mk = ['M0', 'xr']
                    s_xr = wget(tbk(0))
                    S.emit('pool', lambda: nc.gpsimd.memset(xr[:, 0:2], 0.0), [], ['xr'])
                    S.emit('pool', lambda: nc.gpsimd.memset(xr[:, 2 + SEQ:4 + SEQ], 0.0), [], ['xr'])
                    for t in range(4):
                        pb, kb = proj(s_xr, t, (0, 1, 2, 3))
                        S.emit('act', lambda: nc.scalar.copy(out=xr[:, 2 + t * 512:2 + (t + 1) * 512], in_=pb), [kb], ['xr'])
                    wdone(tbk(0))
                    cw = lambda k_: pvec[:, CW + k_ * 8 + c:CW + k_ * 8 + c + 1]
                    S.emit('pool', lambda: nc.gpsimd.tensor_scalar(out=M0[:, :], in0=xr[:, 0:SEQ], scalar1=cw(0),
                                                                   scalar2=pvec[:, CB + c:CB + c + 1], op0=ALU.mult, op1=ALU.add),
                           ['xr', 'pvec'], ['M0'])
                    for k_ in range(1, 4):
                        S.emit('dve', lambda: nc.vector.scalar_tensor_tensor(out=M0[:, :], in0=xr[:, k_:k_ + SEQ], scalar=cw(k_),
                                                                              in1=M0[:, :], op0=ALU.mult, op1=ALU.add),
                               ['xr', 'pvec', 'M0'], ['M0'])
                    S.emit('act', lambda: nc.scalar.copy(out=xcb[:, :], in_=M0[:, :]), ['M0'], ['xcb'])
                    if c > 0:
                        wout_accum(c - 1)
                    for d in range(2):
                        for g_, (buf, bk) in enumerate(((Rb_[d], f'R{d}'), (Ib_[d], f'I{d}'))):
                            gi = (d * 2 + g_) * 8 + c
                            for t in range(4):
                                pb, kb = bank_ap((d * 8 + g_ * 4 + t) % 4)
                                mm(pb, gwb[:, gi, :], xcb[:, tl(t)], True, True, ['gwb', 'xcb'], [kb])
                                S.emit('act', lambda: nc.scalar.activation(out=buf[:, tl(t)], in_=pb, func=AF.Sigmoid,
                                                                           bias=pvec[:, BG + gi:BG + gi + 1], scale=1.0),
                                       [kb, 'pvec'], [bk])
                    for d in range(2):
                        S.emit('act', lambda: nc.scalar.activation(out=Mb_[d], in_=Rb_[d][:, :], func=AF.Exp,
                                                                   scale=cA[:, 16 + d * 8 + c:16 + d * 8 + c + 1]),
                               [f'R{d}', 'cA'], [Mk[d]])
                        S.emit('act', lambda: nc.scalar.activation(out=Rb_[d][:, :], in_=Rb_[d][:, :], func=AF.Exp,
                                                                   scale=cA[:, d * 8 + c:d * 8 + c + 1]),
                               [f'R{d}', 'cA'], [f'R{d}'])
                    for d in range(2):
                        S.emit('act', lambda: nc.scalar.activation(out=Mb_[d], in_=Mb_[d], func=AF.Sqrt,
                                                                   bias=cst[:, 1:2], scale=-1.0), [Mk[d], 'cst'], [Mk[d]])
                    for d, e in ((0, 'dve'), (1, 'dve')):
                        eo = nc.vector if e == 'dve' else nc.gpsimd
                        S.emit(e, lambda: eo.tensor_tensor(out=Ib_[d][:, :], in0=Ib_[d][:, :], in1=xcb[:, :], op=ALU.mult),
                               [f'I{d}', 'xcb'], [f'I{d}'])
                        S.emit(e, lambda: eo.tensor_tensor(out=Ib_[d][:, :], in0=Ib_[d][:, :], in1=Mb_[d], op=ALU.mult),
                               [f'I{d}', Mk[d]], [f'I{d}'])
                    s_gr = wget(tbk(1))
                    for t in range(4):
                        pb, kb = proj(s_gr, t, (0, 1, 2, 3)); b = t % 2
                        S.emit('act', lambda: nc.scalar.activation(out=tmp[:, b, :], in_=pb, func=AF.Square, scale=0.21145921592969426),
                               [kb], [f'tmpA{b}'])
                        S.emit('dve', lambda: nc.vector.scalar_tensor_tensor(out=tmp[:, b, :], in0=tmp[:, b, :], scalar=1.0, in1=pb,
                                                                             op0=ALU.add, op1=ALU.mult),
                               [f'tmpA{b}', kb], [f'tmpA{b}'])
                        S.emit('act', lambda: nc.scalar.activation(out=tmp[:, b, :], in_=tmp[:, b, :], func=AF.Sigmoid,
                                                                   scale=1.5957691216057308), [f'tmpA{b}'], [f'tmpA{b}'])
                        S.emit('dve', lambda: nc.vector.tensor_tensor(out=g2[:, tl(t)], in0=tmp[:, b, :], in1=pb, op=ALU.mult),
                               [f'tmpA{b}', kb], [f'g2.{t}'])
                    wdone(tbk(1))
                    s_ga = wget(tbk(2))
                    for t in range(4):
                        pb, kb = proj(s_ga, t, (0, 1, 2, 3)); b = t % 2
                        S.emit('act', lambda: nc.scalar.activation(out=tmp[:, b, :], in_=pb, func=AF.Sigmoid), [kb], [f'tmpA{b}'])
                        S.emit('dve', lambda: nc.vector.tensor_tensor(out=g2[:, tl(t)], in0=g2[:, tl(t)], in1=tmp[:, b, :], op=ALU.mult),
                               [f'tmpA{b}', f'g2.{t}'], [f'g2.{t}'])
                    wdone(tbk(2))
                    S.emit('dve', lambda: nc.vector.tensor_tensor_scan(out=Mb_[0], data0=Rb_[0][:, :], data1=Ib_[0][:, :],
                                                                       initial=0.0, op0=ALU.mult, op1=ALU.add),
                           ['R0', 'I0'], ['M0'])
                    S.emit('dve', lambda: nc.vector.tensor_tensor_scan(out=Mb_[1][:, ::-1], data0=Rb_[1][:, ::-1],
                                                                       data1=Ib_[1][:, ::-1], initial=0.0,
                                                                       op0=ALU.mult, op1=ALU.add),
                           ['R1', 'I1'], ['xr'])
                    S.emit('dve', lambda: nc.vector.tensor_tensor(out=Mb_[0], in0=Mb_[0], in1=Mb_[1], op=ALU.add),
                           ['M0', 'xr'], ['M0'])
                    S.emit('dve', lambda: nc.vector.tensor_tensor(out=yA[:, :], in0=Mb_[0], in1=g2[:, :], op=ALU.mult),
                           ['M0'] + [f'g2.{t}' for t in range(4)], ['yA'])
                S.fence()
                with ExitStack() as st:
                    qn = sb("qn", [128, SEQ], BF16, st); kn = sb("kn", [128, SEQ], BF16, st)
                    Vc = sb("Vc", [128, 16, 2, 65], BF16, st)
                    sgb = sb("sgb", [128, SEQ], BF16, st)
                    Et = sb("Et", [128, 2, ETAB_COLS], BF16, st)
                    pT = sb("pT", [128, 4, 640], BF16, st)
                    sq = sb("bsq", [128, 2, 512], BF16, st)
                    sd = sb("bsd", [128, 2, 512], F32, st)
                    ybt = sb("ybt", [128, 2, 128], BF16, st)
                    rec = sb("rec", [128, 4], F32, st)
                    ytmp = sb("ytmp", [128, 2, 512], BF16, st)
                    for hh in range(2):
                        S.dma(lambda: nc.sync.dma_start(out=Et[:, hh, :], in_=escr[2 * c + hh]), f'et{hh}', writes=[f'Et{hh}'])
                    s_gb = wget(tbk(3))
                    for t in range(4):
                        pb, kb = proj(s_gb, t, (0, 1, 2, 3, 4, 5))
                        S.emit('act', lambda: nc.scalar.activation(out=sgb[:, tl(t)], in_=pb, func=AF.Sigmoid), [kb], [f'sgb{t}'])
                    wdone(tbk(3))
                    for wi, (dst, dk, col) in enumerate(((qn, 'qn', 0), (kn, 'kn', 1))):
                        s_ = wget(tbk(4 + wi))
                        for t in range(4):
                            pb, kb = proj(s_, t, (0, 1, 2, 3, 4, 5)); b = t % 2
                            S.emit('act', lambda: nc.scalar.activation(out=sq[:, b, :], in_=pb, func=AF.Square), [kb], [f'bsq{b}'])
                            mm(PSD[:, :], bones[:], sq[:, b, :], True, True, ['bones', f'bsq{b}'], PSD_KEYS)
                            S.emit('act', lambda: nc.scalar.activation(out=sd[:, b, :], in_=PSD[:, :], func=AF.Ln,
                                                                       bias=cst[:, 0:1], scale=1.0 / 64), PSD_KEYS + ['cst'], [f'bsd{b}'])
                            S.emit('act', lambda: nc.scalar.activation(out=sd[:, b, :], in_=sd[:, b, :], func=AF.Exp, scale=-0.5),
                                   [f'bsd{b}'], [f'bsd{b}'])
                            S.emit('dve', lambda: nc.vector.scalar_tensor_tensor(out=dst[:, tl(t)], in0=pb, scalar=qk[:, col:col + 1],
                                                                                 in1=sd[:, b, :], op0=ALU.mult, op1=ALU.mult),
                                   [kb, 'qk', f'bsd{b}'], [f'{dk}{t}'])
                        wdone(tbk(4 + wi))
                    s_v = wget(tbk(6))
                    S.emit('pool', lambda: nc.gpsimd.memset(Vc[:, :, :, 64:65], 1.0), [], ['Vones'])
                    for tq in range(4):
                        pb, kb = bank_ap(4 + tq % 2)
                        for tt in range(4):
                            tok = slice((tq * 4 + tt) * 128, (tq * 4 + tt + 1) * 128)
                            for cc in range(8):
                                mm(pb[:, tt * 128:(tt + 1) * 128], hT[:, cc, tok], ring[:, s_v, cc * 128:(cc + 1) * 128],
                                   cc == 0, cc == 7, [f'ring{s_v}', f'hT{cc}.{tq}'], [kb])
                        S.emit('act', lambda: nc.scalar.copy(out=Vc[:, tq * 4:tq * 4 + 4, :, 0:64],
                                                             in_=pb.rearrange("p (a h d) -> p a h d", a=4, h=2)),
                               [kb], [f'Vc{tq}'])
                    wdone(tbk(6))
                    iters = [(m, hh) for m in range(16) for hh in range(2)]

                    def sbuf_of(it):
                        return [(PSA, ['psA0', 'psA1']), (PSB, ['psB0', 'psB1'])][it % 2]

                    def scores(it):
                        m, hh = iters[it]
                        off, js = lay[m]; hb = hh * 64
                        SP, spk = sbuf_of(it)
                        for jj, j in enumerate(js):
                            mm(SP[:, jj * 128:(jj + 1) * 128], kn[hb:hb + 64, j * 128:(j + 1) * 128],
                               qn[hb:hb + 64, m * 128:(m + 1) * 128], True, True,
                               [f'kn{j // 4}', f'qn{m // 4}'], [spk[jj // 4]])

                    def stage_a(it):
                        m, hh = iters[it]
                        off, js = lay[m]; n = len(js)
                        SP, spk = sbuf_of(it)
                        pslot = it % 4
                        S.emit('act', lambda: nc.scalar.activation(out=pT[:, pslot, 0:n * 128], in_=SP[:, 0:n * 128], func=AF.Exp),
                               spk, [f'pT{pslot}'])
                        if it % 2 == 0:
                            S.emit('pool', lambda: nc.gpsimd.tensor_tensor(out=pT[:, pslot, 0:n * 128], in0=pT[:, pslot, 0:n * 128],
                                                                           in1=Et[:, hh, off:off + n * 128], op=ALU.mult),
                                   [f'pT{pslot}', f'Et{hh}'], [f'pT{pslot}'])
                        else:
                            S.emit('dve', lambda: nc.vector.tensor_tensor(out=pT[:, pslot, 0:n * 128], in0=pT[:, pslot, 0:n * 128],
                                                                          in1=Et[:, hh, off:off + n * 128], op=ALU.mult),
                                   [f'pT{pslot}', f'Et{hh}'], [f'pT{pslot}'])

                    def stage_b(it):
                        m, hh = iters[it]
                        off, js = lay[m]; n = len(js); t_q = m // 4; ys = m % 2; hb = hh * 64
                        pslot = it % 4
                        ops, okey = [(PSC[:, 0:65], 'psC0'), (PSC[:, 512:577], 'psC1'), (PSD[:, 0:65], 'psD')][it % 3]
                        for jj, j in enumerate(js):
                            mm(ops, pT[:, pslot, jj * 128:(jj + 1) * 128], Vc[:, j, hh, :], jj == 0, jj == n - 1,
                               [f'pT{pslot}', f'Vc{j // 4}', 'Vones'], [okey])
                        rcol = it % 4
                        S.emit('dve', lambda: nc.vector.reciprocal(out=rec[:, rcol:rcol + 1], in_=ops[:, 64:65]),
                               [okey], [f'rec{rcol}'])
                        S.emit('dve', lambda: nc.vector.tensor_scalar(out=ybt[:, ys, hb:hb + 64], in0=ops[:, 0:64],
                                                                      scalar1=rec[:, rcol:rcol + 1], scalar2=None, op0=ALU.mult),
                               [okey, f'rec{rcol}'], [f'ybt{ys}h{hh}'])
                        if hh == 1:
                            pend.append(lambda m=m, t_q=t_q, ys=ys: tail(m, t_q, ys))

                    def tail(m, t_q, ys):
                        if True:
                            half = t_q % 2
                            S.emit('pe', lambda: nc.tensor.transpose(PST[:, half * 512 + (m % 4) * 128:half * 512 + (m % 4 + 1) * 128],
                                                                     ybt[:, ys, :], identb[:]),
                                   [f'ybt{ys}h0', f'ybt{ys}h1', 'identb'], ['psT'])
                            if m % 4 == 3:
                                S.emit('dve', lambda: nc.vector.tensor_tensor(out=ytmp[:, half, :], in0=PST[:, half * 512:(half + 1) * 512],
                                                                              in1=sgb[:, tl(t_q)], op=ALU.mult),
                                       ['psT', f'sgb{t_q}'], [f'ytmp{half}'])
                                S.emit('pool', lambda: nc.gpsimd.tensor_tensor(out=ytc[:, tl(t_q)], in0=ytmp[:, half, :],
                                                                               in1=yA[:, tl(t_q)], op=ALU.add),
                                       [f'ytmp{half}', 'yA'], [f'ytc{c % 2}.{t_q}'])

                    pend = []
                    scores(0); stage_a(0); scores(1); stage_a(1)
                    for it in range(len(iters)):
                        if it + 2 < len(iters):
                            scores(it + 2)
                            stage_a(it + 2)
                        todo = pend[:]; del pend[:]
                        stage_b(it)
                        for f_ in todo:
                            f_()
                    for f_ in pend:
                        f_()
                S.fence()
        wout_accum(7)
        S.fence()

    for s in range(nseq):
        row0 = s * SEQ
        wb = s * TILES_PER_SEQ
        with ExitStack() as st:
            xin = sb("xin", [128, 2, D], F32, st)
            for tt in range(16):
                sl = tt % 2
                S.dma(lambda: nc.sync.dma_start(out=xin[:, sl, :], in_=x[row0 + tt * 128:row0 + (tt + 1) * 128, :]),
                      f'xin{sl}', writes=[f'xin{sl}'])
                P_, pkeys = (PSA, ['psA0', 'psA1']) if tt % 2 == 0 else (PSB, ['psB0', 'psB1'])
                for c in range(8):
                    S.emit('pe', lambda: nc.tensor.transpose(P_[:, c * 128:(c + 1) * 128], xin[:, sl, c * 128:(c + 1) * 128], ident[:]),
                           [f'xin{sl}', 'ident'], [pkeys[c // 4]])
                t = tt // 4
                for hf in range(2):
                    dst = xT[:, hf * 4:hf * 4 + 4, tt * 128:(tt + 1) * 128]
                    src = P_[:, hf * 512:(hf + 1) * 512].rearrange("p (c j) -> p c j", j=128)
                    wk = [f'xT{c}.{t}' for c in range(hf * 4, hf * 4 + 4)]
                    if t % 2 == 0:
                        S.emit('act', lambda: nc.scalar.copy(out=dst, in_=src), [pkeys[hf]], wk)
                    else:
                        S.emit('dve', lambda: nc.vector.tensor_copy(out=dst, in_=src), [pkeys[hf]], wk)
        S.fence()
        if stop_after != 'load':
            ffn(wb + FFN1_BASE, G1)
            if stop_after != 'ffn1':
                with ExitStack() as mst:
                    mixer(wb + MIX_BASE, mst)
                if stop_after != 'mixer':
                    ffn(wb + FFN2_BASE, G2)
        with ExitStack() as st:
            xo = sb("xo", [128, 2, D], F32, st)
            for tt in range(16):
                sl = tt % 2; t = tt // 4
                P_, pkeys = (PSA, ['psA0', 'psA1']) if tt % 2 == 0 else (PSB, ['psB0', 'psB1'])
                for c in range(8):
                    S.emit('pe', lambda: nc.tensor.transpose(P_[:, c * 128:(c + 1) * 128], xT[:, c, tt * 128:(tt + 1) * 128], ident[:]),
                           [f'xT{c}.{t}', 'ident'], [pkeys[c // 4]])
                for hf in range(2):
                    if hf == 0:
                        S.emit('act', lambda: nc.scalar.copy(out=xo[:, sl, 0:512], in_=P_[:, 0:512]), [pkeys[0]], [f'xo{sl}'])
                    else:
                        S.emit('dve', lambda: nc.vector.tensor_copy(out=xo[:, sl, 512:1024], in_=P_[:, 512:1024]), [pkeys[1]], [f'xo{sl}'])
                S.dma(lambda: nc.sync.dma_start(out=out[row0 + tt * 128:row0 + (tt + 1) * 128, :], in_=xo[:, sl, :]),
                      f'xo{sl}', reads=[f'xo{sl}'])
            S.wait_dma_all('sp')
        S.fence()
    S.wait_dma_all('sp')
    top.close()
    return nc


def _host_layout(inputs):
    f = lambda k: np.ascontiguousarray(np.asarray(inputs[k], dtype=np.float32))
    pc = lambda v: np.ascontiguousarray(v.reshape(8, 128).T)
    cols = [pc(f('norm_ffn1')[0]), pc(f('norm_mix')[0]), pc(f('norm_ffn2')[0])]
    cw = f('conv_w')[0]
    cols += [pc(cw[k]) for k in range(4)]
    cols += [pc(f('conv_b')[0])]
    bgt = f('lru_b_gates')[0]
    cols += [pc(bgt[d, g]) for d in range(2) for g in range(2)]
    lam = f('lru_lambda')[0]
    cols += [pc(lam[d]) for d in range(2)]
    pvec = np.ascontiguousarray(np.concatenate(cols, axis=1))
    assert pvec.shape == (128, NP_COLS)
    wg = f('lru_w_gates')[0]
    gw = np.zeros((128, 32, 128), np.float32)
    for d in range(2):
        for g in range(2):
            for c in range(8):
                for bl in range(2):
                    gw[bl * 64:(bl + 1) * 64, (d * 2 + g) * 8 + c, bl * 64:(bl + 1) * 64] = wg[d, g, 2 * c + bl]
    qk = np.stack([np.tile(f('q_norm')[0], 2), np.tile(f('k_norm')[0], 2)], axis=1)
    dr, dc, valid = _build_index_tables()
    rpb = f('rel_pos_bias')[0]
    eb = rpb[:, dr, dc]
    eb[:, ~valid] = np.float32(NEG)
    return {
        "w_ffn1_gu": f('w_ffn1_gu')[0], "w_ffn2_gu": f('w_ffn2_gu')[0],
        "w_ffn1_down": f('w_ffn1_down')[0], "w_ffn2_down": f('w_ffn2_down')[0],
        "w_in": f('w_in')[0], "w_out": f('w_out')[0],
        "pvec": pvec, "gw": np.ascontiguousarray(gw.reshape(128, 32 * 128)),
        "qk": np.ascontiguousarray(qk.astype(np.float32)),
        "ident": np.eye(128, dtype=np.float32),
        "ebias": np.ascontiguousarray(eb.astype(np.float32)),
    }


_NC_CACHE = {}


def kernel(**inputs):
    shared = _host_layout(inputs)
    x = np.asarray(inputs['x'], dtype=np.float32)
    B = x.shape[0]
    per = B // NCORES
    if 'nc' not in _NC_CACHE:
        _NC_CACHE['nc'] = build_nc(per)
    nc = _NC_CACHE['nc']
    in_maps = []
    for i in range(NCORES):
        m = dict(shared)
        m["x"] = np.ascontiguousarray(x[i * per:(i + 1) * per].reshape(per * SEQ, D))
        in_maps.append(m)
    res = run_bass_kernel_spmd(nc, in_maps, core_ids=list(range(NCORES)))
    outs = [np.asarray(r["out"]).reshape(per, SEQ, D) for r in res.results]
    return np.concatenate(outs, axis=0).astype(np.float32)
```

```python
import bisect
from contextlib import ExitStack

import numpy as np
import concourse.bass as bass
import concourse.mybir as mybir
from concourse.bass_utils import run_bass_kernel_spmd

F32 = mybir.dt.float32
BF16 = mybir.dt.bfloat16
AF = mybir.ActivationFunctionType
ALU = mybir.AluOpType

NCORES = 8
NSEQ = 4
SEQ = 2048
D = 1024
DFF = 2816
NFC = 22
EPS = 1e-6
NSLOT = 14
TILES_PER_SEQ = 196
FFN1_BASE, MIX_BASE, FFN2_BASE = 0, 66, 130
NP_COLS = 112


def _tix(c, k):
    if c == 0:
        return k
    base = 7 + (c - 1) * 8
    return base + k if k < 1 else base + 1 + k


def _wix(c):
    return 8 + 8 * c if c < 7 else 63
ETAB_COLS = 2688
NEG = -30000.0

GRID_W = 64; ROWS = 32; WIN_ROWS = 8; WIN_COLS = 16
SPECIAL = [0, 1, 14, 15]


def _rs_of(r):
    return int(np.clip(r - WIN_ROWS // 2, 0, ROWS - WIN_ROWS))


def _chunks_for_tile(m):
    lo = _rs_of(2 * m); hi = _rs_of(2 * m + 1) + WIN_ROWS - 1
    return list(range(lo // 2, hi // 2 + 1))


def _table_layout():
    lay = {}
    off = 640
    for m in range(16):
        js = _chunks_for_tile(m)
        if m in SPECIAL:
            lay[m] = (off, js); off += 512
        else:
            lay[m] = (0, js)
    return lay


def _build_index_tables():
    lay = _table_layout()
    dr = np.zeros((128, ETAB_COLS), np.int64); dc = np.zeros((128, ETAB_COLS), np.int64)
    valid = np.zeros((128, ETAB_COLS), bool)
    done_interior = False
    for m in range(16):
        off, js = lay[m]
        if m not in SPECIAL and done_interior:
            continue
        for jj, j in enumerate(js):
            for kl in range(2):
                for ql in range(2):
                    kr = 2 * j + kl; qr = 2 * m + ql
                    vr = _rs_of(qr) <= kr < _rs_of(qr) + WIN_ROWS
                    kc = np.arange(64)[:, None]; qc = np.arange(64)[None, :]
                    csq = np.clip(qc - WIN_COLS // 2, 0, GRID_W - WIN_COLS)
                    v = (kc >= csq) & (kc < csq + WIN_COLS) & vr
                    d_r = kr - qr + WIN_ROWS - 1
                    d_c = np.clip(kc - qc, -(WIN_COLS - 1), WIN_COLS - 1) + WIN_COLS - 1
                    sl = (slice(kl * 64, kl * 64 + 64),
                          slice(off + jj * 128 + ql * 64, off + jj * 128 + ql * 64 + 64))
                    valid[sl] = v
                    dr[sl] = np.where(v, d_r, 0)
                    dc[sl] = np.where(v, d_c, 0)
        if m not in SPECIAL:
            done_interior = True
    return dr, dc, valid


class Sched:
    def __init__(self, nc):
        self.nc = nc
        self.eng = {'pe': nc.tensor, 'act': nc.scalar, 'dve': nc.vector, 'pool': nc.gpsimd, 'sp': nc.sync}
        self.sem = {e: nc.alloc_semaphore('prog_' + e) for e in ['pe', 'act', 'dve', 'pool']}
        self.cnt = {e: 0 for e in self.sem}
        self.ins = {e: [] for e in self.sem}
        self.sig_idx = {e: [] for e in self.sem}
        self.sig_cnt = {e: [] for e in self.sem}
        self.obs = {e: {} for e in self.eng}
        self.lw = {}
        self.rd = {}
        self.dma_total = {}
        self.dma_sems = {}

    def cover(self, e, idx):
        pos = bisect.bisect_left(self.sig_idx[e], idx)
        if pos < len(self.sig_idx[e]):
            return self.sig_cnt[e][pos]
        last = len(self.ins[e]) - 1
        self.ins[e][last].then_inc(self.sem[e], 1)
        self.cnt[e] += 1
        self.sig_idx[e].append(last); self.sig_cnt[e].append(self.cnt[e])
        return self.cnt[e]

    def _wait(self, engine, evts):
        need = {}
        for ev in evts:
            if ev[0] == 'c':
                if ev[1] == engine and engine == 'pe':
                    continue
                key = ev[1]; sem = self.sem[key]; val = self.cover(key, ev[2])
            else:
                key = ev[1]; sem = self.dma_sems[key]; val = self.dma_total[key]
            if need.get(key, (None, 0))[1] < val:
                need[key] = (sem, val)
        for key, (sem, val) in need.items():
            if self.obs[engine].get(key, 0) < val:
                self.eng[engine].wait_ge(sem, val)
                self.obs[engine][key] = val

    def _deps(self, reads, writes):
        evts = set()
        for k in reads:
            if k in self.lw:
                evts.add(self.lw[k])
        for k in writes:
            if k in self.lw:
                evts.add(self.lw[k])
            for ev in self.rd.get(k, {}).values():
                evts.add(ev)
        return evts

    def emit(self, engine, fn, reads=(), writes=(), signal=True):
        self._wait(engine, self._deps(reads, writes))
        ins = fn()
        idx = len(self.ins[engine]); self.ins[engine].append(ins)
        if signal:
            ins.then_inc(self.sem[engine], 1)
            self.cnt[engine] += 1
            self.sig_idx[engine].append(idx); self.sig_cnt[engine].append(self.cnt[engine])
        ev = ('c', engine, idx)
        for k in writes:
            self.lw[k] = ev; self.rd[k] = {}
        for k in reads:
            if k not in writes:
                self.rd.setdefault(k, {})[engine] = ev
        return ins

    def dma(self, fn, semkey, reads=(), writes=(), queue='sp'):
        if semkey not in self.dma_sems:
            self.dma_sems[semkey] = self.nc.alloc_semaphore('dma_' + semkey)
            self.dma_total[semkey] = 0
        self._wait(queue, self._deps(reads, writes))
        ins = fn()
        ins.then_inc(self.dma_sems[semkey], 16)
        self.dma_total[semkey] += 16
        ev = ('d', semkey)
        for k in writes:
            self.lw[k] = ev; self.rd[k] = {}
        for k in reads:
            self.rd.setdefault(k, {})[ev] = ev
        return ins

    def fence(self):
        evs = [('c', e, len(self.ins[e]) - 1) for e in self.sem if self.ins[e]]
        for e in ['pe', 'act', 'dve', 'pool', 'sp']:
            self._wait(e, evs)

    def wait_dma_all(self, engine='sp'):
        self._wait(engine, [('d', k) for k in self.dma_sems])


def build_nc(nseq=NSEQ, stop_after=None):
    nc = bass.Bass("TRN2", target_bir_lowering=False)
    S = Sched(nc)
    ntok = nseq * SEQ
    dram = lambda n, s, dt=F32, kind="ExternalInput": nc.dram_tensor(n, s, dt, kind=kind).ap()
    x = dram("x", [ntok, D])
    w_gu = [dram("w_ffn1_gu", [D, 2 * DFF]), dram("w_ffn2_gu", [D, 2 * DFF])]
    w_dn = [dram("w_ffn1_down", [DFF, D]), dram("w_ffn2_down", [DFF, D])]
    w_in = dram("w_in", [D, 7 * D])
    w_out = dram("w_out", [D, D])
    pvec_d = dram("pvec", [128, NP_COLS])
    gw_d = dram("gw", [128, 32 * 128])
    qk_d = dram("qk", [128, 2])
    ident_d = dram("ident", [128, 128])
    ebias_d = dram("ebias", [16, 128, ETAB_COLS])
    out = dram("out", [ntok, D], kind="ExternalOutput")
    wscr = dram("wscr", [TILES_PER_SEQ, 128, 1024], BF16, kind="Internal")
    escr = dram("escr", [16, 128, ETAB_COLS], BF16, kind="Internal")

    lay = _table_layout()
    top = ExitStack()
    uid = [0]

    def sb(n, s, dt, st=top):
        uid[0] += 1
        return st.enter_context(nc.sbuf_tensor(f"s{uid[0]}_{n}", s, dt))

    ps = lambda n, s, dt: top.enter_context(nc.psum_tensor(n, s, dt))

    ident = sb("ident", [128, 128], F32)
    identb = sb("identb", [128, 128], BF16)
    onesb = sb("onesb", [128, 128], BF16)
    bones = sb("bones", [128, 128], BF16)
    pvec = sb("pvec", [128, NP_COLS], F32)
    gwb = sb("gwb", [128, 32, 128], BF16)
    qk = sb("qksc", [128, 2], F32)
    cst = sb("cst", [128, 4], F32)
    cA = sb("cA", [128, 32], F32)
    PSA = ps("psA", [128, 1024], F32); PSB = ps("psB", [128, 1024], F32); PSC = ps("psC", [128, 1024], F32)
    PSD = ps("psD", [128, 512], F32); PST = ps("psT", [128, 1024], BF16)
    banks = [(PSA, 0, 'psA0'), (PSA, 512, 'psA1'), (PSB, 0, 'psB0'), (PSB, 512, 'psB1'),
             (PSC, 0, 'psC0'), (PSC, 512, 'psC1')]
    PSD_KEYS = ['psD']

    def bank_ap(i, n=512):
        t, o, k = banks[i]
        return t[:, o:o + n], k

    G1, GM, G2, CW, CB, BG, LAM = 0, 8, 16, 24, 56, 64, 96

    S.dma(lambda: nc.sync.dma_start(out=ident[:], in_=ident_d[:, :]), 'c0', writes=['ident'])
    S.dma(lambda: nc.sync.dma_start(out=pvec[:], in_=pvec_d[:, :]), 'c0', writes=['pvec'])
    S.dma(lambda: nc.sync.dma_start(out=qk[:], in_=qk_d[:, :]), 'c0', writes=['qk'])
    S.emit('dve', lambda: nc.vector.tensor_copy(out=identb[:], in_=ident[:]), ['ident'], ['identb'])
    S.emit('dve', lambda: nc.vector.memset(onesb[:], 1.0), [], ['onesb'])
    S.emit('dve', lambda: nc.vector.memset(bones[:], 0.0), [], ['bones'])
    S.emit('dve', lambda: nc.vector.memset(bones[0:64, 0:64], 1.0), [], ['bones'])
    S.emit('dve', lambda: nc.vector.memset(bones[64:128, 64:128], 1.0), [], ['bones'])
    S.emit('dve', lambda: nc.vector.memset(cst[:, 0:1], EPS), [], ['cst'])
    S.emit('dve', lambda: nc.vector.memset(cst[:, 1:2], 1.0), [], ['cst'])
    S.emit('dve', lambda: nc.vector.tensor_scalar(out=qk[:, 0:1], in0=qk[:, 0:1], scalar1=0.125, scalar2=None,
                                                  op0=ALU.mult), ['qk'], ['qk'])

    with ExitStack() as pst:
        psb = lambda n, s, dt: sb(n, s, dt, pst)
        lx = psb("lx", [128, 16], F32); lp = psb("lp", [128, 16], F32); lb = psb("lb", [128, 16], F32)
        lm = psb("lm", [128, 16], F32)
        lam = pvec[:, LAM:LAM + 16]
        S.emit('act', lambda: nc.scalar.activation(out=lx[:], in_=lam, func=AF.Exp, scale=-1.0), ['pvec'], ['lx'])
        S.emit('act', lambda: nc.scalar.activation(out=lb[:], in_=lx[:], func=AF.Ln, bias=cst[:, 1:2], scale=1.0),
               ['lx', 'cst'], ['lb'])
        S.emit('dve', lambda: nc.vector.tensor_scalar(out=lp[:], in0=lx[:], scalar1=-0.2, scalar2=0.25,
                                                      op0=ALU.mult, op1=ALU.add), ['lx'], ['lp'])
        for coef in (1.0 / 3.0, 0.5, 1.0):
            S.emit('dve', lambda: nc.vector.tensor_tensor(out=lp[:], in0=lp[:], in1=lx[:], op=ALU.mult), ['lp', 'lx'], ['lp'])
            S.emit('dve', lambda: nc.vector.tensor_scalar(out=lp[:], in0=lp[:], scalar1=-1.0, scalar2=coef,
                                                          op0=ALU.mult, op1=ALU.add), ['lp'], ['lp'])
        S.emit('dve', lambda: nc.vector.tensor_tensor(out=lp[:], in0=lp[:], in1=lx[:], op=ALU.mult), ['lp', 'lx'], ['lp'])
        S.emit('dve', lambda: nc.vector.tensor_single_scalar(out=lm[:], in_=lx[:], scalar=0.05, op=ALU.is_lt), ['lx'], ['lm'])
        S.emit('dve', lambda: nc.vector.tensor_tensor(out=lp[:], in0=lp[:], in1=lb[:], op=ALU.subtract), ['lp', 'lb'], ['lp'])
        S.emit('dve', lambda: nc.vector.tensor_tensor(out=lp[:], in0=lp[:], in1=lm[:], op=ALU.mult), ['lp', 'lm'], ['lp'])
        S.emit('dve', lambda: nc.vector.tensor_tensor(out=lp[:], in0=lp[:], in1=lb[:], op=ALU.add), ['lp', 'lb'], ['lp'])
        S.emit('dve', lambda: nc.vector.tensor_scalar(out=cA[:, 0:16], in0=lp[:], scalar1=-8.0, scalar2=None, op0=ALU.mult),
               ['lp'], ['cA'])
        S.emit('dve', lambda: nc.vector.tensor_scalar(out=cA[:, 16:32], in0=lp[:], scalar1=-16.0, scalar2=None, op0=ALU.mult),
               ['lp'], ['cA'])

        stf = psb("stf", [128, 2, 7168], F32)
        stb = psb("stb", [128, 56, 8, 128], BF16)
        stbf = stb[:].rearrange("p n c j -> p (n c j)")
        cast_engs = ['act', 'dve']
        cnt = [0]

        def cast(out_ap, in_ap, rd, wr):
            e = cast_engs[cnt[0] % 2]; cnt[0] += 1
            if e == 'act':
                S.emit('act', lambda: nc.scalar.copy(out=out_ap, in_=in_ap), rd, wr)
            elif e == 'dve':
                S.emit('dve', lambda: nc.vector.tensor_copy(out=out_ap, in_=in_ap), rd, wr)
            else:
                S.emit('pool', lambda: nc.gpsimd.tensor_copy(out=out_ap, in_=in_ap), rd, wr)

        ld = [0]

        def stage_load(src_ap, ncols, inner=None):
            sl = ld[0] % 2; ld[0] += 1
            dst = stf[:, sl, 0:ncols]
            if inner is not None:
                dst = dst.rearrange("p (f n) -> p f n", n=inner)
            S.dma(lambda: nc.sync.dma_start(out=dst, in_=src_ap), f'stf{sl}', writes=[f'stf{sl}'])
            return sl

        def colchunk_weight(W, ncols, store_fn):
            nt = ncols // 128
            for cc in range(8):
                sl = stage_load(W[cc * 128:(cc + 1) * 128, :], ncols)
                cast(stb[:, 0:nt, cc, :], stf[:, sl, 0:ncols].rearrange("p (n j) -> p n j", j=128),
                     [f'stf{sl}'], ['stb'])
            store_fn()

        def store_tiles(n0, cnt_, i0, step):
            src = stb[:, n0:n0 + cnt_, :, :].rearrange("p n c j -> p n (c j)")
            dst = wscr[i0:i0 + step * (cnt_ - 1) + 1:step].rearrange("n p x -> p n x")
            S.dma(lambda: nc.sync.dma_start(out=dst, in_=src), 'stst', reads=['stb'])

        for fi, base in ((0, FFN1_BASE), (1, FFN2_BASE)):
            def st_gu(base=base):
                for G in range(2):
                    for kind in range(2):
                        store_tiles(kind * NFC + G * 11, 11, base + G * 33 + kind, 2)
            colchunk_weight(w_gu[fi], 2 * DFF, st_gu)
            for G in range(2):
                for half, (f0, nf) in enumerate(((0, 6), (6, 5))):
                    fa = G * 11 + f0
                    sl = stage_load(w_dn[fi][fa * 128:(fa + nf) * 128, :].rearrange("(f p) n -> p f n", p=128), nf * 1024, 1024)
                    cast(stbf[:, 0:nf * 1024], stf[:, sl, 0:nf * 1024], [f'stf{sl}'], ['stb'])
                    i0 = base + G * 33 + 22 + f0
                    S.dma(lambda: nc.sync.dma_start(out=wscr[i0:i0 + nf].rearrange("n p x -> p n x"),
                                                    in_=stbf[:, 0:nf * 1024].rearrange("p (n x) -> p n x", x=1024)),
                          'stst', reads=['stb'])
        KORD = [0, 1, 4, 5, 6, 2, 3]

        def st_in():
            for fs in range(7):
                store_tiles(fs * 8, 1, MIX_BASE + _tix(0, KORD[fs]), 1)
                store_tiles(fs * 8 + 1, 7, MIX_BASE + _tix(1, KORD[fs]), 8)
        colchunk_weight(w_in, 7 * D, st_in)
        for half in range(2):
            sl = stage_load(w_out[half * 512:(half + 1) * 512, :].rearrange("(c p) n -> p c n", p=128), 4096, 1024)
            cast(stbf[:, 0:4096], stf[:, sl, 0:4096], [f'stf{sl}'], ['stb'])
            for ci in range(4):
                i0 = MIX_BASE + _wix(half * 4 + ci)
                S.dma(lambda: nc.sync.dma_start(out=wscr[i0], in_=stbf[:, ci * 1024:(ci + 1) * 1024]),
                      'stst', reads=['stb'])
        sl = stage_load(gw_d[:, :], 4096)
        S.emit('dve', lambda: nc.vector.tensor_copy(out=gwb[:].rearrange("p a b -> p (a b)"), in_=stf[:, sl, 0:4096]),
               [f'stf{sl}'], ['gwb'])
        for h in range(16):
            sl = stage_load(ebias_d[h], ETAB_COLS)
            S.emit('act', lambda: nc.scalar.activation(out=stbf[:, 0:ETAB_COLS], in_=stf[:, sl, 0:ETAB_COLS], func=AF.Exp),
                   [f'stf{sl}'], ['stb'])
            S.dma(lambda: nc.sync.dma_start(out=escr[h], in_=stbf[:, 0:ETAB_COLS]), 'stst', reads=['stb'])
        S.wait_dma_all('sp')
        S.fence()

    xT = sb("xT", [128, 8, SEQ], F32)
    hT = sb("hT", [128, 8, SEQ], BF16)
    ring = sb("ring", [128, NSLOT, 1024], BF16)
    total_tiles = TILES_PER_SEQ * nseq
    wst = {'next': 0}

    def load_upto(n):
        while wst['next'] < min(n, total_tiles):
            i = wst['next']; s = i % NSLOT
            S.dma(lambda: nc.sync.dma_start(out=ring[:, s, :], in_=wscr[i % TILES_PER_SEQ]), f'ring{s}',
                  writes=[f'ring{s}'])
            wst['next'] += 1

    def wget(i):
        load_upto(i + 1)
        return i % NSLOT

    def wdone(i):
        load_upto(i + 1 + NSLOT)

    load_upto(NSLOT)

    def tl(t):
        return slice(t * 512, (t + 1) * 512)

    def mm(out_ap, lhsT, rhs, start, stop, rd, wr):
        S.emit('pe', lambda: nc.tensor.matmul(out_ap, lhsT, rhs, start=start, stop=stop), rd, wr, signal=bool(stop))

    def rmsnorm(gcol, st):
        sq = sb("nsq", [128, 2, 512], BF16, st)
        sd = sb("nsd", [128, 2, 512], F32, st)
        for t in range(4):
            for c in range(8):
                b = c % 2
                S.emit('act', lambda: nc.scalar.activation(out=sq[:, b, :], in_=xT[:, c, tl(t)], func=AF.Square),
                       [f'xT{c}.{t}'], [f'nsq{b}'])
                mm(PSD[:, :], onesb[:], sq[:, b, :], c == 0, c == 7, ['onesb', f'nsq{b}'], PSD_KEYS)
            b = t % 2
            S.emit('act', lambda: nc.scalar.activation(out=sd[:, b, :], in_=PSD[:, :], func=AF.Ln,
                                                       bias=cst[:, 0:1], scale=1.0 / D), PSD_KEYS + ['cst'], [f'nsd{b}'])
            S.emit('act', lambda: nc.scalar.activation(out=sd[:, b, :], in_=sd[:, b, :], func=AF.Exp, scale=-0.5),
                   [f'nsd{b}'], [f'nsd{b}'])
            for c in range(8):
                S.emit('dve', lambda: nc.vector.scalar_tensor_tensor(out=hT[:, c, tl(t)], in0=xT[:, c, tl(t)],
                                                                     scalar=pvec[:, gcol + c:gcol + c + 1],
                                                                     in1=sd[:, b, :], op0=ALU.mult, op1=ALU.mult),
                       [f'xT{c}.{t}', 'pvec', f'nsd{b}'], [f'hT{c}.{t}'])

    def ffn(tbase, gcol):
        with ExitStack() as st:
            rmsnorm(gcol, st)
            act = sb("actT", [128, 11, SEQ], BF16, st)
            sg = sb("sg", [128, 2, 512], F32, st)
            k = 0
            for G in range(2):
                gb_ = tbase + G * 33
                for fl in range(11):
                    sg_ = wget(gb_ + 2 * fl); su_ = wget(gb_ + 2 * fl + 1)
                    for t in range(4):
                        pg, kg = bank_ap(k % 2); pu, ku = bank_ap(2 + k % 2); b = k % 2; k += 1
                        for cc in range(8):
                            mm(pg, ring[:, sg_, cc * 128:(cc + 1) * 128], hT[:, cc, tl(t)], cc == 0, cc == 7,
                               [f'ring{sg_}', f'hT{cc}.{t}'], [kg])
                        for cc in range(8):
                            mm(pu, ring[:, su_, cc * 128:(cc + 1) * 128], hT[:, cc, tl(t)], cc == 0, cc == 7,
                               [f'ring{su_}', f'hT{cc}.{t}'], [ku])
                        S.emit('act', lambda: nc.scalar.activation(out=sg[:, b, :], in_=pg, func=AF.Silu), [kg], [f'sg{b}'])
                        S.emit('dve', lambda: nc.vector.tensor_tensor(out=act[:, fl, tl(t)], in0=sg[:, b, :], in1=pu, op=ALU.mult),
                               [f'sg{b}', ku], [f'act{fl}.{t}'])
                    wdone(gb_ + 2 * fl); wdone(gb_ + 2 * fl + 1)
                dslots = [wget(gb_ + 22 + fl) for fl in range(11)]
                kd = 0
                for o in range(8):
                    for t in range(4):
                        pd, kdk = bank_ap(4 + kd % 2); kd += 1
                        for fl in range(11):
                            mm(pd, ring[:, dslots[fl], o * 128:(o + 1) * 128], act[:, fl, tl(t)], fl == 0, fl == 10,
                               [f'ring{dslots[fl]}', f'act{fl}.{t}'], [kdk])
                        S.emit('dve', lambda: nc.vector.scalar_tensor_tensor(out=xT[:, o, tl(t)], in0=pd, scalar=0.5,
                                                                             in1=xT[:, o, tl(t)], op0=ALU.mult, op1=ALU.add),
                               [kdk, f'xT{o}.{t}'], [f'xT{o}.{t}'])
                for fl in range(11):
                    wdone(gb_ + 22 + fl)
        S.fence()

    pk = [0]

    def proj(slot, t, bank_ids=(4, 5)):
        pb, kb = bank_ap(bank_ids[pk[0] % len(bank_ids)]); pk[0] += 1
        for cc in range(8):
            mm(pb, ring[:, slot, cc * 128:(cc + 1) * 128], hT[:, cc, tl(t)], cc == 0, cc == 7,
               [f'ring{slot}', f'hT{cc}.{t}'], [kb])
        return pb, kb

    def mixer(tbase, mst):
        with ExitStack() as st:
            rmsnorm(GM, st)
        S.fence()
        ytcs = [sb("ytcA", [128, SEQ], BF16, mst), sb("ytcB", [128, SEQ], BF16, mst)]

        def wout_accum(cprev):
            ytp = ytcs[cprev % 2]
            s_wo = wget(tbase + _wix(cprev))
            ko = 0
            for o in range(8):
                for t in range(4):
                    pb, kb = bank_ap(4 + ko % 2); ko += 1
                    mm(pb, ring[:, s_wo, o * 128:(o + 1) * 128], ytp[:, tl(t)], True, True,
                       [f'ring{s_wo}', f'ytc{cprev % 2}.{t}'], [kb])
                    S.emit('dve', lambda: nc.vector.tensor_tensor(out=xT[:, o, tl(t)], in0=pb, in1=xT[:, o, tl(t)], op=ALU.add),
                           [kb, f'xT{o}.{t}'], [f'xT{o}.{t}'])
            wdone(tbase + _wix(cprev))

        for c in range(8):
            tbk = lambda k_: tbase + _tix(c, k_)
            ytc = ytcs[c % 2]
            with ExitStack() as cst_:
                yA = sb("yA", [128, SEQ], BF16, cst_)
                with ExitStack() as st:
                    xr = sb("xr", [128, SEQ + 4], F32, st)
                    xcb = sb("xcb", [128, SEQ], BF16, st)
                    Rb_ = [sb("Rf", [128, SEQ], F32, st), sb("Rb", [128, SEQ], F32, st)]
                    Ib_ = [sb("If", [128, SEQ], F32, st), sb("Ib", [128, SEQ], F32, st)]
                    M0 = sb("M0", [128, SEQ], F32, st)
                    g2 = sb("g2", [128, SEQ], BF16, st)
                    tmp = sb("tmpA", [128, 2, 512], F32, st)
                    Mb_ = [M0[:, :], xr[:, 2:2 + SEQ]]
                    Mk = ['M0', 'xr']
                    s_xr = wget(tbk(0))
                    S.emit('pool', lambda: nc.gpsimd.memset(xr[:, 0:2], 0.0), [], ['xr'])
                    S.emit('pool', lambda: nc.gpsimd.memset(xr[:, 2 + SEQ:4 + SEQ], 0.0), [], ['xr'])
                    for t in range(4):
                        pb, kb = proj(s_xr, t, (0, 1, 2, 3))
                        S.emit('act', lambda: nc.scalar.copy(out=xr[:, 2 + t * 512:2 + (t + 1) * 512], in_=pb), [kb], ['xr'])
                    wdone(tbk(0))
                    cw = lambda k_: pvec[:, CW + k_ * 8 + c:CW + k_ * 8 + c + 1]
                    S.emit('pool', lambda: nc.gpsimd.tensor_scalar(out=M0[:, :], in0=xr[:, 0:SEQ], scalar1=cw(0),
                                                                   scalar2=pvec[:, CB + c:CB + c + 1], op0=ALU.mult, op1=ALU.add),
                           ['xr', 'pvec'], ['M0'])
                    for k_ in range(1, 4):
                        S.emit('dve', lambda: nc.vector.scalar_tensor_tensor(out=M0[:, :], in0=xr[:, k_:k_ + SEQ], scalar=cw(k_),
                                                                              in1=M0[:, :], op0=ALU.mult, op1=ALU.add),
                               ['xr', 'pvec', 'M0'], ['M0'])
                    S.emit('act', lambda: nc.scalar.copy(out=xcb[:, :], in_=M0[:, :]), ['M0'], ['xcb'])
                    for d in range(2):
                        for g_, (buf, bk) in enumerate(((Rb_[d], f'R{d}'), (Ib_[d], f'I{d}'))):
                            gi = (d * 2 + g_) * 8 + c
                            for t in range(4):
                                pb, kb = bank_ap((d * 8 + g_ * 4 + t) % 4)
                                mm(pb, gwb[:, gi, :], xcb[:, tl(t)], True, True, ['gwb', 'xcb'], [kb])
                                S.emit('act', lambda: nc.scalar.activation(out=buf[:, tl(t)], in_=pb, func=AF.Sigmoid,
                                                                           bias=pvec[:, BG + gi:BG + gi + 1], scale=1.0),
                                       [kb, 'pvec'], [bk])
                    if c > 0:
                        wout_accum(c - 1)
                    for d in range(2):
                        S.emit('act', lambda: nc.scalar.activation(out=Mb_[d], in_=Rb_[d][:, :], func=AF.Exp,
                                                                   scale=cA[:, 16 + d * 8 + c:16 + d * 8 + c + 1]),
                               [f'R{d}', 'cA'], [Mk[d]])
                        S.emit('act', lambda: nc.scalar.activation(out=Rb_[d][:, :], in_=Rb_[d][:, :], func=AF.Exp,
                                                                   scale=cA[:, d * 8 + c:d * 8 + c + 1]),
                               [f'R{d}', 'cA'], [f'R{d}'])
                    for d in range(2):
                        S.emit('act', lambda: nc.scalar.activation(out=Mb_[d], in_=Mb_[d], func=AF.Sqrt,
                                                                   bias=cst[:, 1:2], scale=-1.0), [Mk[d], 'cst'], [Mk[d]])
                    for d, e in ((0, 'dve'), (1, 'dve')):
                        eo = nc.vector if e == 'dve' else nc.gpsimd
                        S.emit(e, lambda: eo.tensor_tensor(out=Ib_[d][:, :], in0=Ib_[d][:, :], in1=xcb[:, :], op=ALU.mult),
                               [f'I{d}', 'xcb'], [f'I{d}'])
                        S.emit(e, lambda: eo.tensor_tensor(out=Ib_[d][:, :], in0=Ib_[d][:, :], in1=Mb_[d], op=ALU.mult),
                               [f'I{d}', Mk[d]], [f'I{d}'])
                    s_gr = wget(tbk(1))
                    for t in range(4):
                        pb, kb = proj(s_gr, t, (0, 1, 2, 3)); b = t % 2
                        S.emit('act', lambda: nc.scalar.activation(out=tmp[:, b, :], in_=pb, func=AF.Square, scale=0.21145921592969426),
                               [kb], [f'tmpA{b}'])
                        S.emit('dve', lambda: nc.vector.scalar_tensor_tensor(out=tmp[:, b, :], in0=tmp[:, b, :], scalar=1.0, in1=pb,
                                                                             op0=ALU.add, op1=ALU.mult),
                               [f'tmpA{b}', kb], [f'tmpA{b}'])
                        S.emit('act', lambda: nc.scalar.activation(out=tmp[:, b, :], in_=tmp[:, b, :], func=AF.Sigmoid,
                                                                   scale=1.5957691216057308), [f'tmpA{b}'], [f'tmpA{b}'])
                        S.emit('dve', lambda: nc.vector.tensor_tensor(out=g2[:, tl(t)], in0=tmp[:, b, :], in1=pb, op=ALU.mult),
                               [f'tmpA{b}', kb], [f'g2.{t}'])
                    wdone(tbk(1))
                    s_ga = wget(tbk(2))
                    for t in range(4):
                        pb, kb = proj(s_ga, t, (0, 1, 2, 3)); b = t % 2
                        S.emit('act', lambda: nc.scalar.activation(out=tmp[:, b, :], in_=pb, func=AF.Sigmoid), [kb], [f'tmpA{b}'])
                        S.emit('dve', lambda: nc.vector.tensor_tensor(out=g2[:, tl(t)], in0=g2[:, tl(t)], in1=tmp[:, b, :], op=ALU.mult),
                               [f'tmpA{b}', f'g2.{t}'], [f'g2.{t}'])
                    wdone(tbk(2))
                    S.emit('dve', lambda: nc.vector.tensor_tensor_scan(out=Mb_[0], data0=Rb_[0][:, :], data1=Ib_[0][:, :],
                                                                       initial=0.0, op0=ALU.mult, op1=ALU.add),
                           ['R0', 'I0'], ['M0'])
                    S.emit('dve', lambda: nc.vector.tensor_tensor_scan(out=Mb_[1][:, ::-1], data0=Rb_[1][:, ::-1],
                                                                       data1=Ib_[1][:, ::-1], initial=0.0,
                                                                       op0=ALU.mult, op1=ALU.add),
                           ['R1', 'I1'], ['xr'])
                    S.emit('dve', lambda: nc.vector.tensor_tensor(out=Mb_[0], in0=Mb_[0], in1=Mb_[1], op=ALU.add),
                           ['M0', 'xr'], ['M0'])
                    S.emit('dve', lambda: nc.vector.tensor_tensor(out=yA[:, :], in0=Mb_[0], in1=g2[:, :], op=ALU.mult),
                           ['M0'] + [f'g2.{t}' for t in range(4)], ['yA'])
                S.fence()
                with ExitStack() as st:
                    qn = sb("qn", [128, SEQ], BF16, st); kn = sb("kn", [128, SEQ], BF16, st)
                    Vc = sb("Vc", [128, 16, 2, 65], BF16, st)
                    sgb = sb("sgb", [128, SEQ], BF16, st)
                    Et = sb("Et", [128, 2, ETAB_COLS], BF16, st)
                    pT = sb("pT", [128, 4, 640], BF16, st)
                    sq = sb("bsq", [128, 2, 512], BF16, st)
                    sd = sb("bsd", [128, 2, 512], F32, st)
                    ybt = sb("ybt", [128, 2, 128], BF16, st)
                    rec = sb("rec", [128, 4], F32, st)
                    ytmp = sb("ytmp", [128, 2, 512], BF16, st)
                    for hh in range(2):
                        S.dma(lambda: nc.sync.dma_start(out=Et[:, hh, :], in_=escr[2 * c + hh]), f'et{hh}', writes=[f'Et{hh}'])
                    s_gb = wget(tbk(3))
                    for t in range(4):
                        pb, kb = proj(s_gb, t, (0, 1, 2, 3, 4, 5))
                        S.emit('act', lambda: nc.scalar.activation(out=sgb[:, tl(t)], in_=pb, func=AF.Sigmoid), [kb], [f'sgb{t}'])
                    wdone(tbk(3))
                    for wi, (dst, dk, col) in enumerate(((qn, 'qn', 0), (kn, 'kn', 1))):
                        s_ = wget(tbk(4 + wi))
                        for t in range(4):
                            pb, kb = proj(s_, t, (0, 1, 2, 3, 4, 5)); b = t % 2
                            S.emit('act', lambda: nc.scalar.activation(out=sq[:, b, :], in_=pb, func=AF.Square), [kb], [f'bsq{b}'])
                            mm(PSD[:, :], bones[:], sq[:, b, :], True, True, ['bones', f'bsq{b}'], PSD_KEYS)
                            S.emit('act', lambda: nc.scalar.activation(out=sd[:, b, :], in_=PSD[:, :], func=AF.Ln,
                                                                       bias=cst[:, 0:1], scale=1.0 / 64), PSD_KEYS + ['cst'], [f'bsd{b}'])
                            S.emit('act', lambda: nc.scalar.activation(out=sd[:, b, :], in_=sd[:, b, :], func=AF.Exp, scale=-0.5),
                                   [f'bsd{b}'], [f'bsd{b}'])
                            S.emit('dve', lambda: nc.vector.scalar_tensor_tensor(out=dst[:, tl(t)], in0=pb, scalar=qk[:, col:col + 1],
                                                                                 in1=sd[:, b, :], op0=ALU.mult, op1=ALU.mult),
                                   [kb, 'qk', f'bsd{b}'], [f'{dk}{t}'])
                        wdone(tbk(4 + wi))
                    s_v = wget(tbk(6))
                    S.emit('pool', lambda: nc.gpsimd.memset(Vc[:, :, :, 64:65], 1.0), [], ['Vones'])
                    for tq in range(4):
                        pb, kb = bank_ap(4 + tq % 2)
                        for tt in range(4):
                            tok = slice((tq * 4 + tt) * 128, (tq * 4 + tt + 1) * 128)
                            for cc in range(8):
                                mm(pb[:, tt * 128:(tt + 1) * 128], hT[:, cc, tok], ring[:, s_v, cc * 128:(cc + 1) * 128],
                                   cc == 0, cc == 7, [f'ring{s_v}', f'hT{cc}.{tq}'], [kb])
                        S.emit('act', lambda: nc.scalar.copy(out=Vc[:, tq * 4:tq * 4 + 4, :, 0:64],
                                                             in_=pb.rearrange("p (a h d) -> p a h d", a=4, h=2)),
                               [kb], [f'Vc{tq}'])
                    wdone(tbk(6))
                    iters = [(m, hh) for m in range(16) for hh in range(2)]

                    def sbuf_of(it):
                        return [(PSA, ['psA0', 'psA1']), (PSB, ['psB0', 'psB1'])][it % 2]

                    def scores(it):
                        m, hh = iters[it]
                        off, js = lay[m]; hb = hh * 64
                        SP, spk = sbuf_of(it)
                        for jj, j in enumerate(js):
                            mm(SP[:, jj * 128:(jj + 1) * 128], kn[hb:hb + 64, j * 128:(j + 1) * 128],
                               qn[hb:hb + 64, m * 128:(m + 1) * 128], True, True,
                               [f'kn{j // 4}', f'qn{m // 4}'], [spk[jj // 4]])

                    def stage_a(it):
                        m, hh = iters[it]
                        off, js = lay[m]; n = len(js)
                        SP, spk = sbuf_of(it)
                        pslot = it % 4
                        S.emit('act', lambda: nc.scalar.activation(out=pT[:, pslot, 0:n * 128], in_=SP[:, 0:n * 128], func=AF.Exp),
                               spk, [f'pT{pslot}'])
                        if it % 2 == 0:
                            S.emit('pool', lambda: nc.gpsimd.tensor_tensor(out=pT[:, pslot, 0:n * 128], in0=pT[:, pslot, 0:n * 128],
                                                                           in1=Et[:, hh, off:off + n * 128], op=ALU.mult),
                                   [f'pT{pslot}', f'Et{hh}'], [f'pT{pslot}'])
                        else:
                            S.emit('dve', lambda: nc.vector.tensor_tensor(out=pT[:, pslot, 0:n * 128], in0=pT[:, pslot, 0:n * 128],
                                                                          in1=Et[:, hh, off:off + n * 128], op=ALU.mult),
                                   [f'pT{pslot}', f'Et{hh}'], [f'pT{pslot}'])

                    def stage_b(it):
                        m, hh = iters[it]
                        off, js = lay[m]; n = len(js); t_q = m // 4; ys = m % 2; hb = hh * 64
                        pslot = it % 4
                        ops, okey = [(PSC[:, 0:65], 'psC0'), (PSC[:, 512:577], 'psC1'), (PSD[:, 0:65], 'psD')][it % 3]
                        for jj, j in enumerate(js):
                            mm(ops, pT[:, pslot, jj * 128:(jj + 1) * 128], Vc[:, j, hh, :], jj == 0, jj == n - 1,
                               [f'pT{pslot}', f'Vc{j // 4}', 'Vones'], [okey])
                        rcol = it % 4
                        S.emit('dve', lambda: nc.vector.reciprocal(out=rec[:, rcol:rcol + 1], in_=ops[:, 64:65]),
                               [okey], [f'rec{rcol}'])
                        S.emit('dve', lambda: nc.vector.tensor_scalar(out=ybt[:, ys, hb:hb + 64], in0=ops[:, 0:64],
                                                                      scalar1=rec[:, rcol:rcol + 1], scalar2=None, op0=ALU.mult),
                               [okey, f'rec{rcol}'], [f'ybt{ys}h{hh}'])
                        if hh == 1:
                            pend.append(lambda m=m, t_q=t_q, ys=ys: tail(m, t_q, ys))

                    def tail(m, t_q, ys):
                        if True:
                            half = t_q % 2
                            S.emit('pe', lambda: nc.tensor.transpose(PST[:, half * 512 + (m % 4) * 128:half * 512 + (m % 4 + 1) * 128],
                                                                     ybt[:, ys, :], identb[:]),
                                   [f'ybt{ys}h0', f'ybt{ys}h1', 'identb'], ['psT'])
                            if m % 4 == 3:
                                S.emit('dve', lambda: nc.vector.tensor_tensor(out=ytmp[:, half, :], in0=PST[:, half * 512:(half + 1) * 512],
                                                                              in1=sgb[:, tl(t_q)], op=ALU.mult),
                                       ['psT', f'sgb{t_q}'], [f'ytmp{half}'])
                                S.emit('pool', lambda: nc.gpsimd.tensor_tensor(out=ytc[:, tl(t_q)], in0=ytmp[:, half, :],
                                                                               in1=yA[:, tl(t_q)], op=ALU.add),
                                       [f'ytmp{half}', 'yA'], [f'ytc{c % 2}.{t_q}'])

                    pend = []
                    scores(0); stage_a(0); scores(1); stage_a(1)
                    for it in range(len(iters)):
                        if it + 2 < len(iters):
                            scores(it + 2)
                            stage_a(it + 2)
                        todo = pend[:]; del pend[:]
                        stage_b(it)
                        for f_ in todo:
                            f_()
                    for f_ in pend:
                        f_()
                S.fence()
        wout_accum(7)
        S.fence()

    for s in range(nseq):
        row0 = s * SEQ
        wb = s * TILES_PER_SEQ
        with ExitStack() as st:
            xin = sb("xin", [128, 2, D], F32, st)
            for tt in range(16):
                sl = tt % 2
                S.dma(lambda: nc.sync.dma_start(out=xin[:, sl, :], in_=x[row0 + tt * 128:row0 + (tt + 1) * 128, :]),
                      f'xin{sl}', writes=[f'xin{sl}'])
                P_, pkeys = (PSA, ['psA0', 'psA1']) if tt % 2 == 0 else (PSB, ['psB0', 'psB1'])
                for c in range(8):
                    S.emit('pe', lambda: nc.tensor.transpose(P_[:, c * 128:(c + 1) * 128], xin[:, sl, c * 128:(c + 1) * 128], ident[:]),
                           [f'xin{sl}', 'ident'], [pkeys[c // 4]])
                t = tt // 4
                for hf in range(2):
                    dst = xT[:, hf * 4:hf * 4 + 4, tt * 128:(tt + 1) * 128]
                    src = P_[:, hf * 512:(hf + 1) * 512].rearrange("p (c j) -> p c j", j=128)
                    wk = [f'xT{c}.{t}' for c in range(hf * 4, hf * 4 + 4)]
                    if t % 2 == 0:
                        S.emit('act', lambda: nc.scalar.copy(out=dst, in_=src), [pkeys[hf]], wk)
                    else:
                        S.emit('dve', lambda: nc.vector.tensor_copy(out=dst, in_=src), [pkeys[hf]], wk)
        S.fence()
        if stop_after != 'load':
            ffn(wb + FFN1_BASE, G1)
            if stop_after != 'ffn1':
                with ExitStack() as mst:
                    mixer(wb + MIX_BASE, mst)
                if stop_after != 'mixer':
                    ffn(wb + FFN2_BASE, G2)
        with ExitStack() as st:
            xo = sb("xo", [128, 2, D], F32, st)
            for tt in range(16):
                sl = tt % 2; t = tt // 4
                P_, pkeys = (PSA, ['psA0', 'psA1']) if tt % 2 == 0 else (PSB, ['psB0', 'psB1'])
                for c in range(8):
                    S.emit('pe', lambda: nc.tensor.transpose(P_[:, c * 128:(c + 1) * 128], xT[:, c, tt * 128:(tt + 1) * 128], ident[:]),
                           [f'xT{c}.{t}', 'ident'], [pkeys[c // 4]])
                for hf in range(2):
                    if hf == 0:
                        S.emit('act', lambda: nc.scalar.copy(out=xo[:, sl, 0:512], in_=P_[:, 0:512]), [pkeys[0]], [f'xo{sl}'])
                    else:
                        S.emit('dve', lambda: nc.vector.tensor_copy(out=xo[:, sl, 512:1024], in_=P_[:, 512:1024]), [pkeys[1]], [f'xo{sl}'])
                S.dma(lambda: nc.sync.dma_start(out=out[row0 + tt * 128:row0 + (tt + 1) * 128, :], in_=xo[:, sl, :]),
                      f'xo{sl}', reads=[f'xo{sl}'])
            S.wait_dma_all('sp')
        S.fence()
    S.wait_dma_all('sp')
    top.close()
    return nc


def _host_layout(inputs):
    f = lambda k: np.ascontiguousarray(np.asarray(inputs[k], dtype=np.float32))
    pc = lambda v: np.ascontiguousarray(v.reshape(8, 128).T)
    cols = [pc(f('norm_ffn1')[0]), pc(f('norm_mix')[0]), pc(f('norm_ffn2')[0])]
    cw = f('conv_w')[0]
    cols += [pc(cw[k]) for k in range(4)]
    cols += [pc(f('conv_b')[0])]
    bgt = f('lru_b_gates')[0]
    cols += [pc(bgt[d, g]) for d in range(2) for g in range(2)]
    lam = f('lru_lambda')[0]
    cols += [pc(lam[d]) for d in range(2)]
    pvec = np.ascontiguousarray(np.concatenate(cols, axis=1))
    assert pvec.shape == (128, NP_COLS)
    wg = f('lru_w_gates')[0]
    gw = np.zeros((128, 32, 128), np.float32)
    for d in range(2):
        for g in range(2):
            for c in range(8):
                for bl in range(2):
                    gw[bl * 64:(bl + 1) * 64, (d * 2 + g) * 8 + c, bl * 64:(bl + 1) * 64] = wg[d, g, 2 * c + bl]
    qk = np.stack([np.tile(f('q_norm')[0], 2), np.tile(f('k_norm')[0], 2)], axis=1)
    dr, dc, valid = _build_index_tables()
    rpb = f('rel_pos_bias')[0]
    eb = rpb[:, dr, dc]
    eb[:, ~valid] = np.float32(NEG)
    return {
        "w_ffn1_gu": f('w_ffn1_gu')[0], "w_ffn2_gu": f('w_ffn2_gu')[0],
        "w_ffn1_down": f('w_ffn1_down')[0], "w_ffn2_down": f('w_ffn2_down')[0],
        "w_in": f('w_in')[0], "w_out": f('w_out')[0],
        "pvec": pvec, "gw": np.ascontiguousarray(gw.reshape(128, 32 * 128)),
        "qk": np.ascontiguousarray(qk.astype(np.float32)),
        "ident": np.eye(128, dtype=np.float32),
        "ebias": np.ascontiguousarray(eb.astype(np.float32)),
    }


_NC_CACHE = {}


def kernel(**inputs):
    shared = _host_layout(inputs)
    x = np.asarray(inputs['x'], dtype=np.float32)
    B = x.shape[0]
    per = B // NCORES
    if 'nc' not in _NC_CACHE:
        _NC_CACHE['nc'] = build_nc(per)
    nc = _NC_CACHE['nc']
    in_maps = []
    for i in range(NCORES):
        m = dict(shared)
        m["x"] = np.ascontiguousarray(x[i * per:(i + 1) * per].reshape(per * SEQ, D))
        in_maps.append(m)
    res = run_bass_kernel_spmd(nc, in_maps, core_ids=list(range(NCORES)))
    outs = [np.asarray(r["out"]).reshape(per, SEQ, D) for r in res.results]
    return np.concatenate(outs, axis=0).astype(np.float32)
```

```python
import bisect
from contextlib import ExitStack

import numpy as np
import concourse.bass as bass
import concourse.mybir as mybir
from concourse.bass_utils import run_bass_kernel_spmd

F32 = mybir.dt.float32
BF16 = mybir.dt.bfloat16
AF = mybir.ActivationFunctionType
ALU = mybir.AluOpType

NCORES = 8
NSEQ = 4
SEQ = 2048
D = 1024
DFF = 2816
NFC = 22
EPS = 1e-6
NSLOT = 14
TILES_PER_SEQ = 196
FFN1_BASE, MIX_BASE, FFN2_BASE = 0, 66, 130
NP_COLS = 112


def _tix(c, k):
    if c == 0:
        return k
    base = 7 + (c - 1) * 8
    return base + k if k < 3 else base + 1 + k


def _wix(c):
    return 10 + 8 * c if c < 7 else 63
ETAB_COLS = 2688
NEG = -30000.0

GRID_W = 64; ROWS = 32; WIN_ROWS = 8; WIN_COLS = 16
SPECIAL = [0, 1, 14, 15]


def _rs_of(r):
    return int(np.clip(r - WIN_ROWS // 2, 0, ROWS - WIN_ROWS))


def _chunks_for_tile(m):
    lo = _rs_of(2 * m); hi = _rs_of(2 * m + 1) + WIN_ROWS - 1
    return list(range(lo // 2, hi // 2 + 1))


def _table_layout():
    lay = {}
    off = 640
    for m in range(16):
        js = _chunks_for_tile(m)
        if m in SPECIAL:
            lay[m] = (off, js); off += 512
        else:
            lay[m] = (0, js)
    return lay


def _build_index_tables():
    lay = _table_layout()
    dr = np.zeros((128, ETAB_COLS), np.int64); dc = np.zeros((128, ETAB_COLS), np.int64)
    valid = np.zeros((128, ETAB_COLS), bool)
    done_interior = False
    for m in range(16):
        off, js = lay[m]
        if m not in SPECIAL and done_interior:
            continue
        for jj, j in enumerate(js):
            for kl in range(2):
                for ql in range(2):
                    kr = 2 * j + kl; qr = 2 * m + ql
                    vr = _rs_of(qr) <= kr < _rs_of(qr) + WIN_ROWS
                    kc = np.arange(64)[:, None]; qc = np.arange(64)[None, :]
                    csq = np.clip(qc - WIN_COLS // 2, 0, GRID_W - WIN_COLS)
                    v = (kc >= csq) & (kc < csq + WIN_COLS) & vr
                    d_r = kr - qr + WIN_ROWS - 1
                    d_c = np.clip(kc - qc, -(WIN_COLS - 1), WIN_COLS - 1) + WIN_COLS - 1
                    sl = (slice(kl * 64, kl * 64 + 64),
                          slice(off + jj * 128 + ql * 64, off + jj * 128 + ql * 64 + 64))
                    valid[sl] = v
                    dr[sl] = np.where(v, d_r, 0)
                    dc[sl] = np.where(v, d_c, 0)
        if m not in SPECIAL:
            done_interior = True
    return dr, dc, valid


class Sched:
    def __init__(self, nc):
        self.nc = nc
        self.eng = {'pe': nc.tensor, 'act': nc.scalar, 'dve': nc.vector, 'pool': nc.gpsimd, 'sp': nc.sync}
        self.sem = {e: nc.alloc_semaphore('prog_' + e) for e in ['pe', 'act', 'dve', 'pool']}
        self.cnt = {e: 0 for e in self.sem}
        self.ins = {e: [] for e in self.sem}
        self.sig_idx = {e: [] for e in self.sem}
        self.sig_cnt = {e: [] for e in self.sem}
        self.obs = {e: {} for e in self.eng}
        self.lw = {}
        self.rd = {}
        self.dma_total = {}
        self.dma_sems = {}

    def cover(self, e, idx):
        pos = bisect.bisect_left(self.sig_idx[e], idx)
        if pos < len(self.sig_idx[e]):
            return self.sig_cnt[e][pos]
        last = len(self.ins[e]) - 1
        self.ins[e][last].then_inc(self.sem[e], 1)
        self.cnt[e] += 1
        self.sig_idx[e].append(last); self.sig_cnt[e].append(self.cnt[e])
        return self.cnt[e]

    def _wait(self, engine, evts):
        need = {}
        for ev in evts:
            if ev[0] == 'c':
                if ev[1] == engine and engine == 'pe':
                    continue
                key = ev[1]; sem = self.sem[key]; val = self.cover(key, ev[2])
            else:
                key = ev[1]; sem = self.dma_sems[key]; val = self.dma_total[key]
            if need.get(key, (None, 0))[1] < val:
                need[key] = (sem, val)
        for key, (sem, val) in need.items():
            if self.obs[engine].get(key, 0) < val:
                self.eng[engine].wait_ge(sem, val)
                self.obs[engine][key] = val

    def _deps(self, reads, writes):
        evts = set()
        for k in reads:
            if k in self.lw:
                evts.add(self.lw[k])
        for k in writes:
            if k in self.lw:
                evts.add(self.lw[k])
            for ev in self.rd.get(k, {}).values():
                evts.add(ev)
        return evts

    def emit(self, engine, fn, reads=(), writes=(), signal=True):
        self._wait(engine, self._deps(reads, writes))
        ins = fn()
        idx = len(self.ins[engine]); self.ins[engine].append(ins)
        if signal:
            ins.then_inc(self.sem[engine], 1)
            self.cnt[engine] += 1
            self.sig_idx[engine].append(idx); self.sig_cnt[engine].append(self.cnt[engine])
        ev = ('c', engine, idx)
        for k in writes:
            self.lw[k] = ev; self.rd[k] = {}
        for k in reads:
            if k not in writes:
                self.rd.setdefault(k, {})[engine] = ev
        return ins

    def dma(self, fn, semkey, reads=(), writes=(), queue='sp'):
        if semkey not in self.dma_sems:
            self.dma_sems[semkey] = self.nc.alloc_semaphore('dma_' + semkey)
            self.dma_total[semkey] = 0
        self._wait(queue, self._deps(reads, writes))
        ins = fn()
        ins.then_inc(self.dma_sems[semkey], 16)
        self.dma_total[semkey] += 16
        ev = ('d', semkey)
        for k in writes:
            self.lw[k] = ev; self.rd[k] = {}
        for k in reads:
            self.rd.setdefault(k, {})[ev] = ev
        return ins

    def fence(self):
        evs = [('c', e, len(self.ins[e]) - 1) for e in self.sem if self.ins[e]]
        for e in ['pe', 'act', 'dve', 'pool', 'sp']:
            self._wait(e, evs)

    def wait_dma_all(self, engine='sp'):
        self._wait(engine, [('d', k) for k in self.dma_sems])


def build_nc(nseq=NSEQ, stop_after=None):
    nc = bass.Bass("TRN2", target_bir_lowering=False)
    S = Sched(nc)
    ntok = nseq * SEQ
    dram = lambda n, s, dt=F32, kind="ExternalInput": nc.dram_tensor(n, s, dt, kind=kind).ap()
    x = dram("x", [ntok, D])
    w_gu = [dram("w_ffn1_gu", [D, 2 * DFF]), dram("w_ffn2_gu", [D, 2 * DFF])]
    w_dn = [dram("w_ffn1_down", [DFF, D]), dram("w_ffn2_down", [DFF, D])]
    w_in = dram("w_in", [D, 7 * D])
    w_out = dram("w_out", [D, D])
    pvec_d = dram("pvec", [128, NP_COLS])
    gw_d = dram("gw", [128, 32 * 128])
    qk_d = dram("qk", [128, 2])
    ident_d = dram("ident", [128, 128])
    ebias_d = dram("ebias", [16, 128, ETAB_COLS])
    out = dram("out", [ntok, D], kind="ExternalOutput")
    wscr = dram("wscr", [TILES_PER_SEQ, 128, 1024], BF16, kind="Internal")
    escr = dram("escr", [16, 128, ETAB_COLS], BF16, kind="Internal")

    lay = _table_layout()
    top = ExitStack()
    uid = [0]

    def sb(n, s, dt, st=top):
        uid[0] += 1
        return st.enter_context(nc.sbuf_tensor(f"s{uid[0]}_{n}", s, dt))

    ps = lambda n, s, dt: top.enter_context(nc.psum_tensor(n, s, dt))

    ident = sb("ident", [128, 128], F32)
    identb = sb("identb", [128, 128], BF16)
    onesb = sb("onesb", [128, 128], BF16)
    bones = sb("bones", [128, 128], BF16)
    pvec = sb("pvec", [128, NP_COLS], F32)
    gwb = sb("gwb", [128, 32, 128], BF16)
    qk = sb("qksc", [128, 2], F32)
    cst = sb("cst", [128, 4], F32)
    cA = sb("cA", [128, 32], F32)
    PSA = ps("psA", [128, 1024], F32); PSB = ps("psB", [128, 1024], F32); PSC = ps("psC", [128, 1024], F32)
    PSD = ps("psD", [128, 512], F32); PST = ps("psT", [128, 1024], BF16)
    banks = [(PSA, 0, 'psA0'), (PSA, 512, 'psA1'), (PSB, 0, 'psB0'), (PSB, 512, 'psB1'),
             (PSC, 0, 'psC0'), (PSC, 512, 'psC1')]
    PSD_KEYS = ['psD']

    def bank_ap(i, n=512):
        t, o, k = banks[i]
        return t[:, o:o + n], k

    G1, GM, G2, CW, CB, BG, LAM = 0, 8, 16, 24, 56, 64, 96

    S.dma(lambda: nc.sync.dma_start(out=ident[:], in_=ident_d[:, :]), 'c0', writes=['ident'])
    S.dma(lambda: nc.sync.dma_start(out=pvec[:], in_=pvec_d[:, :]), 'c0', writes=['pvec'])
    S.dma(lambda: nc.sync.dma_start(out=qk[:], in_=qk_d[:, :]), 'c0', writes=['qk'])
    S.emit('dve', lambda: nc.vector.tensor_copy(out=identb[:], in_=ident[:]), ['ident'], ['identb'])
    S.emit('dve', lambda: nc.vector.memset(onesb[:], 1.0), [], ['onesb'])
    S.emit('dve', lambda: nc.vector.memset(bones[:], 0.0), [], ['bones'])
    S.emit('dve', lambda: nc.vector.memset(bones[0:64, 0:64], 1.0), [], ['bones'])
    S.emit('dve', lambda: nc.vector.memset(bones[64:128, 64:128], 1.0), [], ['bones'])
    S.emit('dve', lambda: nc.vector.memset(cst[:, 0:1], EPS), [], ['cst'])
    S.emit('dve', lambda: nc.vector.memset(cst[:, 1:2], 1.0), [], ['cst'])
    S.emit('dve', lambda: nc.vector.tensor_scalar(out=qk[:, 0:1], in0=qk[:, 0:1], scalar1=0.125, scalar2=None,
                                                  op0=ALU.mult), ['qk'], ['qk'])

    with ExitStack() as pst:
        psb = lambda n, s, dt: sb(n, s, dt, pst)
        lx = psb("lx", [128, 16], F32); lp = psb("lp", [128, 16], F32); lb = psb("lb", [128, 16], F32)
        lm = psb("lm", [128, 16], F32)
        lam = pvec[:, LAM:LAM + 16]
        S.emit('act', lambda: nc.scalar.activation(out=lx[:], in_=lam, func=AF.Exp, scale=-1.0), ['pvec'], ['lx'])
        S.emit('act', lambda: nc.scalar.activation(out=lb[:], in_=lx[:], func=AF.Ln, bias=cst[:, 1:2], scale=1.0),
               ['lx', 'cst'], ['lb'])
        S.emit('dve', lambda: nc.vector.tensor_scalar(out=lp[:], in0=lx[:], scalar1=-0.2, scalar2=0.25,
                                                      op0=ALU.mult, op1=ALU.add), ['lx'], ['lp'])
        for coef in (1.0 / 3.0, 0.5, 1.0):
            S.emit('dve', lambda: nc.vector.tensor_tensor(out=lp[:], in0=lp[:], in1=lx[:], op=ALU.mult), ['lp', 'lx'], ['lp'])
            S.emit('dve', lambda: nc.vector.tensor_scalar(out=lp[:], in0=lp[:], scalar1=-1.0, scalar2=coef,
                                                          op0=ALU.mult, op1=ALU.add), ['lp'], ['lp'])
        S.emit('dve', lambda: nc.vector.tensor_tensor(out=lp[:], in0=lp[:], in1=lx[:], op=ALU.mult), ['lp', 'lx'], ['lp'])
        S.emit('dve', lambda: nc.vector.tensor_single_scalar(out=lm[:], in_=lx[:], scalar=0.05, op=ALU.is_lt), ['lx'], ['lm'])
        S.emit('dve', lambda: nc.vector.tensor_tensor(out=lp[:], in0=lp[:], in1=lb[:], op=ALU.subtract), ['lp', 'lb'], ['lp'])
        S.emit('dve', lambda: nc.vector.tensor_tensor(out=lp[:], in0=lp[:], in1=lm[:], op=ALU.mult), ['lp', 'lm'], ['lp'])
        S.emit('dve', lambda: nc.vector.tensor_tensor(out=lp[:], in0=lp[:], in1=lb[:], op=ALU.add), ['lp', 'lb'], ['lp'])
        S.emit('dve', lambda: nc.vector.tensor_scalar(out=cA[:, 0:16], in0=lp[:], scalar1=-8.0, scalar2=None, op0=ALU.mult),
               ['lp'], ['cA'])
        S.emit('dve', lambda: nc.vector.tensor_scalar(out=cA[:, 16:32], in0=lp[:], scalar1=-16.0, scalar2=None, op0=ALU.mult),
               ['lp'], ['cA'])

        stf = psb("stf", [128, 2, 7168], F32)
        stb = psb("stb", [128, 56, 8, 128], BF16)
        stbf = stb[:].rearrange("p n c j -> p (n c j)")
        cast_engs = ['act', 'dve']
        cnt = [0]

        def cast(out_ap, in_ap, rd, wr):
            e = cast_engs[cnt[0] % 2]; cnt[0] += 1
            if e == 'act':
                S.emit('act', lambda: nc.scalar.copy(out=out_ap, in_=in_ap), rd, wr)
            elif e == 'dve':
                S.emit('dve', lambda: nc.vector.tensor_copy(out=out_ap, in_=in_ap), rd, wr)
            else:
                S.emit('pool', lambda: nc.gpsimd.tensor_copy(out=out_ap, in_=in_ap), rd, wr)

        ld = [0]

        def stage_load(src_ap, ncols, inner=None):
            sl = ld[0] % 2; ld[0] += 1
            dst = stf[:, sl, 0:ncols]
            if inner is not None:
                dst = dst.rearrange("p (f n) -> p f n", n=inner)
            S.dma(lambda: nc.sync.dma_start(out=dst, in_=src_ap), f'stf{sl}', writes=[f'stf{sl}'])
            return sl

        def colchunk_weight(W, ncols, store_fn):
            nt = ncols // 128
            for cc in range(8):
                sl = stage_load(W[cc * 128:(cc + 1) * 128, :], ncols)
                cast(stb[:, 0:nt, cc, :], stf[:, sl, 0:ncols].rearrange("p (n j) -> p n j", j=128),
                     [f'stf{sl}'], ['stb'])
            store_fn()

        def store_tiles(n0, cnt_, i0, step):
            src = stb[:, n0:n0 + cnt_, :, :].rearrange("p n c j -> p n (c j)")
            dst = wscr[i0:i0 + step * (cnt_ - 1) + 1:step].rearrange("n p x -> p n x")
            S.dma(lambda: nc.sync.dma_start(out=dst, in_=src), 'stst', reads=['stb'])

        for fi, base in ((0, FFN1_BASE), (1, FFN2_BASE)):
            def st_gu(base=base):
                for G in range(2):
                    for kind in range(2):
                        store_tiles(kind * NFC + G * 11, 11, base + G * 33 + kind, 2)
            colchunk_weight(w_gu[fi], 2 * DFF, st_gu)
            for G in range(2):
                for half, (f0, nf) in enumerate(((0, 6), (6, 5))):
                    fa = G * 11 + f0
                    sl = stage_load(w_dn[fi][fa * 128:(fa + nf) * 128, :].rearrange("(f p) n -> p f n", p=128), nf * 1024, 1024)
                    cast(stbf[:, 0:nf * 1024], stf[:, sl, 0:nf * 1024], [f'stf{sl}'], ['stb'])
                    i0 = base + G * 33 + 22 + f0
                    S.dma(lambda: nc.sync.dma_start(out=wscr[i0:i0 + nf].rearrange("n p x -> p n x"),
                                                    in_=stbf[:, 0:nf * 1024].rearrange("p (n x) -> p n x", x=1024)),
                          'stst', reads=['stb'])
        KORD = [0, 1, 4, 5, 6, 2, 3]

        def st_in():
            for fs in range(7):
                store_tiles(fs * 8, 1, MIX_BASE + _tix(0, KORD[fs]), 1)
                store_tiles(fs * 8 + 1, 7, MIX_BASE + _tix(1, KORD[fs]), 8)
        colchunk_weight(w_in, 7 * D, st_in)
        for half in range(2):
            sl = stage_load(w_out[half * 512:(half + 1) * 512, :].rearrange("(c p) n -> p c n", p=128), 4096, 1024)
            cast(stbf[:, 0:4096], stf[:, sl, 0:4096], [f'stf{sl}'], ['stb'])
            for ci in range(4):
                i0 = MIX_BASE + _wix(half * 4 + ci)
                S.dma(lambda: nc.sync.dma_start(out=wscr[i0], in_=stbf[:, ci * 1024:(ci + 1) * 1024]),
                      'stst', reads=['stb'])
        sl = stage_load(gw_d[:, :], 4096)
        S.emit('dve', lambda: nc.vector.tensor_copy(out=gwb[:].rearrange("p a b -> p (a b)"), in_=stf[:, sl, 0:4096]),
               [f'stf{sl}'], ['gwb'])
        for h in range(16):
            sl = stage_load(ebias_d[h], ETAB_COLS)
            S.emit('act', lambda: nc.scalar.activation(out=stbf[:, 0:ETAB_COLS], in_=stf[:, sl, 0:ETAB_COLS], func=AF.Exp),
                   [f'stf{sl}'], ['stb'])
            S.dma(lambda: nc.sync.dma_start(out=escr[h], in_=stbf[:, 0:ETAB_COLS]), 'stst', reads=['stb'])
        S.wait_dma_all('sp')
        S.fence()

    xT = sb("xT", [128, 8, SEQ], F32)
    hT = sb("hT", [128, 8, SEQ], BF16)
    ring = sb("ring", [128, NSLOT, 1024], BF16)
    total_tiles = TILES_PER_SEQ * nseq
    wst = {'next': 0}

    def load_upto(n):
        while wst['next'] < min(n, total_tiles):
            i = wst['next']; s = i % NSLOT
            S.dma(lambda: nc.sync.dma_start(out=ring[:, s, :], in_=wscr[i % TILES_PER_SEQ]), f'ring{s}',
                  writes=[f'ring{s}'])
            wst['next'] += 1

    def wget(i):
        load_upto(i + 1)
        return i % NSLOT

    def wdone(i):
        load_upto(i + 1 + NSLOT)

    load_upto(NSLOT)

    def tl(t):
        return slice(t * 512, (t + 1) * 512)

    def mm(out_ap, lhsT, rhs, start, stop, rd, wr):
        S.emit('pe', lambda: nc.tensor.matmul(out_ap, lhsT, rhs, start=start, stop=stop), rd, wr, signal=bool(stop))

    def rmsnorm(gcol, st):
        sq = sb("nsq", [128, 2, 512], BF16, st)
        sd = sb("nsd", [128, 2, 512], F32, st)
        for t in range(4):
            for c in range(8):
                b = c % 2
                S.emit('act', lambda: nc.scalar.activation(out=sq[:, b, :], in_=xT[:, c, tl(t)], func=AF.Square),
                       [f'xT{c}.{t}'], [f'nsq{b}'])
                mm(PSD[:, :], onesb[:], sq[:, b, :], c == 0, c == 7, ['onesb', f'nsq{b}'], PSD_KEYS)
            b = t % 2
            S.emit('act', lambda: nc.scalar.activation(out=sd[:, b, :], in_=PSD[:, :], func=AF.Ln,
                                                       bias=cst[:, 0:1], scale=1.0 / D), PSD_KEYS + ['cst'], [f'nsd{b}'])
            S.emit('act', lambda: nc.scalar.activation(out=sd[:, b, :], in_=sd[:, b, :], func=AF.Exp, scale=-0.5),
                   [f'nsd{b}'], [f'nsd{b}'])
            for c in range(8):
                S.emit('dve', lambda: nc.vector.scalar_tensor_tensor(out=hT[:, c, tl(t)], in0=xT[:, c, tl(t)],
                                                                     scalar=pvec[:, gcol + c:gcol + c + 1],
                                                                     in1=sd[:, b, :], op0=ALU.mult, op1=ALU.mult),
                       [f'xT{c}.{t}', 'pvec', f'nsd{b}'], [f'hT{c}.{t}'])

    def ffn(tbase, gcol):
        with ExitStack() as st:
            rmsnorm(gcol, st)
            act = sb("actT", [128, 11, SEQ], BF16, st)
            sg = sb("sg", [128, 2, 512], F32, st)
            k = 0
            for G in range(2):
                gb_ = tbase + G * 33
                for fl in range(11):
                    sg_ = wget(gb_ + 2 * fl); su_ = wget(gb_ + 2 * fl + 1)
                    for t in range(4):
                        pg, kg = bank_ap(k % 2); pu, ku = bank_ap(2 + k % 2); b = k % 2; k += 1
                        for cc in range(8):
                            mm(pg, ring[:, sg_, cc * 128:(cc + 1) * 128], hT[:, cc, tl(t)], cc == 0, cc == 7,
                               [f'ring{sg_}', f'hT{cc}.{t}'], [kg])
                        for cc in range(8):
                            mm(pu, ring[:, su_, cc * 128:(cc + 1) * 128], hT[:, cc, tl(t)], cc == 0, cc == 7,
                               [f'ring{su_}', f'hT{cc}.{t}'], [ku])
                        S.emit('act', lambda: nc.scalar.activation(out=sg[:, b, :], in_=pg, func=AF.Silu), [kg], [f'sg{b}'])
                        S.emit('dve', lambda: nc.vector.tensor_tensor(out=act[:, fl, tl(t)], in0=sg[:, b, :], in1=pu, op=ALU.mult),
                               [f'sg{b}', ku], [f'act{fl}.{t}'])
                    wdone(gb_ + 2 * fl); wdone(gb_ + 2 * fl + 1)
                dslots = [wget(gb_ + 22 + fl) for fl in range(11)]
                kd = 0
                for o in range(8):
                    for t in range(4):
                        pd, kdk = bank_ap(4 + kd % 2); kd += 1
                        for fl in range(11):
                            mm(pd, ring[:, dslots[fl], o * 128:(o + 1) * 128], act[:, fl, tl(t)], fl == 0, fl == 10,
                               [f'ring{dslots[fl]}', f'act{fl}.{t}'], [kdk])
                        S.emit('dve', lambda: nc.vector.scalar_tensor_tensor(out=xT[:, o, tl(t)], in0=pd, scalar=0.5,
                                                                             in1=xT[:, o, tl(t)], op0=ALU.mult, op1=ALU.add),
                               [kdk, f'xT{o}.{t}'], [f'xT{o}.{t}'])
                for fl in range(11):
                    wdone(gb_ + 22 + fl)
        S.fence()

    pk = [0]

    def proj(slot, t, bank_ids=(4, 5)):
        pb, kb = bank_ap(bank_ids[pk[0] % len(bank_ids)]); pk[0] += 1
        for cc in range(8):
            mm(pb, ring[:, slot, cc * 128:(cc + 1) * 128], hT[:, cc, tl(t)], cc == 0, cc == 7,
               [f'ring{slot}', f'hT{cc}.{t}'], [kb])
        return pb, kb

    def mixer(tbase, mst):
        with ExitStack() as st:
            rmsnorm(GM, st)
        S.fence()
        ytcs = [sb("ytcA", [128, SEQ], BF16, mst), sb("ytcB", [128, SEQ], BF16, mst)]

        def wout_steps(cprev):
            ytp = ytcs[cprev % 2]
            s_wo = wget(tbase + _wix(cprev))
            ko = 0
            for o in range(8):
                for t in range(4):
                    pb, kb = bank_ap(4 + ko % 2); ko += 1
                    mm(pb, ring[:, s_wo, o * 128:(o + 1) * 128], ytp[:, tl(t)], True, True,
                       [f'ring{s_wo}', f'ytc{cprev % 2}.{t}'], [kb])
                    S.emit('dve', lambda: nc.vector.tensor_tensor(out=xT[:, o, tl(t)], in0=pb, in1=xT[:, o, tl(t)], op=ALU.add),
                           [kb, f'xT{o}.{t}'], [f'xT{o}.{t}'])
                    yield
            wdone(tbase + _wix(cprev))

        def wout_accum(cprev):
            for _ in wout_steps(cprev):
                pass

        for c in range(8):
            tbk = lambda k_: tbase + _tix(c, k_)
            ytc = ytcs[c % 2]
            with ExitStack() as cst_:
                yA = sb("yA", [128, SEQ], BF16, cst_)
                with ExitStack() as st:
                    xr = sb("xr", [128, SEQ + 4], F32, st)
                    xcb = sb("xcb", [128, SEQ], BF16, st)
                    Rb_ = [sb("Rf", [128, SEQ], F32, st), sb("Rb", [128, SEQ], F32, st)]
                    Ib_ = [sb("If", [128, SEQ], F32, st), sb("Ib", [128, SEQ], F32, st)]
                    M0 = sb("M0", [128, SEQ], F32, st)
                    g2 = sb("g2", [128, SEQ], BF16, st)
                    tmp = sb("tmpA", [128, 2, 512], F32, st)
                    Mb_ = [M0[:, :], xr[:, 2:2 + SEQ]]
                    Mk = ['M0', 'xr']
                    s_xr = wget(tbk(0))
                    S.emit('pool', lambda: nc.gpsimd.memset(xr[:, 0:2], 0.0), [], ['xr'])
                    S.emit('pool', lambda: nc.gpsimd.memset(xr[:, 2 + SEQ:4 + SEQ], 0.0), [], ['xr'])
                    for t in range(4):
                        pb, kb = proj(s_xr, t, (0, 1, 2, 3))
                        S.emit('act', lambda: nc.scalar.copy(out=xr[:, 2 + t * 512:2 + (t + 1) * 512], in_=pb), [kb], ['xr'])
                    wdone(tbk(0))
                    cw = lambda k_: pvec[:, CW + k_ * 8 + c:CW + k_ * 8 + c + 1]
                    S.emit('pool', lambda: nc.gpsimd.tensor_scalar(out=M0[:, :], in0=xr[:, 0:SEQ], scalar1=cw(0),
                                                                   scalar2=pvec[:, CB + c:CB + c + 1], op0=ALU.mult, op1=ALU.add),
                           ['xr', 'pvec'], ['M0'])
                    for k_ in range(1, 4):
                        S.emit('dve', lambda: nc.vector.scalar_tensor_tensor(out=M0[:, :], in0=xr[:, k_:k_ + SEQ], scalar=cw(k_),
                                                                              in1=M0[:, :], op0=ALU.mult, op1=ALU.add),
                               ['xr', 'pvec', 'M0'], ['M0'])
                    S.emit('act', lambda: nc.scalar.copy(out=xcb[:, :], in_=M0[:, :]), ['M0'], ['xcb'])
                    s_gr = wget(tbk(1))
                    for t in range(4):
                        pb, kb = proj(s_gr, t, (0, 1, 2, 3)); b = t % 2
                        S.emit('act', lambda: nc.scalar.activation(out=tmp[:, b, :], in_=pb, func=AF.Square, scale=0.21145921592969426),
                               [kb], [f'tmpA{b}'])
                        S.emit('dve', lambda: nc.vector.scalar_tensor_tensor(out=tmp[:, b, :], in0=tmp[:, b, :], scalar=1.0, in1=pb,
                                                                             op0=ALU.add, op1=ALU.mult),
                               [f'tmpA{b}', kb], [f'tmpA{b}'])
                        S.emit('act', lambda: nc.scalar.activation(out=tmp[:, b, :], in_=tmp[:, b, :], func=AF.Sigmoid,
                                                                   scale=1.5957691216057308), [f'tmpA{b}'], [f'tmpA{b}'])
                        S.emit('dve', lambda: nc.vector.tensor_tensor(out=g2[:, tl(t)], in0=tmp[:, b, :], in1=pb, op=ALU.mult),
                               [f'tmpA{b}', kb], [f'g2.{t}'])
                    wdone(tbk(1))
                    s_ga = wget(tbk(2))
                    for t in range(4):
                        pb, kb = proj(s_ga, t, (0, 1, 2, 3)); b = t % 2
                        S.emit('act', lambda: nc.scalar.activation(out=tmp[:, b, :], in_=pb, func=AF.Sigmoid), [kb], [f'tmpA{b}'])
                        S.emit('dve', lambda: nc.vector.tensor_tensor(out=g2[:, tl(t)], in0=g2[:, tl(t)], in1=tmp[:, b, :], op=ALU.mult),
                               [f'tmpA{b}', f'g2.{t}'], [f'g2.{t}'])
                    wdone(tbk(2))
                    wgen = wout_steps(c - 1) if c > 0 else iter(())
                    for d in range(2):
                        for g_, (buf, bk) in enumerate(((Rb_[d], f'R{d}'), (Ib_[d], f'I{d}'))):
                            gi = (d * 2 + g_) * 8 + c
                            for t in range(4):
                                pb, kb = bank_ap((d * 8 + g_ * 4 + t) % 4)
                                mm(pb, gwb[:, gi, :], xcb[:, tl(t)], True, True, ['gwb', 'xcb'], [kb])
                                S.emit('act', lambda: nc.scalar.activation(out=buf[:, tl(t)], in_=pb, func=AF.Sigmoid,
                                                                           bias=pvec[:, BG + gi:BG + gi + 1], scale=1.0),
                                       [kb, 'pvec'], [bk])
                                next(wgen, None)
                    for _ in wgen:
                        pass
                    for d in range(2):
                        S.emit('act', lambda: nc.scalar.activation(out=Mb_[d], in_=Rb_[d][:, :], func=AF.Exp,
                                                                   scale=cA[:, 16 + d * 8 + c:16 + d * 8 + c + 1]),
                               [f'R{d}', 'cA'], [Mk[d]])
                        S.emit('act', lambda: nc.scalar.activation(out=Rb_[d][:, :], in_=Rb_[d][:, :], func=AF.Exp,
                                                                   scale=cA[:, d * 8 + c:d * 8 + c + 1]),
                               [f'R{d}', 'cA'], [f'R{d}'])
                    for d in range(2):
                        S.emit('act', lambda: nc.scalar.activation(out=Mb_[d], in_=Mb_[d], func=AF.Sqrt,
                                                                   bias=cst[:, 1:2], scale=-1.0), [Mk[d], 'cst'], [Mk[d]])
                    for d, e in ((0, 'dve'), (1, 'dve')):
                        eo = nc.vector if e == 'dve' else nc.gpsimd
                        S.emit(e, lambda: eo.tensor_tensor(out=Ib_[d][:, :], in0=Ib_[d][:, :], in1=xcb[:, :], op=ALU.mult),
                               [f'I{d}', 'xcb'], [f'I{d}'])
                        S.emit(e, lambda: eo.tensor_tensor(out=Ib_[d][:, :], in0=Ib_[d][:, :], in1=Mb_[d], op=ALU.mult),
                               [f'I{d}', Mk[d]], [f'I{d}'])
                    S.emit('dve', lambda: nc.vector.tensor_tensor_scan(out=Mb_[0], data0=Rb_[0][:, :], data1=Ib_[0][:, :],
                                                                       initial=0.0, op0=ALU.mult, op1=ALU.add),
                           ['R0', 'I0'], ['M0'])
                    S.emit('dve', lambda: nc.vector.tensor_tensor_scan(out=Mb_[1][:, ::-1], data0=Rb_[1][:, ::-1],
                                                                       data1=Ib_[1][:, ::-1], initial=0.0,
                                                                       op0=ALU.mult, op1=ALU.add),
                           ['R1', 'I1'], ['xr'])
                    S.emit('dve', lambda: nc.vector.tensor_tensor(out=Mb_[0], in0=Mb_[0], in1=Mb_[1], op=ALU.add),
                           ['M0', 'xr'], ['M0'])
                    S.emit('dve', lambda: nc.vector.tensor_tensor(out=yA[:, :], in0=Mb_[0], in1=g2[:, :], op=ALU.mult),
                           ['M0'] + [f'g2.{t}' for t in range(4)], ['yA'])
                S.fence()
                with ExitStack() as st:
                    qn = sb("qn", [128, SEQ], BF16, st); kn = sb("kn", [128, SEQ], BF16, st)
                    Vc = sb("Vc", [128, 16, 2, 65], BF16, st)
                    sgb = sb("sgb", [128, SEQ], BF16, st)
                    Et = sb("Et", [128, 2, ETAB_COLS], BF16, st)
                    pT = sb("pT", [128, 4, 640], BF16, st)
                    sq = sb("bsq", [128, 2, 512], BF16, st)
                    sd = sb("bsd", [128, 2, 512], F32, st)
                    ybt = sb("ybt", [128, 2, 128], BF16, st)
                    rec = sb("rec", [128, 4], F32, st)
                    ytmp = sb("ytmp", [128, 2, 512], BF16, st)
                    for hh in range(2):
                        S.dma(lambda: nc.sync.dma_start(out=Et[:, hh, :], in_=escr[2 * c + hh]), f'et{hh}', writes=[f'Et{hh}'])
                    s_gb = wget(tbk(3))
                    for t in range(4):
                        pb, kb = proj(s_gb, t, (0, 1, 2, 3, 4, 5))
                        S.emit('act', lambda: nc.scalar.activation(out=sgb[:, tl(t)], in_=pb, func=AF.Sigmoid), [kb], [f'sgb{t}'])
                    wdone(tbk(3))
                    for wi, (dst, dk, col) in enumerate(((qn, 'qn', 0), (kn, 'kn', 1))):
                        s_ = wget(tbk(4 + wi))
                        for t in range(4):
                            pb, kb = proj(s_, t, (0, 1, 2, 3, 4, 5)); b = t % 2
                            S.emit('act', lambda: nc.scalar.activation(out=sq[:, b, :], in_=pb, func=AF.Square), [kb], [f'bsq{b}'])
                            mm(PSD[:, :], bones[:], sq[:, b, :], True, True, ['bones', f'bsq{b}'], PSD_KEYS)
                            S.emit('act', lambda: nc.scalar.activation(out=sd[:, b, :], in_=PSD[:, :], func=AF.Ln,
                                                                       bias=cst[:, 0:1], scale=1.0 / 64), PSD_KEYS + ['cst'], [f'bsd{b}'])
                            S.emit('act', lambda: nc.scalar.activation(out=sd[:, b, :], in_=sd[:, b, :], func=AF.Exp, scale=-0.5),
                                   [f'bsd{b}'], [f'bsd{b}'])
                            S.emit('dve', lambda: nc.vector.scalar_tensor_tensor(out=dst[:, tl(t)], in0=pb, scalar=qk[:, col:col + 1],
                                                                                 in1=sd[:, b, :], op0=ALU.mult, op1=ALU.mult),
                                   [kb, 'qk', f'bsd{b}'], [f'{dk}{t}'])
                        wdone(tbk(4 + wi))
                    s_v = wget(tbk(6))
                    S.emit('pool', lambda: nc.gpsimd.memset(Vc[:, :, :, 64:65], 1.0), [], ['Vones'])
                    for tq in range(4):
                        pb, kb = bank_ap(4 + tq % 2)
                        for tt in range(4):
                            tok = slice((tq * 4 + tt) * 128, (tq * 4 + tt + 1) * 128)
                            for cc in range(8):
                                mm(pb[:, tt * 128:(tt + 1) * 128], hT[:, cc, tok], ring[:, s_v, cc * 128:(cc + 1) * 128],
                                   cc == 0, cc == 7, [f'ring{s_v}', f'hT{cc}.{tq}'], [kb])
                        S.emit('act', lambda: nc.scalar.copy(out=Vc[:, tq * 4:tq * 4 + 4, :, 0:64],
                                                             in_=pb.rearrange("p (a h d) -> p a h d", a=4, h=2)),
                               [kb], [f'Vc{tq}'])
                    wdone(tbk(6))
                    iters = [(m, hh) for m in range(16) for hh in range(2)]

                    def sbuf_of(it):
                        return [(PSA, ['psA0', 'psA1']), (PSB, ['psB0', 'psB1'])][it % 2]

                    def scores(it):
                        m, hh = iters[it]
                        off, js = lay[m]; hb = hh * 64
                        SP, spk = sbuf_of(it)
                        for jj, j in enumerate(js):
                            mm(SP[:, jj * 128:(jj + 1) * 128], kn[hb:hb + 64, j * 128:(j + 1) * 128],
                               qn[hb:hb + 64, m * 128:(m + 1) * 128], True, True,
                               [f'kn{j // 4}', f'qn{m // 4}'], [spk[jj // 4]])

                    def stage_a(it):
                        m, hh = iters[it]
                        off, js = lay[m]; n = len(js)
                        SP, spk = sbuf_of(it)
                        pslot = it % 4
                        S.emit('act', lambda: nc.scalar.activation(out=pT[:, pslot, 0:n * 128], in_=SP[:, 0:n * 128], func=AF.Exp),
                               spk, [f'pT{pslot}'])
                        if it % 2 == 0:
                            S.emit('pool', lambda: nc.gpsimd.tensor_tensor(out=pT[:, pslot, 0:n * 128], in0=pT[:, pslot, 0:n * 128],
                                                                           in1=Et[:, hh, off:off + n * 128], op=ALU.mult),
                                   [f'pT{pslot}', f'Et{hh}'], [f'pT{pslot}'])
                        else:
                            S.emit('dve', lambda: nc.vector.tensor_tensor(out=pT[:, pslot, 0:n * 128], in0=pT[:, pslot, 0:n * 128],
                                                                          in1=Et[:, hh, off:off + n * 128], op=ALU.mult),
                                   [f'pT{pslot}', f'Et{hh}'], [f'pT{pslot}'])

                    def stage_b(it):
                        m, hh = iters[it]
                        off, js = lay[m]; n = len(js); t_q = m // 4; ys = m % 2; hb = hh * 64
                        pslot = it % 4
                        ops, okey = [(PSC[:, 0:65], 'psC0'), (PSC[:, 512:577], 'psC1'), (PSD[:, 0:65], 'psD')][it % 3]
                        for jj, j in enumerate(js):
                            mm(ops, pT[:, pslot, jj * 128:(jj + 1) * 128], Vc[:, j, hh, :], jj == 0, jj == n - 1,
                               [f'pT{pslot}', f'Vc{j // 4}', 'Vones'], [okey])
                        rcol = it % 4
                        S.emit('dve', lambda: nc.vector.reciprocal(out=rec[:, rcol:rcol + 1], in_=ops[:, 64:65]),
                               [okey], [f'rec{rcol}'])
                        S.emit('dve', lambda: nc.vector.tensor_scalar(out=ybt[:, ys, hb:hb + 64], in0=ops[:, 0:64],
                                                                      scalar1=rec[:, rcol:rcol + 1], scalar2=None, op0=ALU.mult),
                               [okey, f'rec{rcol}'], [f'ybt{ys}h{hh}'])
                        if hh == 1:
                            pend.append(lambda m=m, t_q=t_q, ys=ys: tail(m, t_q, ys))

                    def tail(m, t_q, ys):
                        if True:
                            half = t_q % 2
                            S.emit('pe', lambda: nc.tensor.transpose(PST[:, half * 512 + (m % 4) * 128:half * 512 + (m % 4 + 1) * 128],
                                                                     ybt[:, ys, :], identb[:]),
                                   [f'ybt{ys}h0', f'ybt{ys}h1', 'identb'], ['psT'])
                            if m % 4 == 3:
                                S.emit('dve', lambda: nc.vector.tensor_tensor(out=ytmp[:, half, :], in0=PST[:, half * 512:(half + 1) * 512],
                                                                              in1=sgb[:, tl(t_q)], op=ALU.mult),
                                       ['psT', f'sgb{t_q}'], [f'ytmp{half}'])
                                S.emit('pool', lambda: nc.gpsimd.tensor_tensor(out=ytc[:, tl(t_q)], in0=ytmp[:, half, :],
                                                                               in1=yA[:, tl(t_q)], op=ALU.add),
                                       [f'ytmp{half}', 'yA'], [f'ytc{c % 2}.{t_q}'])

                    pend = []
                    scores(0); stage_a(0); scores(1); stage_a(1)
                    for it in range(len(iters)):
                        if it + 2 < len(iters):
                            scores(it + 2)
                            stage_a(it + 2)
                        todo = pend[:]; del pend[:]
                        stage_b(it)
                        for f_ in todo:
                            f_()
                    for f_ in pend:
                        f_()
                S.fence()
        wout_accum(7)
        S.fence()

    for s in range(nseq):
        row0 = s * SEQ
        wb = s * TILES_PER_SEQ
        with ExitStack() as st:
            xin = sb("xin", [128, 2, D], F32, st)
            for tt in range(16):
                sl = tt % 2
                S.dma(lambda: nc.sync.dma_start(out=xin[:, sl, :], in_=x[row0 + tt * 128:row0 + (tt + 1) * 128, :]),
                      f'xin{sl}', writes=[f'xin{sl}'])
                P_, pkeys = (PSA, ['psA0', 'psA1']) if tt % 2 == 0 else (PSB, ['psB0', 'psB1'])
                for c in range(8):
                    S.emit('pe', lambda: nc.tensor.transpose(P_[:, c * 128:(c + 1) * 128], xin[:, sl, c * 128:(c + 1) * 128], ident[:]),
                           [f'xin{sl}', 'ident'], [pkeys[c // 4]])
                t = tt // 4
                for hf in range(2):
                    dst = xT[:, hf * 4:hf * 4 + 4, tt * 128:(tt + 1) * 128]
                    src = P_[:, hf * 512:(hf + 1) * 512].rearrange("p (c j) -> p c j", j=128)
                    wk = [f'xT{c}.{t}' for c in range(hf * 4, hf * 4 + 4)]
                    if t % 2 == 0:
                        S.emit('act', lambda: nc.scalar.copy(out=dst, in_=src), [pkeys[hf]], wk)
                    else:
                        S.emit('dve', lambda: nc.vector.tensor_copy(out=dst, in_=src), [pkeys[hf]], wk)
        S.fence()
        if stop_after != 'load':
            ffn(wb + FFN1_BASE, G1)
            if stop_after != 'ffn1':
                with ExitStack() as mst:
                    mixer(wb + MIX_BASE, mst)
                if stop_after != 'mixer':
                    ffn(wb + FFN2_BASE, G2)
        with ExitStack() as st:
            xo = sb("xo", [128, 2, D], F32, st)
            for tt in range(16):
                sl = tt % 2; t = tt // 4
                P_, pkeys = (PSA, ['psA0', 'psA1']) if tt % 2 == 0 else (PSB, ['psB0', 'psB1'])
                for c in range(8):
                    S.emit('pe', lambda: nc.tensor.transpose(P_[:, c * 128:(c + 1) * 128], xT[:, c, tt * 128:(tt + 1) * 128], ident[:]),
                           [f'xT{c}.{t}', 'ident'], [pkeys[c // 4]])
                for hf in range(2):
                    if hf == 0:
                        S.emit('act', lambda: nc.scalar.copy(out=xo[:, sl, 0:512], in_=P_[:, 0:512]), [pkeys[0]], [f'xo{sl}'])
                    else:
                        S.emit('dve', lambda: nc.vector.tensor_copy(out=xo[:, sl, 512:1024], in_=P_[:, 512:1024]), [pkeys[1]], [f'xo{sl}'])
                S.dma(lambda: nc.sync.dma_start(out=out[row0 + tt * 128:row0 + (tt + 1) * 128, :], in_=xo[:, sl, :]),
                      f'xo{sl}', reads=[f'xo{sl}'])
            S.wait_dma_all('sp')
        S.fence()
    S.wait_dma_all('sp')
    top.close()
    return nc


def _host_layout(inputs):
    f = lambda k: np.ascontiguousarray(np.asarray(inputs[k], dtype=np.float32))
    pc = lambda v: np.ascontiguousarray(v.reshape(8, 128).T)
    cols = [pc(f('norm_ffn1')[0]), pc(f('norm_mix')[0]), pc(f('norm_ffn2')[0])]
    cw = f('conv_w')[0]
    cols += [pc(cw[k]) for k in range(4)]
    cols += [pc(f('conv_b')[0])]
    bgt = f('lru_b_gates')[0]
    cols += [pc(bgt[d, g]) for d in range(2) for g in range(2)]
    lam = f('lru_lambda')[0]
    cols += [pc(lam[d]) for d in range(2)]
    pvec = np.ascontiguousarray(np.concatenate(cols, axis=1))
    assert pvec.shape == (128, NP_COLS)
    wg = f('lru_w_gates')[0]
    gw = np.zeros((128, 32, 128), np.float32)
    for d in range(2):
        for g in range(2):
            for c in range(8):
                for bl in range(2):
                    gw[bl * 64:(bl + 1) * 64, (d * 2 + g) * 8 + c, bl * 64:(bl + 1) * 64] = wg[d, g, 2 * c + bl]
    qk = np.stack([np.tile(f('q_norm')[0], 2), np.tile(f('k_norm')[0], 2)], axis=1)
    dr, dc, valid = _build_index_tables()
    rpb = f('rel_pos_bias')[0]
    eb = rpb[:, dr, dc]
    eb[:, ~valid] = np.float32(NEG)
    return {
        "w_ffn1_gu": f('w_ffn1_gu')[0], "w_ffn2_gu": f('w_ffn2_gu')[0],
        "w_ffn1_down": f('w_ffn1_down')[0], "w_ffn2_down": f('w_ffn2_down')[0],
        "w_in": f('w_in')[0], "w_out": f('w_out')[0],
        "pvec": pvec, "gw": np.ascontiguousarray(gw.reshape(128, 32 * 128)),
        "qk": np.ascontiguousarray(qk.astype(np.float32)),
        "ident": np.eye(128, dtype=np.float32),
        "ebias": np.ascontiguousarray(eb.astype(np.float32)),
    }


_NC_CACHE = {}


def kernel(**inputs):
    shared = _host_layout(inputs)
    x = np.asarray(inputs['x'], dtype=np.float32)
    B = x.shape[0]
    per = B // NCORES
    if 'nc' not in _NC_CACHE:
        _NC_CACHE['nc'] = build_nc(per)
    nc = _NC_CACHE['nc']
    in_maps = []
    for i in range(NCORES):
        m = dict(shared)
        m["x"] = np.ascontiguousarray(x[i * per:(i + 1) * per].reshape(per * SEQ, D))
        in_maps.append(m)
    res = run_bass_kernel_spmd(nc, in_maps, core_ids=list(range(NCORES)))
    outs = [np.asarray(r["out"]).reshape(per, SEQ, D) for r in res.results]
    return np.concatenate(outs, axis=0).astype(np.float32)
```

```python
import bisect
from contextlib import ExitStack

import numpy as np
import concourse.bass as bass
import concourse.mybir as mybir
from concourse.bass_utils import run_bass_kernel_spmd

F32 = mybir.dt.float32
BF16 = mybir.dt.bfloat16
AF = mybir.ActivationFunctionType
ALU = mybir.AluOpType

NCORES = 8
NSEQ = 4
SEQ = 2048
D = 1024
DFF = 2816
NFC = 22
EPS = 1e-6
NSLOT = 14
TILES_PER_SEQ = 196
FFN1_BASE, MIX_BASE, FFN2_BASE = 0, 66, 130
NP_COLS = 112


def _tix(c, k):
    if c == 0:
        return k
    base = 7 + (c - 1) * 8
    return base + k if k < 3 else base + 1 + k


def _wix(c):
    return 10 + 8 * c if c < 7 else 63
ETAB_COLS = 2688
NEG = -30000.0

GRID_W = 64; ROWS = 32; WIN_ROWS = 8; WIN_COLS = 16
SPECIAL = [0, 1, 14, 15]


def _rs_of(r):
    return int(np.clip(r - WIN_ROWS // 2, 0, ROWS - WIN_ROWS))


def _chunks_for_tile(m):
    lo = _rs_of(2 * m); hi = _rs_of(2 * m + 1) + WIN_ROWS - 1
    return list(range(lo // 2, hi // 2 + 1))


def _table_layout():
    lay = {}
    off = 640
    for m in range(16):
        js = _chunks_for_tile(m)
        if m in SPECIAL:
            lay[m] = (off, js); off += 512
        else:
            lay[m] = (0, js)
    return lay


def _build_index_tables():
    lay = _table_layout()
    dr = np.zeros((128, ETAB_COLS), np.int64); dc = np.zeros((128, ETAB_COLS), np.int64)
    valid = np.zeros((128, ETAB_COLS), bool)
    done_interior = False
    for m in range(16):
        off, js = lay[m]
        if m not in SPECIAL and done_interior:
            continue
        for jj, j in enumerate(js):
            for kl in range(2):
                for ql in range(2):
                    kr = 2 * j + kl; qr = 2 * m + ql
                    vr = _rs_of(qr) <= kr < _rs_of(qr) + WIN_ROWS
                    kc = np.arange(64)[:, None]; qc = np.arange(64)[None, :]
                    csq = np.clip(qc - WIN_COLS // 2, 0, GRID_W - WIN_COLS)
                    v = (kc >= csq) & (kc < csq + WIN_COLS) & vr
                    d_r = kr - qr + WIN_ROWS - 1
                    d_c = np.clip(kc - qc, -(WIN_COLS - 1), WIN_COLS - 1) + WIN_COLS - 1
                    sl = (slice(kl * 64, kl * 64 + 64),
                          slice(off + jj * 128 + ql * 64, off + jj * 128 + ql * 64 + 64))
                    valid[sl] = v
                    dr[sl] = np.where(v, d_r, 0)
                    dc[sl] = np.where(v, d_c, 0)
        if m not in SPECIAL:
            done_interior = True
    return dr, dc, valid


class Sched:
    def __init__(self, nc):
        self.nc = nc
        self.eng = {'pe': nc.tensor, 'act': nc.scalar, 'dve': nc.vector, 'pool': nc.gpsimd, 'sp': nc.sync}
        self.sem = {e: nc.alloc_semaphore('prog_' + e) for e in ['pe', 'act', 'dve', 'pool']}
        self.cnt = {e: 0 for e in self.sem}
        self.ins = {e: [] for e in self.sem}
        self.sig_idx = {e: [] for e in self.sem}
        self.sig_cnt = {e: [] for e in self.sem}
        self.obs = {e: {} for e in self.eng}
        self.lw = {}
        self.rd = {}
        self.dma_total = {}
        self.dma_sems = {}

    def cover(self, e, idx):
        pos = bisect.bisect_left(self.sig_idx[e], idx)
        if pos < len(self.sig_idx[e]):
            return self.sig_cnt[e][pos]
        last = len(self.ins[e]) - 1
        self.ins[e][last].then_inc(self.sem[e], 1)
        self.cnt[e] += 1
        self.sig_idx[e].append(last); self.sig_cnt[e].append(self.cnt[e])
        return self.cnt[e]

    def _wait(self, engine, evts):
        need = {}
        for ev in evts:
            if ev[0] == 'c':
                if ev[1] == engine and engine == 'pe':
                    continue
                key = ev[1]; sem = self.sem[key]; val = self.cover(key, ev[2])
            else:
                key = ev[1]; sem = self.dma_sems[key]; val = self.dma_total[key]
            if need.get(key, (None, 0))[1] < val:
                need[key] = (sem, val)
        for key, (sem, val) in need.items():
            if self.obs[engine].get(key, 0) < val:
                self.eng[engine].wait_ge(sem, val)
                self.obs[engine][key] = val

    def _deps(self, reads, writes):
        evts = set()
        for k in reads:
            if k in self.lw:
                evts.add(self.lw[k])
        for k in writes:
            if k in self.lw:
                evts.add(self.lw[k])
            for ev in self.rd.get(k, {}).values():
                evts.add(ev)
        return evts

    def emit(self, engine, fn, reads=(), writes=(), signal=True):
        self._wait(engine, self._deps(reads, writes))
        ins = fn()
        idx = len(self.ins[engine]); self.ins[engine].append(ins)
        if signal:
            ins.then_inc(self.sem[engine], 1)
            self.cnt[engine] += 1
            self.sig_idx[engine].append(idx); self.sig_cnt[engine].append(self.cnt[engine])
        ev = ('c', engine, idx)
        for k in writes:
            self.lw[k] = ev; self.rd[k] = {}
        for k in reads:
            if k not in writes:
                self.rd.setdefault(k, {})[engine] = ev
        return ins

    def dma(self, fn, semkey, reads=(), writes=(), queue='sp'):
        if semkey not in self.dma_sems:
            self.dma_sems[semkey] = self.nc.alloc_semaphore('dma_' + semkey)
            self.dma_total[semkey] = 0
        self._wait(queue, self._deps(reads, writes))
        ins = fn()
        ins.then_inc(self.dma_sems[semkey], 16)
        self.dma_total[semkey] += 16
        ev = ('d', semkey)
        for k in writes:
            self.lw[k] = ev; self.rd[k] = {}
        for k in reads:
            self.rd.setdefault(k, {})[ev] = ev
        return ins

    def fence(self):
        evs = [('c', e, len(self.ins[e]) - 1) for e in self.sem if self.ins[e]]
        for e in ['pe', 'act', 'dve', 'pool', 'sp']:
            self._wait(e, evs)

    def wait_dma_all(self, engine='sp'):
        self._wait(engine, [('d', k) for k in self.dma_sems])


def build_nc(nseq=NSEQ, stop_after=None):
    nc = bass.Bass("TRN2", target_bir_lowering=False)
    S = Sched(nc)
    ntok = nseq * SEQ
    dram = lambda n, s, dt=F32, kind="ExternalInput": nc.dram_tensor(n, s, dt, kind=kind).ap()
    x = dram("x", [ntok, D])
    w_gu = [dram("w_ffn1_gu", [D, 2 * DFF]), dram("w_ffn2_gu", [D, 2 * DFF])]
    w_dn = [dram("w_ffn1_down", [DFF, D]), dram("w_ffn2_down", [DFF, D])]
    w_in = dram("w_in", [D, 7 * D])
    w_out = dram("w_out", [D, D])
    pvec_d = dram("pvec", [128, NP_COLS])
    gw_d = dram("gw", [128, 32 * 128])
    qk_d = dram("qk", [128, 2])
    ident_d = dram("ident", [128, 128])
    ebias_d = dram("ebias", [16, 128, ETAB_COLS])
    out = dram("out", [ntok, D], kind="ExternalOutput")
    wscr = dram("wscr", [TILES_PER_SEQ, 128, 1024], BF16, kind="Internal")
    escr = dram("escr", [16, 128, ETAB_COLS], BF16, kind="Internal")

    lay = _table_layout()
    top = ExitStack()
    uid = [0]

    def sb(n, s, dt, st=top):
        uid[0] += 1
        return st.enter_context(nc.sbuf_tensor(f"s{uid[0]}_{n}", s, dt))

    ps = lambda n, s, dt: top.enter_context(nc.psum_tensor(n, s, dt))

    ident = sb("ident", [128, 128], F32)
    identb = sb("identb", [128, 128], BF16)
    onesb = sb("onesb", [128, 128], BF16)
    bones = sb("bones", [128, 128], BF16)
    pvec = sb("pvec", [128, NP_COLS], F32)
    gwb = sb("gwb", [128, 32, 128], BF16)
    qk = sb("qksc", [128, 2], F32)
    cst = sb("cst", [128, 4], F32)
    cA = sb("cA", [128, 32], F32)
    PSA = ps("psA", [128, 1024], F32); PSB = ps("psB", [128, 1024], F32); PSC = ps("psC", [128, 1024], F32)
    PSD = ps("psD", [128, 512], F32); PST = ps("psT", [128, 1024], BF16)
    banks = [(PSA, 0, 'psA0'), (PSA, 512, 'psA1'), (PSB, 0, 'psB0'), (PSB, 512, 'psB1'),
             (PSC, 0, 'psC0'), (PSC, 512, 'psC1')]
    PSD_KEYS = ['psD']

    def bank_ap(i, n=512):
        t, o, k = banks[i]
        return t[:, o:o + n], k

    G1, GM, G2, CW, CB, BG, LAM = 0, 8, 16, 24, 56, 64, 96

    S.dma(lambda: nc.sync.dma_start(out=ident[:], in_=ident_d[:, :]), 'c0', writes=['ident'])
    S.dma(lambda: nc.sync.dma_start(out=pvec[:], in_=pvec_d[:, :]), 'c0', writes=['pvec'])
    S.dma(lambda: nc.sync.dma_start(out=qk[:], in_=qk_d[:, :]), 'c0', writes=['qk'])
    S.emit('dve', lambda: nc.vector.tensor_copy(out=identb[:], in_=ident[:]), ['ident'], ['identb'])
    S.emit('dve', lambda: nc.vector.memset(onesb[:], 1.0), [], ['onesb'])
    S.emit('dve', lambda: nc.vector.memset(bones[:], 0.0), [], ['bones'])
    S.emit('dve', lambda: nc.vector.memset(bones[0:64, 0:64], 1.0), [], ['bones'])
    S.emit('dve', lambda: nc.vector.memset(bones[64:128, 64:128], 1.0), [], ['bones'])
    S.emit('dve', lambda: nc.vector.memset(cst[:, 0:1], EPS), [], ['cst'])
    S.emit('dve', lambda: nc.vector.memset(cst[:, 1:2], 1.0), [], ['cst'])
    S.emit('dve', lambda: nc.vector.tensor_scalar(out=qk[:, 0:1], in0=qk[:, 0:1], scalar1=0.125, scalar2=None,
                                                  op0=ALU.mult), ['qk'], ['qk'])

    with ExitStack() as pst:
        psb = lambda n, s, dt: sb(n, s, dt, pst)
        lx = psb("lx", [128, 16], F32); lp = psb("lp", [128, 16], F32); lb = psb("lb", [128, 16], F32)
        lm = psb("lm", [128, 16], F32)
        lam = pvec[:, LAM:LAM + 16]
        S.emit('act', lambda: nc.scalar.activation(out=lx[:], in_=lam, func=AF.Exp, scale=-1.0), ['pvec'], ['lx'])
        S.emit('act', lambda: nc.scalar.activation(out=lb[:], in_=lx[:], func=AF.Ln, bias=cst[:, 1:2], scale=1.0),
               ['lx', 'cst'], ['lb'])
        S.emit('dve', lambda: nc.vector.tensor_scalar(out=lp[:], in0=lx[:], scalar1=-0.2, scalar2=0.25,
                                                      op0=ALU.mult, op1=ALU.add), ['lx'], ['lp'])
        for coef in (1.0 / 3.0, 0.5, 1.0):
            S.emit('dve', lambda: nc.vector.tensor_tensor(out=lp[:], in0=lp[:], in1=lx[:], op=ALU.mult), ['lp', 'lx'], ['lp'])
            S.emit('dve', lambda: nc.vector.tensor_scalar(out=lp[:], in0=lp[:], scalar1=-1.0, scalar2=coef,
                                                          op0=ALU.mult, op1=ALU.add), ['lp'], ['lp'])
        S.emit('dve', lambda: nc.vector.tensor_tensor(out=lp[:], in0=lp[:], in1=lx[:], op=ALU.mult), ['lp', 'lx'], ['lp'])
        S.emit('dve', lambda: nc.vector.tensor_single_scalar(out=lm[:], in_=lx[:], scalar=0.05, op=ALU.is_lt), ['lx'], ['lm'])
        S.emit('dve', lambda: nc.vector.tensor_tensor(out=lp[:], in0=lp[:], in1=lb[:], op=ALU.subtract), ['lp', 'lb'], ['lp'])
        S.emit('dve', lambda: nc.vector.tensor_tensor(out=lp[:], in0=lp[:], in1=lm[:], op=ALU.mult), ['lp', 'lm'], ['lp'])
        S.emit('dve', lambda: nc.vector.tensor_tensor(out=lp[:], in0=lp[:], in1=lb[:], op=ALU.add), ['lp', 'lb'], ['lp'])
        S.emit('dve', lambda: nc.vector.tensor_scalar(out=cA[:, 0:16], in0=lp[:], scalar1=-8.0, scalar2=None, op0=ALU.mult),
               ['lp'], ['cA'])
        S.emit('dve', lambda: nc.vector.tensor_scalar(out=cA[:, 16:32], in0=lp[:], scalar1=-16.0, scalar2=None, op0=ALU.mult),
               ['lp'], ['cA'])

        stf = psb("stf", [128, 2, 7168], F32)
        stb = psb("stb", [128, 56, 8, 128], BF16)
        stbf = stb[:].rearrange("p n c j -> p (n c j)")
        cast_engs = ['act', 'dve']
        cnt = [0]

        def cast(out_ap, in_ap, rd, wr):
            e = cast_engs[cnt[0] % 2]; cnt[0] += 1
            if e == 'act':
                S.emit('act', lambda: nc.scalar.copy(out=out_ap, in_=in_ap), rd, wr)
            elif e == 'dve':
                S.emit('dve', lambda: nc.vector.tensor_copy(out=out_ap, in_=in_ap), rd, wr)
            else:
                S.emit('pool', lambda: nc.gpsimd.tensor_copy(out=out_ap, in_=in_ap), rd, wr)

        ld = [0]

        def stage_load(src_ap, ncols, inner=None):
            sl = ld[0] % 2; ld[0] += 1
            dst = stf[:, sl, 0:ncols]
            if inner is not None:
                dst = dst.rearrange("p (f n) -> p f n", n=inner)
            S.dma(lambda: nc.sync.dma_start(out=dst, in_=src_ap), f'stf{sl}', writes=[f'stf{sl}'])
            return sl

        def colchunk_weight(W, ncols, store_fn):
            nt = ncols // 128
            for cc in range(8):
                sl = stage_load(W[cc * 128:(cc + 1) * 128, :], ncols)
                cast(stb[:, 0:nt, cc, :], stf[:, sl, 0:ncols].rearrange("p (n j) -> p n j", j=128),
                     [f'stf{sl}'], ['stb'])
            store_fn()

        def store_tiles(n0, cnt_, i0, step):
            src = stb[:, n0:n0 + cnt_, :, :].rearrange("p n c j -> p n (c j)")
            dst = wscr[i0:i0 + step * (cnt_ - 1) + 1:step].rearrange("n p x -> p n x")
            S.dma(lambda: nc.sync.dma_start(out=dst, in_=src), 'stst', reads=['stb'])

        for fi, base in ((0, FFN1_BASE), (1, FFN2_BASE)):
            def st_gu(base=base):
                for G in range(2):
                    for kind in range(2):
                        store_tiles(kind * NFC + G * 11, 11, base + G * 33 + kind, 2)
            colchunk_weight(w_gu[fi], 2 * DFF, st_gu)
            for G in range(2):
                for half, (f0, nf) in enumerate(((0, 6), (6, 5))):
                    fa = G * 11 + f0
                    sl = stage_load(w_dn[fi][fa * 128:(fa + nf) * 128, :].rearrange("(f p) n -> p f n", p=128), nf * 1024, 1024)
                    cast(stbf[:, 0:nf * 1024], stf[:, sl, 0:nf * 1024], [f'stf{sl}'], ['stb'])
                    i0 = base + G * 33 + 22 + f0
                    S.dma(lambda: nc.sync.dma_start(out=wscr[i0:i0 + nf].rearrange("n p x -> p n x"),
                                                    in_=stbf[:, 0:nf * 1024].rearrange("p (n x) -> p n x", x=1024)),
                          'stst', reads=['stb'])
        KORD = [0, 1, 4, 5, 6, 2, 3]

        def st_in():
            for fs in range(7):
                store_tiles(fs * 8, 1, MIX_BASE + _tix(0, KORD[fs]), 1)
                store_tiles(fs * 8 + 1, 7, MIX_BASE + _tix(1, KORD[fs]), 8)
        colchunk_weight(w_in, 7 * D, st_in)
        for half in range(2):
            sl = stage_load(w_out[half * 512:(half + 1) * 512, :].rearrange("(c p) n -> p c n", p=128), 4096, 1024)
            cast(stbf[:, 0:4096], stf[:, sl, 0:4096], [f'stf{sl}'], ['stb'])
            for ci in range(4):
                i0 = MIX_BASE + _wix(half * 4 + ci)
                S.dma(lambda: nc.sync.dma_start(out=wscr[i0], in_=stbf[:, ci * 1024:(ci + 1) * 1024]),
                      'stst', reads=['stb'])
        sl = stage_load(gw_d[:, :], 4096)
        S.emit('dve', lambda: nc.vector.tensor_copy(out=gwb[:].rearrange("p a b -> p (a b)"), in_=stf[:, sl, 0:4096]),
               [f'stf{sl}'], ['gwb'])
        S.wait_dma_all('sp')
        S.fence()

    xT = sb("xT", [128, 8, SEQ], F32)
    hT = sb("hT", [128, 8, SEQ], BF16)
    ring = sb("ring", [128, NSLOT, 1024], BF16)
    total_tiles = TILES_PER_SEQ * nseq
    wst = {'next': 0}

    def load_upto(n):
        while wst['next'] < min(n, total_tiles):
            i = wst['next']; s = i % NSLOT
            S.dma(lambda: nc.sync.dma_start(out=ring[:, s, :], in_=wscr[i % TILES_PER_SEQ]), f'ring{s}',
                  writes=[f'ring{s}'])
            wst['next'] += 1

    def wget(i):
        load_upto(i + 1)
        return i % NSLOT

    def wdone(i):
        load_upto(i + 1 + NSLOT)

    load_upto(NSLOT)

    def tl(t):
        return slice(t * 512, (t + 1) * 512)

    def mm(out_ap, lhsT, rhs, start, stop, rd, wr):
        S.emit('pe', lambda: nc.tensor.matmul(out_ap, lhsT, rhs, start=start, stop=stop), rd, wr, signal=bool(stop))

    def rmsnorm(gcol, st):
        sq = sb("nsq", [128, 2, 512], BF16, st)
        sd = sb("nsd", [128, 2, 512], F32, st)
        for t in range(4):
            for c in range(8):
                b = c % 2
                S.emit('act', lambda: nc.scalar.activation(out=sq[:, b, :], in_=xT[:, c, tl(t)], func=AF.Square),
                       [f'xT{c}.{t}'], [f'nsq{b}'])
                mm(PSD[:, :], onesb[:], sq[:, b, :], c == 0, c == 7, ['onesb', f'nsq{b}'], PSD_KEYS)
            b = t % 2
            S.emit('act', lambda: nc.scalar.activation(out=sd[:, b, :], in_=PSD[:, :], func=AF.Ln,
                                                       bias=cst[:, 0:1], scale=1.0 / D), PSD_KEYS + ['cst'], [f'nsd{b}'])
            S.emit('act', lambda: nc.scalar.activation(out=sd[:, b, :], in_=sd[:, b, :], func=AF.Exp, scale=-0.5),
                   [f'nsd{b}'], [f'nsd{b}'])
            for c in range(8):
                S.emit('dve', lambda: nc.vector.scalar_tensor_tensor(out=hT[:, c, tl(t)], in0=xT[:, c, tl(t)],
                                                                     scalar=pvec[:, gcol + c:gcol + c + 1],
                                                                     in1=sd[:, b, :], op0=ALU.mult, op1=ALU.mult),
                       [f'xT{c}.{t}', 'pvec', f'nsd{b}'], [f'hT{c}.{t}'])

    def etab_steps(st):
        est = sb("est", [128, ETAB_COLS], F32, st)
        ebf = sb("ebf", [128, ETAB_COLS], BF16, st)
        for h in range(16):
            S.dma(lambda: nc.sync.dma_start(out=est[:, :], in_=ebias_d[h]), 'est', writes=['est'])
            S.emit('act', lambda: nc.scalar.activation(out=ebf[:, :], in_=est[:, :], func=AF.Exp), ['est'], ['ebf'])
            S.dma(lambda: nc.sync.dma_start(out=escr[h], in_=ebf[:, :]), 'ebst', reads=['ebf'], writes=[f'escr{h}'])
            yield

    def ffn(tbase, gcol, with_etab=False):
        with ExitStack() as st:
            bg = etab_steps(st) if with_etab else iter(())
            rmsnorm(gcol, st)
            act = sb("actT", [128, 11, SEQ], BF16, st)
            sg = sb("sg", [128, 2, 512], F32, st)
            k = 0
            for G in range(2):
                gb_ = tbase + G * 33
                for fl in range(11):
                    sg_ = wget(gb_ + 2 * fl); su_ = wget(gb_ + 2 * fl + 1)
                    for t in range(4):
                        pg, kg = bank_ap(k % 2); pu, ku = bank_ap(2 + k % 2); b = k % 2; k += 1
                        for cc in range(8):
                            mm(pg, ring[:, sg_, cc * 128:(cc + 1) * 128], hT[:, cc, tl(t)], cc == 0, cc == 7,
                               [f'ring{sg_}', f'hT{cc}.{t}'], [kg])
                        for cc in range(8):
                            mm(pu, ring[:, su_, cc * 128:(cc + 1) * 128], hT[:, cc, tl(t)], cc == 0, cc == 7,
                               [f'ring{su_}', f'hT{cc}.{t}'], [ku])
                        S.emit('act', lambda: nc.scalar.activation(out=sg[:, b, :], in_=pg, func=AF.Silu), [kg], [f'sg{b}'])
                        S.emit('dve', lambda: nc.vector.tensor_tensor(out=act[:, fl, tl(t)], in0=sg[:, b, :], in1=pu, op=ALU.mult),
                               [f'sg{b}', ku], [f'act{fl}.{t}'])
                    wdone(gb_ + 2 * fl); wdone(gb_ + 2 * fl + 1)
                    next(bg, None)
                dslots = [wget(gb_ + 22 + fl) for fl in range(11)]
                kd = 0
                for o in range(8):
                    for t in range(4):
                        pd, kdk = bank_ap(4 + kd % 2); kd += 1
                        for fl in range(11):
                            mm(pd, ring[:, dslots[fl], o * 128:(o + 1) * 128], act[:, fl, tl(t)], fl == 0, fl == 10,
                               [f'ring{dslots[fl]}', f'act{fl}.{t}'], [kdk])
                        S.emit('dve', lambda: nc.vector.scalar_tensor_tensor(out=xT[:, o, tl(t)], in0=pd, scalar=0.5,
                                                                             in1=xT[:, o, tl(t)], op0=ALU.mult, op1=ALU.add),
                               [kdk, f'xT{o}.{t}'], [f'xT{o}.{t}'])
                for fl in range(11):
                    wdone(gb_ + 22 + fl)
            for _ in bg:
                pass
            if with_etab:
                for e_ in ['pe', 'act', 'dve', 'pool', 'sp']:
                    S._wait(e_, [('d', 'ebst'), ('d', 'est')])
        S.fence()

    pk = [0]

    def proj(slot, t, bank_ids=(4, 5)):
        pb, kb = bank_ap(bank_ids[pk[0] % len(bank_ids)]); pk[0] += 1
        for cc in range(8):
            mm(pb, ring[:, slot, cc * 128:(cc + 1) * 128], hT[:, cc, tl(t)], cc == 0, cc == 7,
               [f'ring{slot}', f'hT{cc}.{t}'], [kb])
        return pb, kb

    def mixer(tbase, mst):
        with ExitStack() as st:
            rmsnorm(GM, st)
        S.fence()
        ytcs = [sb("ytcA", [128, SEQ], BF16, mst), sb("ytcB", [128, SEQ], BF16, mst)]

        def wout_steps(cprev):
            ytp = ytcs[cprev % 2]
            s_wo = wget(tbase + _wix(cprev))
            ko = 0
            for o in range(8):
                for t in range(4):
                    pb, kb = bank_ap(4 + ko % 2); ko += 1
                    mm(pb, ring[:, s_wo, o * 128:(o + 1) * 128], ytp[:, tl(t)], True, True,
                       [f'ring{s_wo}', f'ytc{cprev % 2}.{t}'], [kb])
                    S.emit('dve', lambda: nc.vector.tensor_tensor(out=xT[:, o, tl(t)], in0=pb, in1=xT[:, o, tl(t)], op=ALU.add),
                           [kb, f'xT{o}.{t}'], [f'xT{o}.{t}'])
                    yield
            wdone(tbase + _wix(cprev))

        def wout_accum(cprev):
            for _ in wout_steps(cprev):
                pass

        for c in range(8):
            tbk = lambda k_: tbase + _tix(c, k_)
            ytc = ytcs[c % 2]
            with ExitStack() as cst_:
                yA = sb("yA", [128, SEQ], BF16, cst_)
                with ExitStack() as st:
                    xr = sb("xr", [128, SEQ + 4], F32, st)
                    xcb = sb("xcb", [128, SEQ], BF16, st)
                    Rb_ = [sb("Rf", [128, SEQ], F32, st), sb("Rb", [128, SEQ], F32, st)]
                    Ib_ = [sb("If", [128, SEQ], F32, st), sb("Ib", [128, SEQ], F32, st)]
                    M0 = sb("M0", [128, SEQ], F32, st)
                    g2 = sb("g2", [128, SEQ], BF16, st)
                    tmp = sb("tmpA", [128, 2, 512], F32, st)
                    Mb_ = [M0[:, :], xr[:, 2:2 + SEQ]]
                    Mk = ['M0', 'xr']
                    s_xr = wget(tbk(0))
                    S.emit('pool', lambda: nc.gpsimd.memset(xr[:, 0:2], 0.0), [], ['xr'])
                    S.emit('pool', lambda: nc.gpsimd.memset(xr[:, 2 + SEQ:4 + SEQ], 0.0), [], ['xr'])
                    for t in range(4):
                        pb, kb = proj(s_xr, t, (0, 1, 2, 3))
                        S.emit('act', lambda: nc.scalar.copy(out=xr[:, 2 + t * 512:2 + (t + 1) * 512], in_=pb), [kb], ['xr'])
                    wdone(tbk(0))
                    cw = lambda k_: pvec[:, CW + k_ * 8 + c:CW + k_ * 8 + c + 1]
                    S.emit('pool', lambda: nc.gpsimd.tensor_scalar(out=M0[:, :], in0=xr[:, 0:SEQ], scalar1=cw(0),
                                                                   scalar2=pvec[:, CB + c:CB + c + 1], op0=ALU.mult, op1=ALU.add),
                           ['xr', 'pvec'], ['M0'])
                    for k_ in range(1, 4):
                        S.emit('dve', lambda: nc.vector.scalar_tensor_tensor(out=M0[:, :], in0=xr[:, k_:k_ + SEQ], scalar=cw(k_),
                                                                              in1=M0[:, :], op0=ALU.mult, op1=ALU.add),
                               ['xr', 'pvec', 'M0'], ['M0'])
                    S.emit('act', lambda: nc.scalar.copy(out=xcb[:, :], in_=M0[:, :]), ['M0'], ['xcb'])
                    s_gr = wget(tbk(1))
                    for t in range(4):
                        pb, kb = proj(s_gr, t, (0, 1, 2, 3)); b = t % 2
                        S.emit('act', lambda: nc.scalar.activation(out=tmp[:, b, :], in_=pb, func=AF.Square, scale=0.21145921592969426),
                               [kb], [f'tmpA{b}'])
                        S.emit('dve', lambda: nc.vector.scalar_tensor_tensor(out=tmp[:, b, :], in0=tmp[:, b, :], scalar=1.0, in1=pb,
                                                                             op0=ALU.add, op1=ALU.mult),
                               [f'tmpA{b}', kb], [f'tmpA{b}'])
                        S.emit('act', lambda: nc.scalar.activation(out=tmp[:, b, :], in_=tmp[:, b, :], func=AF.Sigmoid,
                                                                   scale=1.5957691216057308), [f'tmpA{b}'], [f'tmpA{b}'])
                        S.emit('dve', lambda: nc.vector.tensor_tensor(out=g2[:, tl(t)], in0=tmp[:, b, :], in1=pb, op=ALU.mult),
                               [f'tmpA{b}', kb], [f'g2.{t}'])
                    wdone(tbk(1))
                    s_ga = wget(tbk(2))
                    for t in range(4):
                        pb, kb = proj(s_ga, t, (0, 1, 2, 3)); b = t % 2
                        S.emit('act', lambda: nc.scalar.activation(out=tmp[:, b, :], in_=pb, func=AF.Sigmoid), [kb], [f'tmpA{b}'])
                        S.emit('dve', lambda: nc.vector.tensor_tensor(out=g2[:, tl(t)], in0=g2[:, tl(t)], in1=tmp[:, b, :], op=ALU.mult),
                               [f'tmpA{b}', f'g2.{t}'], [f'g2.{t}'])
                    wdone(tbk(2))
                    wgen = wout_steps(c - 1) if c > 0 else iter(())
                    for d in range(2):
                        for g_, (buf, bk) in enumerate(((Rb_[d], f'R{d}'), (Ib_[d], f'I{d}'))):
                            gi = (d * 2 + g_) * 8 + c
                            for t in range(4):
                                pb, kb = bank_ap((d * 8 + g_ * 4 + t) % 4)
                                mm(pb, gwb[:, gi, :], xcb[:, tl(t)], True, True, ['gwb', 'xcb'], [kb])
                                S.emit('act', lambda: nc.scalar.activation(out=buf[:, tl(t)], in_=pb, func=AF.Sigmoid,
                                                                           bias=pvec[:, BG + gi:BG + gi + 1], scale=1.0),
                                       [kb, 'pvec'], [bk])
                                next(wgen, None)
                    for _ in wgen:
                        pass
                    for d in range(2):
                        S.emit('act', lambda: nc.scalar.activation(out=Mb_[d], in_=Rb_[d][:, :], func=AF.Exp,
                                                                   scale=cA[:, 16 + d * 8 + c:16 + d * 8 + c + 1]),
                               [f'R{d}', 'cA'], [Mk[d]])
                        S.emit('act', lambda: nc.scalar.activation(out=Rb_[d][:, :], in_=Rb_[d][:, :], func=AF.Exp,
                                                                   scale=cA[:, d * 8 + c:d * 8 + c + 1]),
                               [f'R{d}', 'cA'], [f'R{d}'])
                    for d in range(2):
                        S.emit('act', lambda: nc.scalar.activation(out=Mb_[d], in_=Mb_[d], func=AF.Sqrt,
                                                                   bias=cst[:, 1:2], scale=-1.0), [Mk[d], 'cst'], [Mk[d]])
                    for d, e in ((0, 'dve'), (1, 'dve')):
                        eo = nc.vector if e == 'dve' else nc.gpsimd
                        S.emit(e, lambda: eo.tensor_tensor(out=Ib_[d][:, :], in0=Ib_[d][:, :], in1=xcb[:, :], op=ALU.mult),
                               [f'I{d}', 'xcb'], [f'I{d}'])
                        S.emit(e, lambda: eo.tensor_tensor(out=Ib_[d][:, :], in0=Ib_[d][:, :], in1=Mb_[d], op=ALU.mult),
                               [f'I{d}', Mk[d]], [f'I{d}'])
                    S.emit('dve', lambda: nc.vector.tensor_tensor_scan(out=Mb_[0], data0=Rb_[0][:, :], data1=Ib_[0][:, :],
                                                                       initial=0.0, op0=ALU.mult, op1=ALU.add),
                           ['R0', 'I0'], ['M0'])
                    S.emit('dve', lambda: nc.vector.tensor_tensor_scan(out=Mb_[1][:, ::-1], data0=Rb_[1][:, ::-1],
                                                                       data1=Ib_[1][:, ::-1], initial=0.0,
                                                                       op0=ALU.mult, op1=ALU.add),
                           ['R1', 'I1'], ['xr'])
                    S.emit('dve', lambda: nc.vector.tensor_tensor(out=Mb_[0], in0=Mb_[0], in1=Mb_[1], op=ALU.add),
                           ['M0', 'xr'], ['M0'])
                    S.emit('dve', lambda: nc.vector.tensor_tensor(out=yA[:, :], in0=Mb_[0], in1=g2[:, :], op=ALU.mult),
                           ['M0'] + [f'g2.{t}' for t in range(4)], ['yA'])
                S.fence()
                with ExitStack() as st:
                    qn = sb("qn", [128, SEQ], BF16, st); kn = sb("kn", [128, SEQ], BF16, st)
                    Vc = sb("Vc", [128, 16, 2, 65], BF16, st)
                    sgb = sb("sgb", [128, SEQ], BF16, st)
                    Et = sb("Et", [128, 2, ETAB_COLS], BF16, st)
                    pT = sb("pT", [128, 4, 640], BF16, st)
                    sq = sb("bsq", [128, 2, 512], BF16, st)
                    sd = sb("bsd", [128, 2, 512], F32, st)
                    ybt = sb("ybt", [128, 2, 128], BF16, st)
                    rec = sb("rec", [128, 4], F32, st)
                    ytmp = sb("ytmp", [128, 2, 512], BF16, st)
                    for hh in range(2):
                        S.dma(lambda: nc.sync.dma_start(out=Et[:, hh, :], in_=escr[2 * c + hh]), f'et{hh}', reads=[f'escr{2 * c + hh}'], writes=[f'Et{hh}'])
                    s_gb = wget(tbk(3))
                    for t in range(4):
                        pb, kb = proj(s_gb, t, (0, 1, 2, 3, 4, 5))
                        S.emit('act', lambda: nc.scalar.activation(out=sgb[:, tl(t)], in_=pb, func=AF.Sigmoid), [kb], [f'sgb{t}'])
                    wdone(tbk(3))
                    for wi, (dst, dk, col) in enumerate(((qn, 'qn', 0), (kn, 'kn', 1))):
                        s_ = wget(tbk(4 + wi))
                        for t in range(4):
                            pb, kb = proj(s_, t, (0, 1, 2, 3, 4, 5)); b = t % 2
                            S.emit('act', lambda: nc.scalar.activation(out=sq[:, b, :], in_=pb, func=AF.Square), [kb], [f'bsq{b}'])
                            mm(PSD[:, :], bones[:], sq[:, b, :], True, True, ['bones', f'bsq{b}'], PSD_KEYS)
                            S.emit('act', lambda: nc.scalar.activation(out=sd[:, b, :], in_=PSD[:, :], func=AF.Ln,
                                                                       bias=cst[:, 0:1], scale=1.0 / 64), PSD_KEYS + ['cst'], [f'bsd{b}'])
                            S.emit('act', lambda: nc.scalar.activation(out=sd[:, b, :], in_=sd[:, b, :], func=AF.Exp, scale=-0.5),
                                   [f'bsd{b}'], [f'bsd{b}'])
                            S.emit('dve', lambda: nc.vector.scalar_tensor_tensor(out=dst[:, tl(t)], in0=pb, scalar=qk[:, col:col + 1],
                                                                                 in1=sd[:, b, :], op0=ALU.mult, op1=ALU.mult),
                                   [kb, 'qk', f'bsd{b}'], [f'{dk}{t}'])
                        wdone(tbk(4 + wi))
                    s_v = wget(tbk(6))
                    S.emit('pool', lambda: nc.gpsimd.memset(Vc[:, :, :, 64:65], 1.0), [], ['Vones'])
                    for tq in range(4):
                        pb, kb = bank_ap(4 + tq % 2)
                        for tt in range(4):
                            tok = slice((tq * 4 + tt) * 128, (tq * 4 + tt + 1) * 128)
                            for cc in range(8):
                                mm(pb[:, tt * 128:(tt + 1) * 128], hT[:, cc, tok], ring[:, s_v, cc * 128:(cc + 1) * 128],
                                   cc == 0, cc == 7, [f'ring{s_v}', f'hT{cc}.{tq}'], [kb])
                        S.emit('act', lambda: nc.scalar.copy(out=Vc[:, tq * 4:tq * 4 + 4, :, 0:64],
                                                             in_=pb.rearrange("p (a h d) -> p a h d", a=4, h=2)),
                               [kb], [f'Vc{tq}'])
                    wdone(tbk(6))
                    iters = [(m, hh) for m in range(16) for hh in range(2)]

                    def sbuf_of(it):
                        return [(PSA, ['psA0', 'psA1']), (PSB, ['psB0', 'psB1'])][it % 2]

                    def scores(it):
                        m, hh = iters[it]
                        off, js = lay[m]; hb = hh * 64
                        SP, spk = sbuf_of(it)
                        for jj, j in enumerate(js):
                            mm(SP[:, jj * 128:(jj + 1) * 128], kn[hb:hb + 64, j * 128:(j + 1) * 128],
                               qn[hb:hb + 64, m * 128:(m + 1) * 128], True, True,
                               [f'kn{j // 4}', f'qn{m // 4}'], [spk[jj // 4]])

                    def stage_a(it):
                        m, hh = iters[it]
                        off, js = lay[m]; n = len(js)
                        SP, spk = sbuf_of(it)
                        pslot = it % 4
                        S.emit('act', lambda: nc.scalar.activation(out=pT[:, pslot, 0:n * 128], in_=SP[:, 0:n * 128], func=AF.Exp),
                               spk, [f'pT{pslot}'])
                        if it % 2 == 0:
                            S.emit('pool', lambda: nc.gpsimd.tensor_tensor(out=pT[:, pslot, 0:n * 128], in0=pT[:, pslot, 0:n * 128],
                                                                           in1=Et[:, hh, off:off + n * 128], op=ALU.mult),
                                   [f'pT{pslot}', f'Et{hh}'], [f'pT{pslot}'])
                        else:
                            S.emit('dve', lambda: nc.vector.tensor_tensor(out=pT[:, pslot, 0:n * 128], in0=pT[:, pslot, 0:n * 128],
                                                                          in1=Et[:, hh, off:off + n * 128], op=ALU.mult),
                                   [f'pT{pslot}', f'Et{hh}'], [f'pT{pslot}'])

                    def stage_b(it):
                        m, hh = iters[it]
                        off, js = lay[m]; n = len(js); t_q = m // 4; ys = m % 2; hb = hh * 64
                        pslot = it % 4
                        ops, okey = [(PSC[:, 0:65], 'psC0'), (PSC[:, 512:577], 'psC1'), (PSD[:, 0:65], 'psD')][it % 3]
                        for jj, j in enumerate(js):
                            mm(ops, pT[:, pslot, jj * 128:(jj + 1) * 128], Vc[:, j, hh, :], jj == 0, jj == n - 1,
                               [f'pT{pslot}', f'Vc{j // 4}', 'Vones'], [okey])
                        rcol = it % 4
                        S.emit('dve', lambda: nc.vector.reciprocal(out=rec[:, rcol:rcol + 1], in_=ops[:, 64:65]),
                               [okey], [f'rec{rcol}'])
                        S.emit('dve', lambda: nc.vector.tensor_scalar(out=ybt[:, ys, hb:hb + 64], in0=ops[:, 0:64],
                                                                      scalar1=rec[:, rcol:rcol + 1], scalar2=None, op0=ALU.mult),
                               [okey, f'rec{rcol}'], [f'ybt{ys}h{hh}'])
                        if hh == 1:
                            pend.append(lambda m=m, t_q=t_q, ys=ys: tail(m, t_q, ys))

                    def tail(m, t_q, ys):
                        if True:
                            half = t_q % 2
                            S.emit('pe', lambda: nc.tensor.transpose(PST[:, half * 512 + (m % 4) * 128:half * 512 + (m % 4 + 1) * 128],
                                                                     ybt[:, ys, :], identb[:]),
                                   [f'ybt{ys}h0', f'ybt{ys}h1', 'identb'], ['psT'])
                            if m % 4 == 3:
                                S.emit('dve', lambda: nc.vector.tensor_tensor(out=ytmp[:, half, :], in0=PST[:, half * 512:(half + 1) * 512],
                                                                              in1=sgb[:, tl(t_q)], op=ALU.mult),
                                       ['psT', f'sgb{t_q}'], [f'ytmp{half}'])
                                S.emit('pool', lambda: nc.gpsimd.tensor_tensor(out=ytc[:, tl(t_q)], in0=ytmp[:, half, :],
                                                                               in1=yA[:, tl(t_q)], op=ALU.add),
                                       [f'ytmp{half}', 'yA'], [f'ytc{c % 2}.{t_q}'])

                    pend = []
                    scores(0); stage_a(0); scores(1); stage_a(1)
                    for it in range(len(iters)):
                        if it + 2 < len(iters):
                            scores(it + 2)
                            stage_a(it + 2)
                        todo = pend[:]; del pend[:]
                        stage_b(it)
                        for f_ in todo:
                            f_()
                    for f_ in pend:
                        f_()
                S.fence()
        wout_accum(7)
        S.fence()

    for s in range(nseq):
        row0 = s * SEQ
        wb = s * TILES_PER_SEQ
        with ExitStack() as st:
            xin = sb("xin", [128, 2, D], F32, st)
            for tt in range(16):
                sl = tt % 2
                S.dma(lambda: nc.sync.dma_start(out=xin[:, sl, :], in_=x[row0 + tt * 128:row0 + (tt + 1) * 128, :]),
                      f'xin{sl}', writes=[f'xin{sl}'])
                P_, pkeys = (PSA, ['psA0', 'psA1']) if tt % 2 == 0 else (PSB, ['psB0', 'psB1'])
                for c in range(8):
                    S.emit('pe', lambda: nc.tensor.transpose(P_[:, c * 128:(c + 1) * 128], xin[:, sl, c * 128:(c + 1) * 128], ident[:]),
                           [f'xin{sl}', 'ident'], [pkeys[c // 4]])
                t = tt // 4
                for hf in range(2):
                    dst = xT[:, hf * 4:hf * 4 + 4, tt * 128:(tt + 1) * 128]
                    src = P_[:, hf * 512:(hf + 1) * 512].rearrange("p (c j) -> p c j", j=128)
                    wk = [f'xT{c}.{t}' for c in range(hf * 4, hf * 4 + 4)]
                    if t % 2 == 0:
                        S.emit('act', lambda: nc.scalar.copy(out=dst, in_=src), [pkeys[hf]], wk)
                    else:
                        S.emit('dve', lambda: nc.vector.tensor_copy(out=dst, in_=src), [pkeys[hf]], wk)
        S.fence()
        if stop_after != 'load':
            ffn(wb + FFN1_BASE, G1, with_etab=(s == 0))
            if stop_after != 'ffn1':
                with ExitStack() as mst:
                    mixer(wb + MIX_BASE, mst)
                if stop_after != 'mixer':
                    ffn(wb + FFN2_BASE, G2)
        with ExitStack() as st:
            xo = sb("xo", [128, 2, D], F32, st)
            for tt in range(16):
                sl = tt % 2; t = tt // 4
                P_, pkeys = (PSA, ['psA0', 'psA1']) if tt % 2 == 0 else (PSB, ['psB0', 'psB1'])
                for c in range(8):
                    S.emit('pe', lambda: nc.tensor.transpose(P_[:, c * 128:(c + 1) * 128], xT[:, c, tt * 128:(tt + 1) * 128], ident[:]),
                           [f'xT{c}.{t}', 'ident'], [pkeys[c // 4]])
                for hf in range(2):
                    if hf == 0:
                        S.emit('act', lambda: nc.scalar.copy(out=xo[:, sl, 0:512], in_=P_[:, 0:512]), [pkeys[0]], [f'xo{sl}'])
                    else:
                        S.emit('dve', lambda: nc.vector.tensor_copy(out=xo[:, sl, 512:1024], in_=P_[:, 512:1024]), [pkeys[1]], [f'xo{sl}'])
                S.dma(lambda: nc.sync.dma_start(out=out[row0 + tt * 128:row0 + (tt + 1) * 128, :], in_=xo[:, sl, :]),
                      f'xo{sl}', reads=[f'xo{sl}'])
            S.wait_dma_all('sp')
        S.fence()
    S.wait_dma_all('sp')
    top.close()
    return nc


def _host_layout(inputs):
    f = lambda k: np.ascontiguousarray(np.asarray(inputs[k], dtype=np.float32))
    pc = lambda v: np.ascontiguousarray(v.reshape(8, 128).T)
    cols = [pc(f('norm_ffn1')[0]), pc(f('norm_mix')[0]), pc(f('norm_ffn2')[0])]
    cw = f('conv_w')[0]
    cols += [pc(cw[k]) for k in range(4)]
    cols += [pc(f('conv_b')[0])]
    bgt = f('lru_b_gates')[0]
    cols += [pc(bgt[d, g]) for d in range(2) for g in range(2)]
    lam = f('lru_lambda')[0]
    cols += [pc(lam[d]) for d in range(2)]
    pvec = np.ascontiguousarray(np.concatenate(cols, axis=1))
    assert pvec.shape == (128, NP_COLS)
    wg = f('lru_w_gates')[0]
    gw = np.zeros((128, 32, 128), np.float32)
    for d in range(2):
        for g in range(2):
            for c in range(8):
                for bl in range(2):
                    gw[bl * 64:(bl + 1) * 64, (d * 2 + g) * 8 + c, bl * 64:(bl + 1) * 64] = wg[d, g, 2 * c + bl]
    qk = np.stack([np.tile(f('q_norm')[0], 2), np.tile(f('k_norm')[0], 2)], axis=1)
    dr, dc, valid = _build_index_tables()
    rpb = f('rel_pos_bias')[0]
    eb = rpb[:, dr, dc]
    eb[:, ~valid] = np.float32(NEG)
    return {
        "w_ffn1_gu": f('w_ffn1_gu')[0], "w_ffn2_gu": f('w_ffn2_gu')[0],
        "w_ffn1_down": f('w_ffn1_down')[0], "w_ffn2_down": f('w_ffn2_down')[0],
        "w_in": f('w_in')[0], "w_out": f('w_out')[0],
        "pvec": pvec, "gw": np.ascontiguousarray(gw.reshape(128, 32 * 128)),
        "qk": np.ascontiguousarray(qk.astype(np.float32)),
        "ident": np.eye(128, dtype=np.float32),
        "ebias": np.ascontiguousarray(eb.astype(np.float32)),
    }


_NC_CACHE = {}


def kernel(**inputs):
    shared = _host_layout(inputs)
    x = np.asarray(inputs['x'], dtype=np.float32)
    B = x.shape[0]
    per = B // NCORES
    if 'nc' not in _NC_CACHE:
        _NC_CACHE['nc'] = build_nc(per)
    nc = _NC_CACHE['nc']
    in_maps = []
    for i in range(NCORES):
        m = dict(shared)
        m["x"] = np.ascontiguousarray(x[i * per:(i + 1) * per].reshape(per * SEQ, D))
        in_maps.append(m)
    res = run_bass_kernel_spmd(nc, in_maps, core_ids=list(range(NCORES)))
    outs = [np.asarray(r["out"]).reshape(per, SEQ, D) for r in res.results]
    return np.concatenate(outs, axis=0).astype(np.float32)
```

```python
import bisect
from contextlib import ExitStack

import numpy as np
import concourse.bass as bass
import concourse.mybir as mybir
from concourse.bass_utils import run_bass_kernel_spmd

F32 = mybir.dt.float32
BF16 = mybir.dt.bfloat16
AF = mybir.ActivationFunctionType
ALU = mybir.AluOpType

NCORES = 8
NSEQ = 4
SEQ = 2048
D = 1024
DFF = 2816
NFC = 22
EPS = 1e-6
NSLOT = 14
TILES_PER_SEQ = 196
FFN1_BASE, MIX_BASE, FFN2_BASE = 0, 66, 130
NP_COLS = 112


def _tix(c, k):
    if c == 0:
        return k
    base = 7 + (c - 1) * 8
    return base + k if k < 2 else base + 1 + k


def _wix(c):
    return 9 + 8 * c if c < 7 else 63
ETAB_COLS = 2688
NEG = -30000.0

GRID_W = 64; ROWS = 32; WIN_ROWS = 8; WIN_COLS = 16
SPECIAL = [0, 1, 14, 15]


def _rs_of(r):
    return int(np.clip(r - WIN_ROWS // 2, 0, ROWS - WIN_ROWS))


def _chunks_for_tile(m):
    lo = _rs_of(2 * m); hi = _rs_of(2 * m + 1) + WIN_ROWS - 1
    return list(range(lo // 2, hi // 2 + 1))


def _table_layout():
    lay = {}
    off = 640
    for m in range(16):
        js = _chunks_for_tile(m)
        if m in SPECIAL:
            lay[m] = (off, js); off += 512
        else:
            lay[m] = (0, js)
    return lay


def _build_index_tables():
    lay = _table_layout()
    dr = np.zeros((128, ETAB_COLS), np.int64); dc = np.zeros((128, ETAB_COLS), np.int64)
    valid = np.zeros((128, ETAB_COLS), bool)
    done_interior = False
    for m in range(16):
        off, js = lay[m]
        if m not in SPECIAL and done_interior:
            continue
        for jj, j in enumerate(js):
            for kl in range(2):
                for ql in range(2):
                    kr = 2 * j + kl; qr = 2 * m + ql
                    vr = _rs_of(qr) <= kr < _rs_of(qr) + WIN_ROWS
                    kc = np.arange(64)[:, None]; qc = np.arange(64)[None, :]
                    csq = np.clip(qc - WIN_COLS // 2, 0, GRID_W - WIN_COLS)
                    v = (kc >= csq) & (kc < csq + WIN_COLS) & vr
                    d_r = kr - qr + WIN_ROWS - 1
                    d_c = np.clip(kc - qc, -(WIN_COLS - 1), WIN_COLS - 1) + WIN_COLS - 1
                    sl = (slice(kl * 64, kl * 64 + 64),
                          slice(off + jj * 128 + ql * 64, off + jj * 128 + ql * 64 + 64))
                    valid[sl] = v
                    dr[sl] = np.where(v, d_r, 0)
                    dc[sl] = np.where(v, d_c, 0)
        if m not in SPECIAL:
            done_interior = True
    return dr, dc, valid


class Sched:
    def __init__(self, nc):
        self.nc = nc
        self.eng = {'pe': nc.tensor, 'act': nc.scalar, 'dve': nc.vector, 'pool': nc.gpsimd, 'sp': nc.sync}
        self.sem = {e: nc.alloc_semaphore('prog_' + e) for e in ['pe', 'act', 'dve', 'pool']}
        self.cnt = {e: 0 for e in self.sem}
        self.ins = {e: [] for e in self.sem}
        self.sig_idx = {e: [] for e in self.sem}
        self.sig_cnt = {e: [] for e in self.sem}
        self.obs = {e: {} for e in self.eng}
        self.lw = {}
        self.rd = {}
        self.dma_total = {}
        self.dma_sems = {}

    def cover(self, e, idx):
        pos = bisect.bisect_left(self.sig_idx[e], idx)
        if pos < len(self.sig_idx[e]):
            return self.sig_cnt[e][pos]
        last = len(self.ins[e]) - 1
        self.ins[e][last].then_inc(self.sem[e], 1)
        self.cnt[e] += 1
        self.sig_idx[e].append(last); self.sig_cnt[e].append(self.cnt[e])
        return self.cnt[e]

    def _wait(self, engine, evts):
        need = {}
        for ev in evts:
            if ev[0] == 'c':
                if ev[1] == engine and engine == 'pe':
                    continue
                key = ev[1]; sem = self.sem[key]; val = self.cover(key, ev[2])
            else:
                key = ev[1]; sem = self.dma_sems[key]; val = self.dma_total[key]
            if need.get(key, (None, 0))[1] < val:
                need[key] = (sem, val)
        for key, (sem, val) in need.items():
            if self.obs[engine].get(key, 0) < val:
                self.eng[engine].wait_ge(sem, val)
                self.obs[engine][key] = val

    def _deps(self, reads, writes):
        evts = set()
        for k in reads:
            if k in self.lw:
                evts.add(self.lw[k])
        for k in writes:
            if k in self.lw:
                evts.add(self.lw[k])
            for ev in self.rd.get(k, {}).values():
                evts.add(ev)
        return evts

    def emit(self, engine, fn, reads=(), writes=(), signal=True):
        self._wait(engine, self._deps(reads, writes))
        ins = fn()
        idx = len(self.ins[engine]); self.ins[engine].append(ins)
        if signal:
            ins.then_inc(self.sem[engine], 1)
            self.cnt[engine] += 1
            self.sig_idx[engine].append(idx); self.sig_cnt[engine].append(self.cnt[engine])
        ev = ('c', engine, idx)
        for k in writes:
            self.lw[k] = ev; self.rd[k] = {}
        for k in reads:
            if k not in writes:
                self.rd.setdefault(k, {})[engine] = ev
        return ins

    def dma(self, fn, semkey, reads=(), writes=(), queue='sp'):
        if semkey not in self.dma_sems:
            self.dma_sems[semkey] = self.nc.alloc_semaphore('dma_' + semkey)
            self.dma_total[semkey] = 0
        self._wait(queue, self._deps(reads, writes))
        ins = fn()
        ins.then_inc(self.dma_sems[semkey], 16)
        self.dma_total[semkey] += 16
        ev = ('d', semkey)
        for k in writes:
            self.lw[k] = ev; self.rd[k] = {}
        for k in reads:
            self.rd.setdefault(k, {})[ev] = ev
        return ins

    def fence(self):
        evs = [('c', e, len(self.ins[e]) - 1) for e in self.sem if self.ins[e]]
        for e in ['pe', 'act', 'dve', 'pool', 'sp']:
            self._wait(e, evs)

    def wait_dma_all(self, engine='sp'):
        self._wait(engine, [('d', k) for k in self.dma_sems])


def build_nc(nseq=NSEQ, stop_after=None):
    nc = bass.Bass("TRN2", target_bir_lowering=False)
    S = Sched(nc)
    ntok = nseq * SEQ
    dram = lambda n, s, dt=F32, kind="ExternalInput": nc.dram_tensor(n, s, dt, kind=kind).ap()
    x = dram("x", [ntok, D])
    w_gu = [dram("w_ffn1_gu", [D, 2 * DFF]), dram("w_ffn2_gu", [D, 2 * DFF])]
    w_dn = [dram("w_ffn1_down", [DFF, D]), dram("w_ffn2_down", [DFF, D])]
    w_in = dram("w_in", [D, 7 * D])
    w_out = dram("w_out", [D, D])
    pvec_d = dram("pvec", [128, NP_COLS])
    gw_d = dram("gw", [128, 32 * 128])
    qk_d = dram("qk", [128, 2])
    ident_d = dram("ident", [128, 128])
    ebias_d = dram("ebias", [16, 128, ETAB_COLS])
    out = dram("out", [ntok, D], kind="ExternalOutput")
    wscr = dram("wscr", [TILES_PER_SEQ, 128, 1024], BF16, kind="Internal")
    escr = dram("escr", [16, 128, ETAB_COLS], BF16, kind="Internal")

    lay = _table_layout()
    top = ExitStack()
    uid = [0]

    def sb(n, s, dt, st=top):
        uid[0] += 1
        return st.enter_context(nc.sbuf_tensor(f"s{uid[0]}_{n}", s, dt))

    ps = lambda n, s, dt: top.enter_context(nc.psum_tensor(n, s, dt))

    ident = sb("ident", [128, 128], F32)
    identb = sb("identb", [128, 128], BF16)
    onesb = sb("onesb", [128, 128], BF16)
    bones = sb("bones", [128, 128], BF16)
    pvec = sb("pvec", [128, NP_COLS], F32)
    gwb = sb("gwb", [128, 32, 128], BF16)
    qk = sb("qksc", [128, 2], F32)
    cst = sb("cst", [128, 4], F32)
    cA = sb("cA", [128, 32], F32)
    PSA = ps("psA", [128, 1024], F32); PSB = ps("psB", [128, 1024], F32); PSC = ps("psC", [128, 1024], F32)
    PSD = ps("psD", [128, 512], F32); PST = ps("psT", [128, 1024], BF16)
    banks = [(PSA, 0, 'psA0'), (PSA, 512, 'psA1'), (PSB, 0, 'psB0'), (PSB, 512, 'psB1'),
             (PSC, 0, 'psC0'), (PSC, 512, 'psC1')]
    PSD_KEYS = ['psD']

    def bank_ap(i, n=512):
        t, o, k = banks[i]
        return t[:, o:o + n], k

    G1, GM, G2, CW, CB, BG, LAM = 0, 8, 16, 24, 56, 64, 96

    S.dma(lambda: nc.sync.dma_start(out=ident[:], in_=ident_d[:, :]), 'c0', writes=['ident'])
    S.dma(lambda: nc.sync.dma_start(out=pvec[:], in_=pvec_d[:, :]), 'c0', writes=['pvec'])
    S.dma(lambda: nc.sync.dma_start(out=qk[:], in_=qk_d[:, :]), 'c0', writes=['qk'])
    S.emit('dve', lambda: nc.vector.tensor_copy(out=identb[:], in_=ident[:]), ['ident'], ['identb'])
    S.emit('dve', lambda: nc.vector.memset(onesb[:], 1.0), [], ['onesb'])
    S.emit('dve', lambda: nc.vector.memset(bones[:], 0.0), [], ['bones'])
    S.emit('dve', lambda: nc.vector.memset(bones[0:64, 0:64], 1.0), [], ['bones'])
    S.emit('dve', lambda: nc.vector.memset(bones[64:128, 64:128], 1.0), [], ['bones'])
    S.emit('dve', lambda: nc.vector.memset(cst[:, 0:1], EPS), [], ['cst'])
    S.emit('dve', lambda: nc.vector.memset(cst[:, 1:2], 1.0), [], ['cst'])
    S.emit('dve', lambda: nc.vector.tensor_scalar(out=qk[:, 0:1], in0=qk[:, 0:1], scalar1=0.125, scalar2=None,
                                                  op0=ALU.mult), ['qk'], ['qk'])

    with ExitStack() as pst:
        psb = lambda n, s, dt: sb(n, s, dt, pst)
        lx = psb("lx", [128, 16], F32); lp = psb("lp", [128, 16], F32); lb = psb("lb", [128, 16], F32)
        lm = psb("lm", [128, 16], F32)
        lam = pvec[:, LAM:LAM + 16]
        S.emit('act', lambda: nc.scalar.activation(out=lx[:], in_=lam, func=AF.Exp, scale=-1.0), ['pvec'], ['lx'])
        S.emit('act', lambda: nc.scalar.activation(out=lb[:], in_=lx[:], func=AF.Ln, bias=cst[:, 1:2], scale=1.0),
               ['lx', 'cst'], ['lb'])
        S.emit('dve', lambda: nc.vector.tensor_scalar(out=lp[:], in0=lx[:], scalar1=-0.2, scalar2=0.25,
                                                      op0=ALU.mult, op1=ALU.add), ['lx'], ['lp'])
        for coef in (1.0 / 3.0, 0.5, 1.0):
            S.emit('dve', lambda: nc.vector.tensor_tensor(out=lp[:], in0=lp[:], in1=lx[:], op=ALU.mult), ['lp', 'lx'], ['lp'])
            S.emit('dve', lambda: nc.vector.tensor_scalar(out=lp[:], in0=lp[:], scalar1=-1.0, scalar2=coef,
                                                          op0=ALU.mult, op1=ALU.add), ['lp'], ['lp'])
        S.emit('dve', lambda: nc.vector.tensor_tensor(out=lp[:], in0=lp[:], in1=lx[:], op=ALU.mult), ['lp', 'lx'], ['lp'])
        S.emit('dve', lambda: nc.vector.tensor_single_scalar(out=lm[:], in_=lx[:], scalar=0.05, op=ALU.is_lt), ['lx'], ['lm'])
        S.emit('dve', lambda: nc.vector.tensor_tensor(out=lp[:], in0=lp[:], in1=lb[:], op=ALU.subtract), ['lp', 'lb'], ['lp'])
        S.emit('dve', lambda: nc.vector.tensor_tensor(out=lp[:], in0=lp[:], in1=lm[:], op=ALU.mult), ['lp', 'lm'], ['lp'])
        S.emit('dve', lambda: nc.vector.tensor_tensor(out=lp[:], in0=lp[:], in1=lb[:], op=ALU.add), ['lp', 'lb'], ['lp'])
        S.emit('dve', lambda: nc.vector.tensor_scalar(out=cA[:, 0:16], in0=lp[:], scalar1=-8.0, scalar2=None, op0=ALU.mult),
               ['lp'], ['cA'])
        S.emit('dve', lambda: nc.vector.tensor_scalar(out=cA[:, 16:32], in0=lp[:], scalar1=-16.0, scalar2=None, op0=ALU.mult),
               ['lp'], ['cA'])

        stf = psb("stf", [128, 2, 7168], F32)
        stb = psb("stb", [128, 56, 8, 128], BF16)
        stbf = stb[:].rearrange("p n c j -> p (n c j)")
        cast_engs = ['act', 'dve']
        cnt = [0]

        def cast(out_ap, in_ap, rd, wr):
            e = cast_engs[cnt[0] % 2]; cnt[0] += 1
            if e == 'act':
                S.emit('act', lambda: nc.scalar.copy(out=out_ap, in_=in_ap), rd, wr)
            elif e == 'dve':
                S.emit('dve', lambda: nc.vector.tensor_copy(out=out_ap, in_=in_ap), rd, wr)
            else:
                S.emit('pool', lambda: nc.gpsimd.tensor_copy(out=out_ap, in_=in_ap), rd, wr)

        ld = [0]

        def stage_load(src_ap, ncols, inner=None):
            sl = ld[0] % 2; ld[0] += 1
            dst = stf[:, sl, 0:ncols]
            if inner is not None:
                dst = dst.rearrange("p (f n) -> p f n", n=inner)
            S.dma(lambda: nc.sync.dma_start(out=dst, in_=src_ap), f'stf{sl}', writes=[f'stf{sl}'])
            return sl

        def colchunk_weight(W, ncols, store_fn):
            nt = ncols // 128
            for cc in range(8):
                sl = stage_load(W[cc * 128:(cc + 1) * 128, :], ncols)
                cast(stb[:, 0:nt, cc, :], stf[:, sl, 0:ncols].rearrange("p (n j) -> p n j", j=128),
                     [f'stf{sl}'], ['stb'])
            store_fn()

        def store_tiles(n0, cnt_, i0, step):
            src = stb[:, n0:n0 + cnt_, :, :].rearrange("p n c j -> p n (c j)")
            dst = wscr[i0:i0 + step * (cnt_ - 1) + 1:step].rearrange("n p x -> p n x")
            S.dma(lambda: nc.sync.dma_start(out=dst, in_=src), 'stst', reads=['stb'])

        for fi, base in ((0, FFN1_BASE), (1, FFN2_BASE)):
            def st_gu(base=base):
                for G in range(2):
                    for kind in range(2):
                        store_tiles(kind * NFC + G * 11, 11, base + G * 33 + kind, 2)
            colchunk_weight(w_gu[fi], 2 * DFF, st_gu)
            for G in range(2):
                for half, (f0, nf) in enumerate(((0, 6), (6, 5))):
                    fa = G * 11 + f0
                    sl = stage_load(w_dn[fi][fa * 128:(fa + nf) * 128, :].rearrange("(f p) n -> p f n", p=128), nf * 1024, 1024)
                    cast(stbf[:, 0:nf * 1024], stf[:, sl, 0:nf * 1024], [f'stf{sl}'], ['stb'])
                    i0 = base + G * 33 + 22 + f0
                    S.dma(lambda: nc.sync.dma_start(out=wscr[i0:i0 + nf].rearrange("n p x -> p n x"),
                                                    in_=stbf[:, 0:nf * 1024].rearrange("p (n x) -> p n x", x=1024)),
                          'stst', reads=['stb'])
        KORD = [0, 1, 4, 5, 6, 2, 3]

        def st_in():
            for fs in range(7):
                store_tiles(fs * 8, 1, MIX_BASE + _tix(0, KORD[fs]), 1)
                store_tiles(fs * 8 + 1, 7, MIX_BASE + _tix(1, KORD[fs]), 8)
        colchunk_weight(w_in, 7 * D, st_in)
        for half in range(2):
            sl = stage_load(w_out[half * 512:(half + 1) * 512, :].rearrange("(c p) n -> p c n", p=128), 4096, 1024)
            cast(stbf[:, 0:4096], stf[:, sl, 0:4096], [f'stf{sl}'], ['stb'])
            for ci in range(4):
                i0 = MIX_BASE + _wix(half * 4 + ci)
                S.dma(lambda: nc.sync.dma_start(out=wscr[i0], in_=stbf[:, ci * 1024:(ci + 1) * 1024]),
                      'stst', reads=['stb'])
        sl = stage_load(gw_d[:, :], 4096)
        S.emit('dve', lambda: nc.vector.tensor_copy(out=gwb[:].rearrange("p a b -> p (a b)"), in_=stf[:, sl, 0:4096]),
               [f'stf{sl}'], ['gwb'])
        for h in range(16):
            sl = stage_load(ebias_d[h], ETAB_COLS)
            S.emit('act', lambda: nc.scalar.activation(out=stbf[:, 0:ETAB_COLS], in_=stf[:, sl, 0:ETAB_COLS], func=AF.Exp),
                   [f'stf{sl}'], ['stb'])
            S.dma(lambda: nc.sync.dma_start(out=escr[h], in_=stbf[:, 0:ETAB_COLS]), 'stst', reads=['stb'])
        S.wait_dma_all('sp')
        S.fence()

    xT = sb("xT", [128, 8, SEQ], F32)
    hT = sb("hT", [128, 8, SEQ], BF16)
    ring = sb("ring", [128, NSLOT, 1024], BF16)
    total_tiles = TILES_PER_SEQ * nseq
    wst = {'next': 0}

    def load_upto(n):
        while wst['next'] < min(n, total_tiles):
            i = wst['next']; s = i % NSLOT
            S.dma(lambda: nc.sync.dma_start(out=ring[:, s, :], in_=wscr[i % TILES_PER_SEQ]), f'ring{s}',
                  writes=[f'ring{s}'])
            wst['next'] += 1

    def wget(i):
        load_upto(i + 1)
        return i % NSLOT

    def wdone(i):
        load_upto(i + 1 + NSLOT)

    load_upto(NSLOT)

    def tl(t):
        return slice(t * 512, (t + 1) * 512)

    def mm(out_ap, lhsT, rhs, start, stop, rd, wr):
        S.emit('pe', lambda: nc.tensor.matmul(out_ap, lhsT, rhs, start=start, stop=stop), rd, wr, signal=bool(stop))

    def rmsnorm(gcol, st):
        sq = sb("nsq", [128, 2, 512], BF16, st)
        sd = sb("nsd", [128, 2, 512], F32, st)
        for t in range(4):
            for c in range(8):
                b = c % 2
                S.emit('act', lambda: nc.scalar.activation(out=sq[:, b, :], in_=xT[:, c, tl(t)], func=AF.Square),
                       [f'xT{c}.{t}'], [f'nsq{b}'])
                mm(PSD[:, :], onesb[:], sq[:, b, :], c == 0, c == 7, ['onesb', f'nsq{b}'], PSD_KEYS)
            b = t % 2
            S.emit('act', lambda: nc.scalar.activation(out=sd[:, b, :], in_=PSD[:, :], func=AF.Ln,
                                                       bias=cst[:, 0:1], scale=1.0 / D), PSD_KEYS + ['cst'], [f'nsd{b}'])
            S.emit('act', lambda: nc.scalar.activation(out=sd[:, b, :], in_=sd[:, b, :], func=AF.Exp, scale=-0.5),
                   [f'nsd{b}'], [f'nsd{b}'])
            for c in range(8):
                S.emit('dve', lambda: nc.vector.scalar_tensor_tensor(out=hT[:, c, tl(t)], in0=xT[:, c, tl(t)],
                                                                     scalar=pvec[:, gcol + c:gcol + c + 1],
                                                                     in1=sd[:, b, :], op0=ALU.mult, op1=ALU.mult),
                       [f'xT{c}.{t}', 'pvec', f'nsd{b}'], [f'hT{c}.{t}'])

    def ffn(tbase, gcol):
        with ExitStack() as st:
            rmsnorm(gcol, st)
            act = sb("actT", [128, 11, SEQ], BF16, st)
            sg = sb("sg", [128, 2, 512], F32, st)
            k = 0
            for G in range(2):
                gb_ = tbase + G * 33
                for fl in range(11):
                    sg_ = wget(gb_ + 2 * fl); su_ = wget(gb_ + 2 * fl + 1)
                    for t in range(4):
                        pg, kg = bank_ap(k % 2); pu, ku = bank_ap(2 + k % 2); b = k % 2; k += 1
                        for cc in range(8):
                            mm(pg, ring[:, sg_, cc * 128:(cc + 1) * 128], hT[:, cc, tl(t)], cc == 0, cc == 7,
                               [f'ring{sg_}', f'hT{cc}.{t}'], [kg])
                        for cc in range(8):
                            mm(pu, ring[:, su_, cc * 128:(cc + 1) * 128], hT[:, cc, tl(t)], cc == 0, cc == 7,
                               [f'ring{su_}', f'hT{cc}.{t}'], [ku])
                        S.emit('act', lambda: nc.scalar.activation(out=sg[:, b, :], in_=pg, func=AF.Silu), [kg], [f'sg{b}'])
                        S.emit('dve', lambda: nc.vector.tensor_tensor(out=act[:, fl, tl(t)], in0=sg[:, b, :], in1=pu, op=ALU.mult),
                               [f'sg{b}', ku], [f'act{fl}.{t}'])
                    wdone(gb_ + 2 * fl); wdone(gb_ + 2 * fl + 1)
                dslots = [wget(gb_ + 22 + fl) for fl in range(11)]
                kd = 0
                for o in range(8):
                    for t in range(4):
                        pd, kdk = bank_ap(4 + kd % 2); kd += 1
                        for fl in range(11):
                            mm(pd, ring[:, dslots[fl], o * 128:(o + 1) * 128], act[:, fl, tl(t)], fl == 0, fl == 10,
                               [f'ring{dslots[fl]}', f'act{fl}.{t}'], [kdk])
                        S.emit('dve', lambda: nc.vector.scalar_tensor_tensor(out=xT[:, o, tl(t)], in0=pd, scalar=0.5,
                                                                             in1=xT[:, o, tl(t)], op0=ALU.mult, op1=ALU.add),
                               [kdk, f'xT{o}.{t}'], [f'xT{o}.{t}'])
                for fl in range(11):
                    wdone(gb_ + 22 + fl)
        S.fence()

    pk = [0]

    def proj(slot, t, bank_ids=(4, 5)):
        pb, kb = bank_ap(bank_ids[pk[0] % len(bank_ids)]); pk[0] += 1
        for cc in range(8):
            mm(pb, ring[:, slot, cc * 128:(cc + 1) * 128], hT[:, cc, tl(t)], cc == 0, cc == 7,
               [f'ring{slot}', f'hT{cc}.{t}'], [kb])
        return pb, kb

    def mixer(tbase, mst):
        with ExitStack() as st:
            rmsnorm(GM, st)
        S.fence()
        ytcs = [sb("ytcA", [128, SEQ], BF16, mst), sb("ytcB", [128, SEQ], BF16, mst)]

        def wout_steps(cprev):
            ytp = ytcs[cprev % 2]
            s_wo = wget(tbase + _wix(cprev))
            ko = 0
            for o in range(8):
                for t in range(4):
                    pb, kb = bank_ap(4 + ko % 2); ko += 1
                    mm(pb, ring[:, s_wo, o * 128:(o + 1) * 128], ytp[:, tl(t)], True, True,
                       [f'ring{s_wo}', f'ytc{cprev % 2}.{t}'], [kb])
                    S.emit('dve', lambda: nc.vector.tensor_tensor(out=xT[:, o, tl(t)], in0=pb, in1=xT[:, o, tl(t)], op=ALU.add),
                           [kb, f'xT{o}.{t}'], [f'xT{o}.{t}'])
                    yield
            wdone(tbase + _wix(cprev))

        def wout_accum(cprev):
            for _ in wout_steps(cprev):
                pass

        for c in range(8):
            tbk = lambda k_: tbase + _tix(c, k_)
            ytc = ytcs[c % 2]
            with ExitStack() as cst_:
                yA = sb("yA", [128, SEQ], BF16, cst_)
                with ExitStack() as st:
                    xr = sb("xr", [128, SEQ + 4], F32, st)
                    xcb = sb("xcb", [128, SEQ], BF16, st)
                    Rb_ = [sb("Rf", [128, SEQ], F32, st), sb("Rb", [128, SEQ], F32, st)]
                    Ib_ = [sb("If", [128, SEQ], F32, st), sb("Ib", [128, SEQ], F32, st)]
                    M0 = sb("M0", [128, SEQ], F32, st)
                    g2 = sb("g2", [128, SEQ], BF16, st)
                    tmp = sb("tmpA", [128, 4, 512], BF16, st)
                    Mb_ = [M0[:, :], xr[:, 2:2 + SEQ]]
                    Mk = ['M0', 'xr']
                    s_xr = wget(tbk(0))
                    S.emit('pool', lambda: nc.gpsimd.memset(xr[:, 0:2], 0.0), [], ['xr'])
                    S.emit('pool', lambda: nc.gpsimd.memset(xr[:, 2 + SEQ:4 + SEQ], 0.0), [], ['xr'])
                    for t in range(4):
                        pb, kb = proj(s_xr, t, (0, 1, 2, 3))
                        S.emit('act', lambda: nc.scalar.copy(out=xr[:, 2 + t * 512:2 + (t + 1) * 512], in_=pb), [kb], ['xr'])
                    wdone(tbk(0))
                    cw = lambda k_: pvec[:, CW + k_ * 8 + c:CW + k_ * 8 + c + 1]
                    S.emit('pool', lambda: nc.gpsimd.tensor_scalar(out=M0[:, :], in0=xr[:, 0:SEQ], scalar1=cw(0),
                                                                   scalar2=pvec[:, CB + c:CB + c + 1], op0=ALU.mult, op1=ALU.add),
                           ['xr', 'pvec'], ['M0'])
                    for k_ in range(1, 4):
                        S.emit('dve', lambda: nc.vector.scalar_tensor_tensor(out=M0[:, :], in0=xr[:, k_:k_ + SEQ], scalar=cw(k_),
                                                                              in1=M0[:, :], op0=ALU.mult, op1=ALU.add),
                               ['xr', 'pvec', 'M0'], ['M0'])
                    S.emit('act', lambda: nc.scalar.copy(out=xcb[:, :], in_=M0[:, :]), ['M0'], ['xcb'])
                    s_gr = wget(tbk(1))
                    gtiles = []
                    for t in range(4):
                        pb, kb = proj(s_gr, t, (0, 1, 2, 3))
                        gtiles.append((pb, kb))
                        S.emit('act', lambda: nc.scalar.activation(out=tmp[:, t, :], in_=pb, func=AF.Square, scale=0.21145921592969426),
                               [kb], [f'tmpA{t}'])
                        S.emit('dve', lambda: nc.vector.scalar_tensor_tensor(out=tmp[:, t, :], in0=tmp[:, t, :], scalar=1.0, in1=pb,
                                                                             op0=ALU.add, op1=ALU.mult),
                               [f'tmpA{t}', kb], [f'tmpA{t}'])
                    wdone(tbk(1))
                    for t in range(4):
                        pb, kb = gtiles[t]
                        S.emit('act', lambda: nc.scalar.activation(out=tmp[:, t, :], in_=tmp[:, t, :], func=AF.Sigmoid,
                                                                   scale=1.5957691216057308), [f'tmpA{t}'], [f'tmpA{t}'])
                        S.emit('dve', lambda: nc.vector.tensor_tensor(out=g2[:, tl(t)], in0=tmp[:, t, :], in1=pb, op=ALU.mult),
                               [f'tmpA{t}', kb], [f'g2.{t}'])
                    wgen = wout_steps(c - 1) if c > 0 else iter(())
                    for d in range(2):
                        for g_, (buf, bk) in enumerate(((Rb_[d], f'R{d}'), (Ib_[d], f'I{d}'))):
                            gi = (d * 2 + g_) * 8 + c
                            for t in range(4):
                                pb, kb = bank_ap((d * 8 + g_ * 4 + t) % 4)
                                mm(pb, gwb[:, gi, :], xcb[:, tl(t)], True, True, ['gwb', 'xcb'], [kb])
                                S.emit('act', lambda: nc.scalar.activation(out=buf[:, tl(t)], in_=pb, func=AF.Sigmoid,
                                                                           bias=pvec[:, BG + gi:BG + gi + 1], scale=1.0),
                                       [kb, 'pvec'], [bk])
                                next(wgen, None)
                    for _ in wgen:
                        pass
                    for d in range(2):
                        S.emit('act', lambda: nc.scalar.activation(out=Mb_[d], in_=Rb_[d][:, :], func=AF.Exp,
                                                                   scale=cA[:, 16 + d * 8 + c:16 + d * 8 + c + 1]),
                               [f'R{d}', 'cA'], [Mk[d]])
                        S.emit('act', lambda: nc.scalar.activation(out=Rb_[d][:, :], in_=Rb_[d][:, :], func=AF.Exp,
                                                                   scale=cA[:, d * 8 + c:d * 8 + c + 1]),
                               [f'R{d}', 'cA'], [f'R{d}'])
                    for d in range(2):
                        S.emit('act', lambda: nc.scalar.activation(out=Mb_[d], in_=Mb_[d], func=AF.Sqrt,
                                                                   bias=cst[:, 1:2], scale=-1.0), [Mk[d], 'cst'], [Mk[d]])
                    for d, e in ((0, 'dve'), (1, 'dve')):
                        eo = nc.vector if e == 'dve' else nc.gpsimd
                        S.emit(e, lambda: eo.tensor_tensor(out=Ib_[d][:, :], in0=Ib_[d][:, :], in1=xcb[:, :], op=ALU.mult),
                               [f'I{d}', 'xcb'], [f'I{d}'])
                        S.emit(e, lambda: eo.tensor_tensor(out=Ib_[d][:, :], in0=Ib_[d][:, :], in1=Mb_[d], op=ALU.mult),
                               [f'I{d}', Mk[d]], [f'I{d}'])
                    s_ga = wget(tbk(2))
                    for t in range(4):
                        pb, kb = proj(s_ga, t, (0, 1, 2, 3))
                        S.emit('act', lambda: nc.scalar.activation(out=tmp[:, t, :], in_=pb, func=AF.Sigmoid), [kb], [f'tmpA{t}'])
                        S.emit('dve', lambda: nc.vector.tensor_tensor(out=g2[:, tl(t)], in0=g2[:, tl(t)], in1=tmp[:, t, :], op=ALU.mult),
                               [f'tmpA{t}', f'g2.{t}'], [f'g2.{t}'])
                    wdone(tbk(2))
                    S.emit('dve', lambda: nc.vector.tensor_tensor_scan(out=Mb_[0], data0=Rb_[0][:, :], data1=Ib_[0][:, :],
                                                                       initial=0.0, op0=ALU.mult, op1=ALU.add),
                           ['R0', 'I0'], ['M0'])
                    S.emit('dve', lambda: nc.vector.tensor_tensor_scan(out=Mb_[1][:, ::-1], data0=Rb_[1][:, ::-1],
                                                                       data1=Ib_[1][:, ::-1], initial=0.0,
                                                                       op0=ALU.mult, op1=ALU.add),
                           ['R1', 'I1'], ['xr'])
                    S.emit('dve', lambda: nc.vector.tensor_tensor(out=Mb_[0], in0=Mb_[0], in1=Mb_[1], op=ALU.add),
                           ['M0', 'xr'], ['M0'])
                    S.emit('dve', lambda: nc.vector.tensor_tensor(out=yA[:, :], in0=Mb_[0], in1=g2[:, :], op=ALU.mult),
                           ['M0'] + [f'g2.{t}' for t in range(4)], ['yA'])
                S.fence()
                with ExitStack() as st:
                    qn = sb("qn", [128, SEQ], BF16, st); kn = sb("kn", [128, SEQ], BF16, st)
                    Vc = sb("Vc", [128, 16, 2, 65], BF16, st)
                    sgb = sb("sgb", [128, SEQ], BF16, st)
                    Et = sb("Et", [128, 2, ETAB_COLS], BF16, st)
                    pT = sb("pT", [128, 4, 640], BF16, st)
                    sq = sb("bsq", [128, 2, 512], BF16, st)
                    sd = sb("bsd", [128, 2, 512], F32, st)
                    ybt = sb("ybt", [128, 2, 128], BF16, st)
                    rec = sb("rec", [128, 4], F32, st)
                    ytmp = sb("ytmp", [128, 2, 512], BF16, st)
                    for hh in range(2):
                        S.dma(lambda: nc.sync.dma_start(out=Et[:, hh, :], in_=escr[2 * c + hh]), f'et{hh}', writes=[f'Et{hh}'])
                    s_gb = wget(tbk(3))
                    for t in range(4):
                        pb, kb = proj(s_gb, t, (0, 1, 2, 3, 4, 5))
                        S.emit('act', lambda: nc.scalar.activation(out=sgb[:, tl(t)], in_=pb, func=AF.Sigmoid), [kb], [f'sgb{t}'])
                    wdone(tbk(3))
                    for wi, (dst, dk, col) in enumerate(((qn, 'qn', 0), (kn, 'kn', 1))):
                        s_ = wget(tbk(4 + wi))
                        for t in range(4):
                            pb, kb = proj(s_, t, (0, 1, 2, 3, 4, 5)); b = t % 2
                            S.emit('act', lambda: nc.scalar.activation(out=sq[:, b, :], in_=pb, func=AF.Square), [kb], [f'bsq{b}'])
                            mm(PSD[:, :], bones[:], sq[:, b, :], True, True, ['bones', f'bsq{b}'], PSD_KEYS)
                            S.emit('act', lambda: nc.scalar.activation(out=sd[:, b, :], in_=PSD[:, :], func=AF.Ln,
                                                                       bias=cst[:, 0:1], scale=1.0 / 64), PSD_KEYS + ['cst'], [f'bsd{b}'])
                            S.emit('act', lambda: nc.scalar.activation(out=sd[:, b, :], in_=sd[:, b, :], func=AF.Exp, scale=-0.5),
                                   [f'bsd{b}'], [f'bsd{b}'])
                            S.emit('dve', lambda: nc.vector.scalar_tensor_tensor(out=dst[:, tl(t)], in0=pb, scalar=qk[:, col:col + 1],
                                                                                 in1=sd[:, b, :], op0=ALU.mult, op1=ALU.mult),
                                   [kb, 'qk', f'bsd{b}'], [f'{dk}{t}'])
                        wdone(tbk(4 + wi))
                    s_v = wget(tbk(6))
                    S.emit('pool', lambda: nc.gpsimd.memset(Vc[:, :, :, 64:65], 1.0), [], ['Vones'])
                    for tq in range(4):
                        pb, kb = bank_ap(4 + tq % 2)
                        for tt in range(4):
                            tok = slice((tq * 4 + tt) * 128, (tq * 4 + tt + 1) * 128)
                            for cc in range(8):
                                mm(pb[:, tt * 128:(tt + 1) * 128], hT[:, cc, tok], ring[:, s_v, cc * 128:(cc + 1) * 128],
                                   cc == 0, cc == 7, [f'ring{s_v}', f'hT{cc}.{tq}'], [kb])
                        S.emit('act', lambda: nc.scalar.copy(out=Vc[:, tq * 4:tq * 4 + 4, :, 0:64],
                                                             in_=pb.rearrange("p (a h d) -> p a h d", a=4, h=2)),
                               [kb], [f'Vc{tq}'])
                    wdone(tbk(6))
                    iters = [(m, hh) for m in range(16) for hh in range(2)]

                    def sbuf_of(it):
                        return [(PSA, ['psA0', 'psA1']), (PSB, ['psB0', 'psB1'])][it % 2]

                    def scores(it):
                        m, hh = iters[it]
                        off, js = lay[m]; hb = hh * 64
                        SP, spk = sbuf_of(it)
                        for jj, j in enumerate(js):
                            mm(SP[:, jj * 128:(jj + 1) * 128], kn[hb:hb + 64, j * 128:(j + 1) * 128],
                               qn[hb:hb + 64, m * 128:(m + 1) * 128], True, True,
                               [f'kn{j // 4}', f'qn{m // 4}'], [spk[jj // 4]])

                    def stage_a(it):
                        m, hh = iters[it]
                        off, js = lay[m]; n = len(js)
                        SP, spk = sbuf_of(it)
                        pslot = it % 4
                        S.emit('act', lambda: nc.scalar.activation(out=pT[:, pslot, 0:n * 128], in_=SP[:, 0:n * 128], func=AF.Exp),
                               spk, [f'pT{pslot}'])
                        if it % 2 == 0:
                            S.emit('pool', lambda: nc.gpsimd.tensor_tensor(out=pT[:, pslot, 0:n * 128], in0=pT[:, pslot, 0:n * 128],
                                                                           in1=Et[:, hh, off:off + n * 128], op=ALU.mult),
                                   [f'pT{pslot}', f'Et{hh}'], [f'pT{pslot}'])
                        else:
                            S.emit('dve', lambda: nc.vector.tensor_tensor(out=pT[:, pslot, 0:n * 128], in0=pT[:, pslot, 0:n * 128],
                                                                          in1=Et[:, hh, off:off + n * 128], op=ALU.mult),
                                   [f'pT{pslot}', f'Et{hh}'], [f'pT{pslot}'])

                    def stage_b(it):
                        m, hh = iters[it]
                        off, js = lay[m]; n = len(js); t_q = m // 4; ys = m % 2; hb = hh * 64
                        pslot = it % 4
                        ops, okey = [(PSC[:, 0:65], 'psC0'), (PSC[:, 512:577], 'psC1'), (PSD[:, 0:65], 'psD')][it % 3]
                        for jj, j in enumerate(js):
                            mm(ops, pT[:, pslot, jj * 128:(jj + 1) * 128], Vc[:, j, hh, :], jj == 0, jj == n - 1,
                               [f'pT{pslot}', f'Vc{j // 4}', 'Vones'], [okey])
                        rcol = it % 4
                        S.emit('dve', lambda: nc.vector.reciprocal(out=rec[:, rcol:rcol + 1], in_=ops[:, 64:65]),
                               [okey], [f'rec{rcol}'])
                        S.emit('dve', lambda: nc.vector.tensor_scalar(out=ybt[:, ys, hb:hb + 64], in0=ops[:, 0:64],
                                                                      scalar1=rec[:, rcol:rcol + 1], scalar2=None, op0=ALU.mult),
                               [okey, f'rec{rcol}'], [f'ybt{ys}h{hh}'])
                        if hh == 1:
                            pend.append(lambda m=m, t_q=t_q, ys=ys: tail(m, t_q, ys))

                    def tail(m, t_q, ys):
                        if True:
                            half = t_q % 2
                            S.emit('pe', lambda: nc.tensor.transpose(PST[:, half * 512 + (m % 4) * 128:half * 512 + (m % 4 + 1) * 128],
                                                                     ybt[:, ys, :], identb[:]),
                                   [f'ybt{ys}h0', f'ybt{ys}h1', 'identb'], ['psT'])
                            if m % 4 == 3:
                                S.emit('dve', lambda: nc.vector.tensor_tensor(out=ytmp[:, half, :], in0=PST[:, half * 512:(half + 1) * 512],
                                                                              in1=sgb[:, tl(t_q)], op=ALU.mult),
                                       ['psT', f'sgb{t_q}'], [f'ytmp{half}'])
                                S.emit('pool', lambda: nc.gpsimd.tensor_tensor(out=ytc[:, tl(t_q)], in0=ytmp[:, half, :],
                                                                               in1=yA[:, tl(t_q)], op=ALU.add),
                                       [f'ytmp{half}', 'yA'], [f'ytc{c % 2}.{t_q}'])

                    pend = []
                    scores(0); stage_a(0); scores(1); stage_a(1)
                    for it in range(len(iters)):
                        if it + 2 < len(iters):
                            scores(it + 2)
                            stage_a(it + 2)
                        todo = pend[:]; del pend[:]
                        stage_b(it)
                        for f_ in todo:
                            f_()
                    for f_ in pend:
                        f_()
                S.fence()
        wout_accum(7)
        S.fence()

    for s in range(nseq):
        row0 = s * SEQ
        wb = s * TILES_PER_SEQ
        with ExitStack() as st:
            xin = sb("xin", [128, 2, D], F32, st)
            for tt in range(16):
                sl = tt % 2
                S.dma(lambda: nc.sync.dma_start(out=xin[:, sl, :], in_=x[row0 + tt * 128:row0 + (tt + 1) * 128, :]),
                      f'xin{sl}', writes=[f'xin{sl}'])
                P_, pkeys = (PSA, ['psA0', 'psA1']) if tt % 2 == 0 else (PSB, ['psB0', 'psB1'])
                for c in range(8):
                    S.emit('pe', lambda: nc.tensor.transpose(P_[:, c * 128:(c + 1) * 128], xin[:, sl, c * 128:(c + 1) * 128], ident[:]),
                           [f'xin{sl}', 'ident'], [pkeys[c // 4]])
                t = tt // 4
                for hf in range(2):
                    dst = xT[:, hf * 4:hf * 4 + 4, tt * 128:(tt + 1) * 128]
                    src = P_[:, hf * 512:(hf + 1) * 512].rearrange("p (c j) -> p c j", j=128)
                    wk = [f'xT{c}.{t}' for c in range(hf * 4, hf * 4 + 4)]
                    if t % 2 == 0:
                        S.emit('act', lambda: nc.scalar.copy(out=dst, in_=src), [pkeys[hf]], wk)
                    else:
                        S.emit('dve', lambda: nc.vector.tensor_copy(out=dst, in_=src), [pkeys[hf]], wk)
        S.fence()
        if stop_after != 'load':
            ffn(wb + FFN1_BASE, G1)
            if stop_after != 'ffn1':
                with ExitStack() as mst:
                    mixer(wb + MIX_BASE, mst)
                if stop_after != 'mixer':
                    ffn(wb + FFN2_BASE, G2)
        with ExitStack() as st:
            xo = sb("xo", [128, 2, D], F32, st)
            for tt in range(16):
                sl = tt % 2; t = tt // 4
                P_, pkeys = (PSA, ['psA0', 'psA1']) if tt % 2 == 0 else (PSB, ['psB0', 'psB1'])
                for c in range(8):
                    S.emit('pe', lambda: nc.tensor.transpose(P_[:, c * 128:(c + 1) * 128], xT[:, c, tt * 128:(tt + 1) * 128], ident[:]),
                           [f'xT{c}.{t}', 'ident'], [pkeys[c // 4]])
                for hf in range(2):
                    if hf == 0:
                        S.emit('act', lambda: nc.scalar.copy(out=xo[:, sl, 0:512], in_=P_[:, 0:512]), [pkeys[0]], [f'xo{sl}'])
                    else:
                        S.emit('dve', lambda: nc.vector.tensor_copy(out=xo[:, sl, 512:1024], in_=P_[:, 512:1024]), [pkeys[1]], [f'xo{sl}'])
                S.dma(lambda: nc.sync.dma_start(out=out[row0 + tt * 128:row0 + (tt + 1) * 128, :], in_=xo[:, sl, :]),
                      f'xo{sl}', reads=[f'xo{sl}'])
            S.wait_dma_all('sp')
        S.fence()
    S.wait_dma_all('sp')
    top.close()
    return nc


def _host_layout(inputs):
    f = lambda k: np.ascontiguousarray(np.asarray(inputs[k], dtype=np.float32))
    pc = lambda v: np.ascontiguousarray(v.reshape(8, 128).T)
    cols = [pc(f('norm_ffn1')[0]), pc(f('norm_mix')[0]), pc(f('norm_ffn2')[0])]
    cw = f('conv_w')[0]
    cols += [pc(cw[k]) for k in range(4)]
    cols += [pc(f('conv_b')[0])]
    bgt = f('lru_b_gates')[0]
    cols += [pc(bgt[d, g]) for d in range(2) for g in range(2)]
    lam = f('lru_lambda')[0]
    cols += [pc(lam[d]) for d in range(2)]
    pvec = np.ascontiguousarray(np.concatenate(cols, axis=1))
    assert pvec.shape == (128, NP_COLS)
    wg = f('lru_w_gates')[0]
    gw = np.zeros((128, 32, 128), np.float32)
    for d in range(2):
        for g in range(2):
            for c in range(8):
                for bl in range(2):
                    gw[bl * 64:(bl + 1) * 64, (d * 2 + g) * 8 + c, bl * 64:(bl + 1) * 64] = wg[d, g, 2 * c + bl]
    qk = np.stack([np.tile(f('q_norm')[0], 2), np.tile(f('k_norm')[0], 2)], axis=1)
    dr, dc, valid = _build_index_tables()
    rpb = f('rel_pos_bias')[0]
    eb = rpb[:, dr, dc]
    eb[:, ~valid] = np.float32(NEG)
    return {
        "w_ffn1_gu": f('w_ffn1_gu')[0], "w_ffn2_gu": f('w_ffn2_gu')[0],
        "w_ffn1_down": f('w_ffn1_down')[0], "w_ffn2_down": f('w_ffn2_down')[0],
        "w_in": f('w_in')[0], "w_out": f('w_out')[0],
        "pvec": pvec, "gw": np.ascontiguousarray(gw.reshape(128, 32 * 128)),
        "qk": np.ascontiguousarray(qk.astype(np.float32)),
        "ident": np.eye(128, dtype=np.float32),
        "ebias": np.ascontiguousarray(eb.astype(np.float32)),
    }


_NC_CACHE = {}


def kernel(**inputs):
    shared = _host_layout(inputs)
    x = np.asarray(inputs['x'], dtype=np.float32)
    B = x.shape[0]
    per = B // NCORES
    if 'nc' not in _NC_CACHE:
        _NC_CACHE['nc'] = build_nc(per)
    nc = _NC_CACHE['nc']
    in_maps = []
    for i in range(NCORES):
        m = dict(shared)
        m["x"] = np.ascontiguousarray(x[i * per:(i + 1) * per].reshape(per * SEQ, D))
        in_maps.append(m)
    res = run_bass_kernel_spmd(nc, in_maps, core_ids=list(range(NCORES)))
    outs = [np.asarray(r["out"]).reshape(per, SEQ, D) for r in res.results]
    return np.concatenate(outs, axis=0).astype(np.float32)
```
